# Optimizing a Trainium2 kernel written in Bass

```python
import jax, jax.numpy as jnp
from jax import lax
import numpy as np

D_MODEL = 1024
BATCH = 8
SEQ = 4096
DEPTH = 2

EXPAND = 2
D_INNER = EXPAND * D_MODEL
D_POOL = D_INNER // 2
D_SB = D_INNER - D_POOL
POOL_WINDOWS = (2, 4, 8, 16)
N_POOL_GROUPS = len(POOL_WINDOWS)
POOL_GROUP = D_POOL // N_POOL_GROUPS
SB_HEAD_DIM = 64
SB_HEADS = D_SB // SB_HEAD_DIM
SB_BLOCK = 128
CONV_WIDTH = 3
D_CONV = D_INNER
D_IN_EVEN = D_POOL + 3 * D_SB + D_INNER
D_IN_ODD = 3 * D_CONV + D_INNER
N_EVEN = (DEPTH + 1) // 2
N_ODD = DEPTH // 2
EPS = 1e-6

kernel_name = "hybrid_pool_stickbreak_shortconv_adaln"


def rmsnorm(x, g):
    xf = x.astype(jnp.float32)
    y = xf * lax.rsqrt(jnp.mean(xf * xf, axis=-1, keepdims=True) + EPS)
    return (y * g.astype(jnp.float32)).astype(x.dtype)


def adaln_params(c, w, b):
    m = jax.nn.silu(c) @ w + b
    shift, scale, gate = jnp.split(m, 3, axis=-1)
    return shift[:, None, :], scale[:, None, :], gate[:, None, :]


def pool_mixer(u, w_grp, scale):
    bsz, s, _ = u.shape
    uf = u.astype(jnp.float32)
    cs = jnp.cumsum(uf, axis=1)
    pos = jnp.arange(s, dtype=jnp.float32)
    outs = []
    for gi, w in enumerate(POOL_WINDOWS):
        sl = slice(gi * POOL_GROUP, (gi + 1) * POOL_GROUP)
        cg = cs[..., sl]
        prev = jnp.pad(cg, ((0, 0), (w, 0), (0, 0)))[:, :s]
        count = jnp.minimum(pos + 1.0, float(w))[None, :, None]
        outs.append((cg - prev) / count - uf[..., sl])
    p = jnp.stack(outs, axis=2).astype(u.dtype)
    y = jnp.einsum('bsgc,gcd->bsgd', p, w_grp).reshape(bsz, s, D_POOL)
    return y * scale


def stick_breaking_attention(q, k, v):
    bsz, s, _, _ = q.shape
    qh, kh, vh = (jnp.swapaxes(t, 1, 2) for t in (q, k, v))
    inv_sqrt = 1.0 / np.sqrt(SB_HEAD_DIM).astype(np.float32)
    outs = []
    for blk in range(s // SB_BLOCK):
        q0 = blk * SB_BLOCK
        end = q0 + SB_BLOCK
        qb = qh[:, :, q0:end]
        kb = kh[:, :, :end]
        vb = vh[:, :, :end]
        z = jnp.einsum('bhqd,bhkd->bhqk', qb, kb).astype(jnp.float32) * inv_sqrt
        q_pos = jnp.arange(q0, end)[:, None]
        k_pos = jnp.arange(end)[None, :]
        causal = k_pos < q_pos
        log_fail = jnp.where(causal, -jax.nn.softplus(z), 0.0)
        after = lax.cumsum(log_fail, axis=3, reverse=True) - log_fail
        a = jnp.where(causal, jnp.exp(jax.nn.log_sigmoid(z) + after), 0.0)
        outs.append(jnp.einsum('bhqk,bhkd->bhqd', a.astype(vb.dtype), vb))
    o = jnp.concatenate(outs, axis=2)
    return jnp.swapaxes(o, 1, 2).reshape(bsz, s, SB_HEADS * SB_HEAD_DIM)


def even_mixer(h, w_in, pool_w, pool_scale, w_out):
    bsz, s, _ = h.shape
    proj = h @ w_in
    u_pool, q, k, v, gate = jnp.split(
        proj, [D_POOL, D_POOL + D_SB, D_POOL + 2 * D_SB, D_POOL + 3 * D_SB], axis=-1)
    y_pool = pool_mixer(u_pool, pool_w, pool_scale)
    shp = (bsz, s, SB_HEADS, SB_HEAD_DIM)
    y_sb = stick_breaking_attention(q.reshape(shp), k.reshape(shp), v.reshape(shp))
    y = jnp.concatenate([y_pool, y_sb], axis=-1) * jax.nn.silu(gate)
    return y @ w_out


def odd_mixer(h, w_in, conv_w, conv_b, w_out):
    s = h.shape[1]
    proj = h @ w_in
    gb, gc, u, gate = jnp.split(proj, [D_CONV, 2 * D_CONV, 3 * D_CONV], axis=-1)
    u = gc * u
    up = jnp.pad(u, ((0, 0), (CONV_WIDTH - 1, 0), (0, 0)))
    conv = conv_b + sum(up[:, j:j + s] * conv_w[j] for j in range(CONV_WIDTH))
    y = gb * conv * jax.nn.silu(gate)
    return y @ w_out


def setup_inputs(seed: int = 0) -> dict:
    key = jax.random.key(seed)
    ks = jax.random.split(key, 16)
    f32 = jnp.float32
    D = D_MODEL
    nrm = lambda k, shp, sc: jax.random.normal(k, shp, f32) * sc
    return {
        "x": nrm(ks[0], (BATCH, SEQ, D), 1.0),
        "c": nrm(ks[1], (BATCH, D), 1.0),
        "norm_g": 1.0 + nrm(ks[2], (DEPTH, D), 0.02),
        "ada_w": nrm(ks[3], (DEPTH, D, 3 * D), 0.1 * D ** -0.5),
        "ada_b": nrm(ks[4], (DEPTH, 3 * D), 0.01),
        "even_w_in": nrm(ks[5], (N_EVEN, D, D_IN_EVEN), D ** -0.5),
        "pool_w": nrm(ks[6], (N_EVEN, N_POOL_GROUPS, POOL_GROUP, POOL_GROUP), POOL_GROUP ** -0.5),
        "pool_scale": 1.0 + nrm(ks[7], (N_EVEN, D_POOL), 0.02),
        "even_w_out": nrm(ks[8], (N_EVEN, D_INNER, D), D_INNER ** -0.5),
        "odd_w_in": nrm(ks[9], (N_ODD, D, D_IN_ODD), D ** -0.5),
        "conv_w": nrm(ks[10], (N_ODD, CONV_WIDTH, D_CONV), CONV_WIDTH ** -0.5),
        "conv_b": nrm(ks[11], (N_ODD, D_CONV), 0.01),
        "odd_w_out": nrm(ks[12], (N_ODD, D_INNER, D), D_INNER ** -0.5),
        "final_g": 1.0 + nrm(ks[13], (D,), 0.02),
    }


def reference(x, c, norm_g, ada_w, ada_b, even_w_in, pool_w, pool_scale, even_w_out,
              odd_w_in, conv_w, conv_b, odd_w_out, final_g):
    for i in range(DEPTH):
        shift, scale, gate = adaln_params(c, ada_w[i], ada_b[i])
        h = rmsnorm(x, norm_g[i]) * (1.0 + scale) + shift
        j = i // 2
        if i % 2 == 0:
            y = even_mixer(h, even_w_in[j], pool_w[j], pool_scale[j], even_w_out[j])
        else:
            y = odd_mixer(h, odd_w_in[j], conv_w[j], conv_b[j], odd_w_out[j])
        x = x + ((1.0 + gate) * y).astype(x.dtype)
    return rmsnorm(x, final_g)
```

```python
import contextlib
import numpy as np
import concourse.bass as bass
import concourse.mybir as mybir
from concourse.bass_utils import run_bass_kernel_spmd

F32 = mybir.dt.float32
BF16 = mybir.dt.bfloat16
AF = mybir.ActivationFunctionType
ALU = mybir.AluOpType

D = 1024
TT = 512
NEG = -30000.0
EPS = 1e-6
WINS = (2, 4, 8, 16)
NWSLOT = 3
USE_WCACHE = True


class Op:
    __slots__ = ("eng", "fn", "deps", "dma", "sem", "sigval", "needs_sig", "pos", "grp")


class Prog:
    ENGS = ("pe", "act", "dve", "pool", "sp")

    def __init__(self, nc, stack):
        self.nc = nc
        self.stack = stack
        self.ops = {e: [] for e in self.ENGS}
        self.lastw = {}
        self.readers = {}
        self.esem = {e: stack.enter_context(nc.semaphore("S_" + e)) for e in self.ENGS}
        self.dsem = {}
        self.dcount = {}
        self.cur_grp = {e: None for e in self.ENGS}
        self.grp_ctr = 0

    def group_begin(self, eng):
        self.grp_ctr += 1
        self.cur_grp[eng] = self.grp_ctr

    def group_end(self, eng):
        self.cur_grp[eng] = None

    def op(self, eng, fn, reads=(), writes=(), dma_slot=None, ndma=1):
        o = Op()
        o.eng = eng
        o.fn = fn
        o.dma = dma_slot is not None
        o.needs_sig = False
        o.sem = None
        o.sigval = 0
        o.pos = len(self.ops[eng])
        o.grp = self.cur_grp[eng]
        if eng != "pe":
            extra = [k for k in reads if isinstance(k, tuple) and k[0] == "ps" and k not in writes]
            if extra:
                writes = list(writes) + extra
        deps = []
        for k in reads:
            w = self.lastw.get(k)
            if w is not None:
                deps.append(w)
        for k in writes:
            w = self.lastw.get(k)
            if w is not None:
                deps.append(w)
            r = self.readers.get(k)
            if r:
                deps.extend(r[0].values())
                deps.extend(r[1])
        best = {}
        dl = []
        seen = set()
        for d in deps:
            if d.dma:
                if id(d) not in seen:
                    seen.add(id(d))
                    dl.append(d)
            else:
                if d.eng == "pe" and eng == "pe" and not o.dma:
                    continue
                b = best.get(d.eng)
                if b is None or d.pos > b.pos:
                    best[d.eng] = d
        for d in best.values():
            d.needs_sig = True
            dl.append(d)
        o.deps = dl
        for k in reads:
            r = self.readers.get(k)
            if r is None:
                r = self.readers[k] = ({}, [])
            if o.dma:
                r[1].append(o)
            else:
                r[0][eng] = o
        for k in writes:
            self.lastw[k] = o
            self.readers[k] = ({}, [])
        if o.dma:
            if dma_slot not in self.dsem:
                self.dsem[dma_slot] = self.stack.enter_context(self.nc.semaphore("D_" + dma_slot))
                self.dcount[dma_slot] = 0
            self.dcount[dma_slot] += 16 * ndma
            o.sem = self.dsem[dma_slot]
            o.sigval = self.dcount[dma_slot]
        self.ops[eng].append(o)
        return o

    def emit(self):
        for e in self.ENGS:
            c = 0
            for o in self.ops[e]:
                if not o.dma and o.needs_sig:
                    c += 1
                    o.sigval = c
                    o.sem = self.esem[e]
        with self.nc.Block() as block:
            decos = {"pe": block.tensor, "act": block.scalar, "dve": block.vector,
                     "pool": block.gpsimd, "sp": block.sync}
            for e in self.ENGS:
                def body(eng, e=e):
                    waited = {}
                    ops = self.ops[e]
                    n_ops = len(ops)
                    idx = 0
                    while idx < n_ops:
                        o = ops[idx]
                        j = idx + 1
                        if o.grp is not None:
                            while j < n_ops and ops[j].grp == o.grp:
                                j += 1
                        need = {}
                        for oo in ops[idx:j]:
                            for d in oo.deps:
                                if d.eng == e and (not d.dma) and d.pos >= idx:
                                    continue
                                k = id(d.sem)
                                if d.sigval > need.get(k, (None, 0))[1]:
                                    need[k] = (d.sem, d.sigval)
                        for k, (sem, val) in need.items():
                            if waited.get(k, 0) >= val:
                                continue
                            eng.wait_ge(sem, val)
                            waited[k] = val
                        for oo in ops[idx:j]:
                            ins = oo.fn(eng)
                            if oo.dma:
                                for i_ in (ins if isinstance(ins, (list, tuple)) else [ins]):
                                    i_.then_inc(oo.sem, 16)
                            elif oo.needs_sig:
                                ins.then_inc(oo.sem, 1)
                        idx = j
                    if e == "sp":
                        for slot, sem in self.dsem.items():
                            eng.wait_ge(sem, self.dcount[slot])
                decos[e](body)


def build_nc(SEQ=4096, do0=True, do1=True, final=True, overlap=True):
    NT = SEQ // TT
    NBLK = SEQ // 128
    nc = bass.Bass("TRN2", target_bir_lowering=False)

    def din(name, shape):
        return nc.dram_tensor(name, list(shape), F32, kind="ExternalInput").ap()

    x_d = din("x", [SEQ, D])
    cT_d = din("cT", [128, 8])
    ngT_d = din("normgT", [128, 2, 8])
    adaw_d = din("ada_w", [2, D, 3 * D])
    adab_d = din("ada_b", [1, 2 * 3 * D])
    win0_d = din("w_in0", [D, 6144])
    poolw_d = din("pool_w", [4, 256, 256])
    pscT_d = din("pscaleT", [128, 8])
    wout0_d = din("w_out0", [2048, D])
    win1_d = din("w_in1", [D, 8192])
    cwT_d = din("convwT", [128, 16, 3])
    cbT_d = din("convbT", [128, 16])
    wout1_d = din("w_out1", [2048, D])
    fg_d = din("final_g", [1, D])
    out_d = nc.dram_tensor("out", [SEQ, D], F32, kind="ExternalOutput").ap()
    kd = nc.dram_tensor("kd", [8, 128, SEQ], BF16).ap()
    vd = nc.dram_tensor("vd", [8, 128, NBLK, 128], BF16).ap()
    NPIECE = 64
    wc = nc.dram_tensor("wcache", [NPIECE, 128, 8, 512], BF16).ap()

    with contextlib.ExitStack() as stack:
        P = Prog(nc, stack)

        def sb(name, shape, dtype):
            return stack.enter_context(nc.sbuf_tensor("s_" + name, list(shape), dtype))

        psall = stack.enter_context(nc.psum_tensor("psall", [128, 8, 512], F32))
        ps = [psall[:, b, :] for b in range(8)]
        psk = [("ps", b) for b in range(8)]
        bank_rr = [0]

        def nbank():
            b = bank_rr[0]
            bank_rr[0] = (b + 1) % 8
            return b

        onesf = sb("onesf", [128, 128], F32)
        identf = sb("identf", [128, 128], F32)
        tmpf = sb("tmpf", [128, 128], F32)
        negf = sb("negf", [128, 128], F32)
        tri = sb("tri", [128, 128], BF16)
        onesb = sb("onesb", [128, 128], BF16)
        identb = sb("identb", [128, 128], BF16)
        maskb = sb("maskb", [128, 128], BF16)
        zerob = sb("zerob", [128, 128], BF16)
        invc = sb("invc", [128, 4, 16], F32)
        cT = sb("cT", [128, 8], F32)
        scb = sb("scb", [128, 8], BF16)
        ngT = sb("ngT", [128, 2, 8], F32)
        GT = sb("GT", [128, 2, 8], F32)
        shT = sb("shT", [128, 2, 8], F32)
        G1bc = sb("G1bc", [128, 2, D], F32)
        fgbc = sb("fgbc", [128, D], F32)
        pscT = sb("pscT", [128, 8], F32)
        cwT = sb("cwT", [128, 16, 3], F32)
        cbT = sb("cbT", [128, 16], F32)
        uh = sb("uh", [128, 8, 16], F32)
        uuh = sb("uuh", [128, 16, 2], F32)
        ss = sb("ss", [128, 8], F32)
        rstd = sb("rstd", [128, 8], F32)
        epsb = sb("epsb", [128, 1], F32)
        xt = [sb("xt%d" % j, [128, 4, D], F32) for j in range(2)]
        xn = sb("xn", [128, 4, D], F32)
        MSBK = [("xn", 0), ("xn", 1), ("xn", 2)]
        ADBK = [("xt", 1)]

        def msb_ap(c0, c1):
            b = c0 // D
            assert (c1 - 1) // D == b
            return xn[0:1, b, c0 - b * D:c1 - b * D]

        def adab_ap(c0, c1):
            b = c0 // D
            assert (c1 - 1) // D == b
            return xt[1][0:1, b, c0 - b * D:c1 - b * D]
        fgrow = xn[0:1, 3, :]
        hT = sb("hT", [128, 8, TT], BF16)
        qpad = sb("qpad", [128, 16, TT], BF16)
        pT = sb("pT", [128, 8, TT], BF16)
        yT = sb("yT", [128, 16, TT], BF16)
        kst = pT
        vst = sb("vst", [128, 4, D], BF16)
        kbuf = [sb("kbuf%d" % j, [128, SEQ], BF16) for j in range(2)]
        vbuf = [sb("vbuf%d" % j, [128, NBLK, 128], BF16) for j in range(2)]
        F2 = [sb("F2_%d" % j, [128, 2, 528], F32) for j in range(5)]
        Fs = [F2[j // 2][:, j % 2, :] for j in range(10)]
        SP2 = [sb("SP2_%d" % j, [128, 2, 512], BF16) for j in range(2)]
        A2 = [sb("A2_%d" % j, [128, 2, 512], BF16) for j in range(2)]
        otri = sb("otri", [128, 128], BF16)
        wsl = [sb("w%d" % j, [128, 8, 512], BF16) for j in range(NWSLOT)]
        wcnt = [0]

        def MM(out, lhsT, rhs, start, stop, reads, writes, sgc=False):
            if sgc:
                P.op("pe", lambda e: e.matmul(out, lhsT, rhs, start=start, stop=stop, skip_group_check=True),
                     reads, writes)
            else:
                P.op("pe", lambda e: e.matmul(out, lhsT, rhs, start=start, stop=stop), reads, writes)

        def TR(out, in_, reads, writes):
            P.op("pe", lambda e: e.transpose(out, in_, identf[:]), list(reads) + ["identf"], writes)

        def ACT(out, in_, func, reads, writes, bias=None, scale=None, accum=None):
            kw = {}
            if bias is not None:
                kw["bias"] = bias
            if scale is not None:
                kw["scale"] = scale
            if accum is not None:
                kw["accum_out"] = accum
            P.op("act", lambda e: e.activation(out, in_, func, **kw), reads, writes)

        def TTo(eng, out, in0, in1, op, reads, writes):
            P.op(eng, lambda e: e.tensor_tensor(out, in0, in1, op), reads, writes)

        def TS(eng, out, in0, s1, s2, op0, op1, reads, writes):
            if s2 is None:
                P.op(eng, lambda e: e.tensor_scalar(out, in0, s1, None, op0), reads, writes)
            else:
                P.op(eng, lambda e: e.tensor_scalar(out, in0, s1, s2, op0, op1), reads, writes)

        def STT(eng, out, in0, sc, in1, op0, op1, reads, writes):
            P.op(eng, lambda e: e.scalar_tensor_tensor(out, in0, sc, in1, op0, op1), reads, writes)

        def CP(eng, out, in_, reads, writes):
            if eng == "act":
                P.op("act", lambda e: e.activation(out, in_, AF.Copy), reads, writes)
            else:
                P.op(eng, lambda e: e.tensor_copy(out, in_), reads, writes)

        def MS(eng, ap, val, writes):
            P.op(eng, lambda e: e.memset(ap, val), (), writes)

        tctx = {"i": -1, "k": 0}

        def wload(dst_fn, src, piece=None, tile=None):
            s = wcnt[0] % NWSLOT
            wcnt[0] += 1
            dst = dst_fn(wsl[s])
            if piece is None:
                tile = tctx["i"]
                if tile >= 0:
                    piece = tctx["k"]
                    tctx["k"] += 1
            if piece is None or tile < 0 or not USE_WCACHE:
                P.op("pool", lambda e: e.dma_start(out=dst, in_=src), (), [("w", s)], dma_slot="w%d" % s)
                return s
            assert piece < NPIECE
            cv = dst_fn(wc[piece])
            if tile == 0:
                P.op("pool", lambda e: e.dma_start(out=dst, in_=src), (), [("w", s)], dma_slot="w%d" % s)
                P.op("sp", lambda e: e.dma_start(out=cv, in_=dst), [("w", s)], [("wc", piece)], dma_slot="wst%d" % s)
            else:
                P.op("pool", lambda e: e.dma_start(out=dst, in_=cv), [("wc", piece)], [("w", s)], dma_slot="w%d" % s)
            return s

        def wload_std(wd, col0, piece=None, tile=None):
            src = wd[:, col0:col0 + 512].rearrange("(kc p) n -> p kc n", p=128)
            return wload(lambda w: w[:, :, :], src, piece, tile)

        MS("pool", onesf[:], 1.0, ["onesf"])
        MS("pool", epsb[:], EPS, ["epsb"])
        MS("pool", negf[:], NEG, ["negf"])
        MS("pool", zerob[:], 0.0, ["zerob"])
        MS("pool", onesb[:], 1.0, ["onesb"])
        MS("pool", qpad[:], 0.0, ["qpad"])
        MS("pool", uh[:], 0.0, ["uh"])
        MS("pool", uuh[:], 0.0, ["uuh"])
        P.op("pool", lambda e: e.affine_select(out=identf[:], in_=onesf[:], pattern=[[-1, 128]],
                                               compare_op=ALU.is_equal, fill=0.0, base=0,
                                               channel_multiplier=1), ["onesf"], ["identf"])
        P.op("pool", lambda e: e.affine_select(out=tmpf[:], in_=onesf[:], pattern=[[-1, 128]],
                                               compare_op=ALU.is_ge, fill=0.0, base=0,
                                               channel_multiplier=1), ["onesf"], ["tmpf"])
        CP("dve", tri[:], tmpf[:], ["tmpf"], ["tri"])
        TS("dve", otri[:], tmpf[:], -1.0, 1.0, ALU.mult, ALU.add, ["tmpf"], ["otri"])
        CP("dve", identb[:], identf[:], ["identf"], ["identb"])
        P.op("pool", lambda e: e.affine_select(out=tmpf[:], in_=negf[:], pattern=[[-1, 128]],
                                               compare_op=ALU.is_ge, fill=0.0, base=0,
                                               channel_multiplier=1), ["negf"], ["tmpf"])
        CP("dve", maskb[:], tmpf[:], ["tmpf"], ["maskb"])
        for g, w in enumerate(WINS):
            for t in range(w - 1):
                MS("dve", invc[:, g, t:t + 1], 1.0 / (t + 1), ["invc"])
            MS("dve", invc[:, g, w - 1:16], 1.0 / w, ["invc"])

        def const_loads(e):
            return [
                e.dma_start(out=cT[:], in_=cT_d),
                e.dma_start(out=ngT[:], in_=ngT_d),
                e.dma_start(out=pscT[:], in_=pscT_d),
                e.dma_start(out=cwT[:], in_=cwT_d),
                e.dma_start(out=cbT[:], in_=cbT_d),
                e.dma_start(out=fgrow, in_=fg_d),
            ]
        P.op("sp", const_loads, (), ["cT", "ngT", "pscT", "cwT", "cbT", ("xn", 3)],
             dma_slot="const", ndma=6)

        def xload(i):
            j = i % 2
            src = x_d[i * TT:(i + 1) * TT, :].rearrange("(b p) d -> p b d", p=128)
            P.op("sp", lambda e: e.dma_start(out=xt[j][:], in_=src), (), [("xt", j)], dma_slot="x%d" % j)

        xload(0)

        ACT(scb[:], cT[:], AF.Silu, ["cT"], ["scb"])
        for l in range(2):
            if (l == 0 and not do0) or (l == 1 and not do1):
                continue
            P.op("sp", lambda e, l=l: e.dma_start(
                out=xt[1][0:1, 0:3, :], in_=adab_d[0:1, l * 3072:(l + 1) * 3072].rearrange("o (b n) -> o b n", n=D)),
                (), ADBK, dma_slot="ab")
            for kc in range(8):
                src = adaw_d[l, kc * 128:(kc + 1) * 128, :].rearrange("p (g n) -> p g n", n=512)
                s = wload(lambda w: w[:, 0:6, :], src)
                for g in range(6):
                    MM(ps[g][0:1, :], scb[:, kc:kc + 1], wsl[s][:, g, :], kc == 0, kc == 7,
                       ["scb", ("w", s)], [psk[g]])
            for g in range(6):
                TTo("dve", msb_ap(g * 512, (g + 1) * 512), ps[g][0:1, :], adab_ap(g * 512, (g + 1) * 512), ALU.add,
                    [psk[g]] + ADBK, MSBK)
            for b_ in (1, 2):
                TS("dve", xn[0:1, b_, :], xn[0:1, b_, :], 1.0, None, ALU.add, None, MSBK, MSBK)
            for h in range(2):
                MM(ps[6][:, :], onesf[0:1, 0:128], msb_ap(2048 + h * 512, 2048 + (h + 1) * 512),
                   True, True, ["onesf"] + MSBK, [psk[6]])
                CP("act", G1bc[:, l, h * 512:(h + 1) * 512], ps[6][:, :], [psk[6]], [("G1bc", l)])
            for j in range(16):
                MM(ps[7][:, j:j + 1], msb_ap(j * 128, (j + 1) * 128), onesf[0:1, 0:1], True, True,
                   MSBK + ["onesf"], [psk[7]])
            CP("dve", shT[:, l, :], ps[7][:, 0:8], [psk[7]], [("shT", l)])
            TTo("dve", GT[:, l, :], ps[7][:, 8:16], ngT[:, l, :], ALU.mult, [psk[7], "ngT"], [("GT", l)])
        if final:
            for h in range(2):
                MM(ps[6][:, :], onesf[0:1, 0:128], fgrow[:, h * 512:(h + 1) * 512], True, True,
                   ["onesf", ("xn", 3)], [psk[6]])
                CP("act", fgbc[:, h * 512:(h + 1) * 512], ps[6][:, :], [psk[6]], ["fgbc"])

        def stage_norm(i, l, X):
            xk = ("xt", i % 2)
            for blk in range(4):
                ACT(xn[:, blk, :], X[:, blk, :], AF.Square, [xk], [("xn", blk), ("ss", blk)],
                    accum=ss[:, blk:blk + 1])
                ACT(rstd[:, blk:blk + 1], ss[:, blk:blk + 1], AF.Ln, [("ss", blk), "epsb"], [("rstd", blk)],
                    bias=epsb[:, 0:1], scale=1.0 / D)
                ACT(rstd[:, blk:blk + 1], rstd[:, blk:blk + 1], AF.Exp, [("rstd", blk)], [("rstd", blk)],
                    scale=-0.5)
                TS("dve", xn[:, blk, :], X[:, blk, :], rstd[:, blk:blk + 1], None, ALU.mult, None,
                   [xk, ("rstd", blk)], [("xn", blk)])
            for kc in range(8):
                for blk in range(4):
                    TR(ps[kc][:, blk * 128:(blk + 1) * 128], xn[:, blk, kc * 128:(kc + 1) * 128],
                       [("xn", blk)], [psk[kc]])
            for kc in range(8):
                ACT(hT[:, kc, :], ps[kc][:, :], AF.Identity, [psk[kc], ("GT", l), ("shT", l)], [("hT", kc)],
                    bias=shT[:, l, kc:kc + 1], scale=GT[:, l, kc:kc + 1])

        def proj_fm(s, c, b):
            for kc in range(8):
                MM(ps[b][:, :], wsl[s][:, kc, c * 128:(c + 1) * 128], hT[:, kc, :], kc == 0, kc == 7,
                   [("w", s), ("hT", kc)], [psk[b]])

        def stage_outproj(i, l, X, wd):
            xk = ("xt", i % 2)
            for half in range(2):
                banks = [nbank() for _ in range(4)]
                for j in range(2):
                    src = wd[j * 1024:(j + 1) * 1024, half * 512:(half + 1) * 512].rearrange(
                        "(kc p) n -> p kc n", p=128)
                    s = wload(lambda w: w[:, :, :], src)
                    for blk in range(4):
                        for kc in range(8):
                            ch = j * 8 + kc
                            MM(ps[banks[blk]][:, :], yT[:, ch, blk * 128:(blk + 1) * 128], wsl[s][:, kc, :],
                               j == 0 and kc == 0, j == 1 and kc == 7,
                               [("yT", ch), ("w", s)], [psk[banks[blk]]])
                for blk in range(4):
                    f = 6 + (blk % 2)
                    TTo("dve", Fs[f][:, 0:512], ps[banks[blk]][:, :], G1bc[:, l, half * 512:(half + 1) * 512],
                        ALU.mult, [psk[banks[blk]], ("G1bc", l)], [("F", f)])
                    TTo("dve", X[:, blk, half * 512:(half + 1) * 512], X[:, blk, half * 512:(half + 1) * 512],
                        Fs[f][:, 0:512], ALU.add, [("F", f), xk], [xk])

        BGB = [6, 7]
        bgb_ctr = [0]

        def bg_bank():
            b = BGB[bgb_ctr[0] % 2]
            bgb_ctr[0] += 1
            return b

        def run_all(units):
            for u in units:
                for _ in u():
                    pass

        def y4ap(yb, c, a=0, b=512):
            return vst[:, 2 * yb + c // 2, (c % 2) * 512 + a:(c % 2) * 512 + b]

        def norm_units_l1(i):
            X = xt[i % 2]
            xk = ("xt", i % 2)
            units = []
            for blk in range(4):
                for half in range(2):
                    def u(blk=blk, half=half):
                        if half == 0:
                            ACT(xn[:, blk, :], X[:, blk, :], AF.Square, [xk], [("xn", blk), ("ss", blk)],
                                accum=ss[:, blk:blk + 1])
                            ACT(rstd[:, blk:blk + 1], ss[:, blk:blk + 1], AF.Ln, [("ss", blk), "epsb"],
                                [("rstd", blk)], bias=epsb[:, 0:1], scale=1.0 / D)
                            ACT(rstd[:, blk:blk + 1], rstd[:, blk:blk + 1], AF.Exp, [("rstd", blk)],
                                [("rstd", blk)], scale=-0.5)
                            TS("dve", xn[:, blk, :], X[:, blk, :], rstd[:, blk:blk + 1], None, ALU.mult, None,
                               [xk, ("rstd", blk)], [("xn", blk)])
                            yield
                        b = bg_bank()
                        for k4 in range(4):
                            kc = half * 4 + k4
                            TR(ps[b][:, k4 * 128:(k4 + 1) * 128], xn[:, blk, kc * 128:(kc + 1) * 128],
                               [("xn", blk)], [psk[b]])
                            if k4 == 1:
                                yield
                        yield
                        for k4 in range(4):
                            kc = half * 4 + k4
                            TS("dve", hT[:, kc, blk * 128:(blk + 1) * 128], ps[b][:, k4 * 128:(k4 + 1) * 128],
                               GT[:, 1, kc:kc + 1], shT[:, 1, kc:kc + 1], ALU.mult, ALU.add,
                               [psk[b], ("GT", 1), ("shT", 1)], [("hT", kc)])
                    units.append(u)
            return units

        def proj_steps(s, c, b):
            for kc in range(8):
                MM(ps[b][:, :], wsl[s][:, kc, c * 128:(c + 1) * 128], hT[:, kc, :], kc == 0, kc == 7,
                   [("w", s), ("hT", kc)], [psk[b]])
                if kc % 2 == 1 and kc < 7:
                    yield

        def layer1_units(i):
            X = xt[i % 2]
            xk = ("xt", i % 2)
            units = list(norm_units_l1(i))
            first_norm = units[0]

            def kick():
                need(0)
                yield from first_norm()
            units[0] = kick
            USB = [xn[:, 0, 0:512], xn[:, 0, 512:1024], xn[:, 1, 0:512], xn[:, 1, 512:1024]]
            USK = [("xn", 0), ("xn", 0), ("xn", 1), ("xn", 1)]
            UU, UK = xn[:, 2, 0:514], ("xn", 2)
            T1, TK = xn[:, 3, 0:512], ("xn", 3)
            T2 = xn[:, 3, 512:1024]
            PB = 40
            slots = {}

            def do_load(k):
                q, t = k // 5, k % 5
                if t < 4:
                    col = [(12 + q), (8 + q), (4 + q), q][t] * 512
                    return wload_std(win1_d, col, PB + k, i)
                src = wout1_d[q * 512:(q + 1) * 512, :].rearrange("(kc p) n -> p kc n", p=128)
                return wload(lambda w: w[:, :, :].rearrange("p a b -> p (a b)").rearrange(
                    "p (a b) -> p a b", b=1024), src, PB + k, i)

            def need(k):
                for kk in (k, k + 1):
                    if kk < 20 and kk not in slots:
                        slots[kk] = do_load(kk)
                return slots[k]

            for q in range(4):
                yb = q % 2
                hg, hu, hc, hb, ho = [None], [None], [None], [None], [None]
                for c in range(4):
                    def u(c=c, hg=hg, yb=yb, q=q):
                        if c == 0:
                            hg[0] = need(q * 5 + 0)
                        b = bg_bank()
                        yield from proj_steps(hg[0], c, b)
                        yield
                        f = T1 if c % 2 == 0 else T2
                        ACT(f, ps[b][:, :], AF.Exp, [psk[b]], [TK], scale=-1.0)
                        TS("dve", f, f, 1.0, None, ALU.add, None, [TK], [TK])
                        P.op("dve", lambda e, o=f: e.reciprocal(o, o), [TK], [TK])
                        TTo("dve", y4ap(yb, c), ps[b][:, :], f, ALU.mult, [psk[b], TK], ["vst"])
                    units.append(u)
                for c in range(4):
                    def u(c=c, hu=hu, q=q):
                        if c == 0:
                            hu[0] = need(q * 5 + 1)
                        b = bg_bank()
                        yield from proj_steps(hu[0], c, b)
                        yield
                        CP("dve", USB[c], ps[b][:, :], [psk[b]], [USK[c]])
                    units.append(u)
                for c in range(4):
                    def u(c=c, hc=hc, q=q):
                        if c == 0:
                            hc[0] = need(q * 5 + 2)
                        ch = q * 4 + c
                        b = bg_bank()
                        yield from proj_steps(hc[0], c, b)
                        yield
                        CP("dve", UU[:, 0:2], uuh[:, ch, :], [("uuh", ch)], [UK])
                        TTo("dve", UU[:, 2:514], ps[b][:, :], USB[c], ALU.mult, [psk[b], USK[c]], [UK])
                        CP("dve", uuh[:, ch, :], UU[:, 512:514], [UK], [("uuh", ch)])
                        TS("dve", T1, UU[:, 2:514], cwT[:, ch, 2:3], cbT[:, ch:ch + 1],
                           ALU.mult, ALU.add, [UK, "cwT", "cbT"], [TK])
                        STT("dve", T1, UU[:, 1:513], cwT[:, ch, 1:2], T1, ALU.mult, ALU.add, [UK, "cwT", TK], [TK])
                        STT("dve", USB[c], UU[:, 0:512], cwT[:, ch, 0:1], T1, ALU.mult, ALU.add,
                            [UK, "cwT", TK], [USK[c]])
                    units.append(u)
                for c in range(4):
                    def u(c=c, hb=hb, yb=yb, q=q):
                        if c == 0:
                            hb[0] = need(q * 5 + 3)
                        b = bg_bank()
                        yield from proj_steps(hb[0], c, b)
                        yield
                        TTo("dve", T1, ps[b][:, :], USB[c], ALU.mult, [psk[b], USK[c]], [TK])
                        TTo("dve", y4ap(yb, c), T1, y4ap(yb, c), ALU.mult, [TK, "vst"], ["vst"])
                    units.append(u)
                for half in range(2):
                    for blk in range(4):
                        def u(half=half, blk=blk, ho=ho, yb=yb, q=q):
                            if half == 0 and blk == 0:
                                ho[0] = need(q * 5 + 4)
                            b = bg_bank()
                            wv = wsl[ho[0]][:, :, :].rearrange("p a b -> p (a b)").rearrange("p (a b) -> p a b", b=1024)
                            for c in range(4):
                                MM(ps[b][:, :], y4ap(yb, c, blk * 128, (blk + 1) * 128),
                                   wv[:, c, half * 512:(half + 1) * 512], c == 0, c == 3,
                                   ["vst", ("w", ho[0])], [psk[b]])
                                if c == 1:
                                    yield
                            yield
                            f = T1 if blk % 2 == 0 else T2
                            TTo("dve", f, ps[b][:, :], G1bc[:, 1, half * 512:(half + 1) * 512],
                                ALU.mult, [psk[b], ("G1bc", 1)], [TK])
                            TTo("dve", X[:, blk, half * 512:(half + 1) * 512], X[:, blk, half * 512:(half + 1) * 512],
                                f, ALU.add, [TK, xk], [xk])
                        units.append(u)
            return units

        def finish_units(i):
            def mk(blk):
                def u():
                    finish_blk(i, blk)
                    return
                    yield
                return u
            return [mk(blk) for blk in range(4)]

        def layer0(i, bg=()):
            X = xt[i % 2]
            tok0 = i * TT
            stage_norm(i, 0, X)
            for g in range(2):
                s = wload_std(win0_d, (4 + g) * 512)
                for c in range(4):
                    hp = g * 4 + c
                    b = nbank()
                    proj_fm(s, c, b)
                    CP("act" if c % 2 == 0 else "dve", kst[:, hp, :], ps[b][:, :], [psk[b]], [("pT", hp)])
            P.op("sp", lambda e: e.dma_start(out=kd[:, :, tok0:tok0 + TT].rearrange("h p t -> p h t"),
                                             in_=kst[:]), [("pT", c_) for c_ in range(8)], [("kd", i)], dma_slot="ks")
            for vh in range(2):
                s = wload_std(win0_d, (6 + vh) * 512)
                for blk in range(4):
                    b = nbank()
                    for kc in range(8):
                        MM(ps[b][:, :], hT[:, kc, blk * 128:(blk + 1) * 128], wsl[s][:, kc, :], kc == 0, kc == 7,
                           [("hT", kc), ("w", s)], [psk[b]])
                    CP("act" if blk % 2 == 0 else "dve", vst[:, blk, vh * 512:(vh + 1) * 512], ps[b][:, :],
                       [psk[b]], ["vst"])

            def vstore(e):
                return [e.dma_start(out=vd[hp, :, 4 * i:4 * i + 4, :], in_=vst[:, :, hp * 128:(hp + 1) * 128])
                        for hp in range(8)]
            P.op("sp", vstore, ["vst"], [("vd", i)], dma_slot="vs", ndma=8)
            for g in range(4):
                s = wload_std(win0_d, (8 + g) * 512)
                for c in range(4):
                    ch = g * 4 + c
                    b = nbank()
                    proj_fm(s, c, b)
                    ACT(yT[:, ch, :], ps[b][:, :], AF.Silu, [psk[b]], [("yT", ch)])
            for g in range(2):
                s = wload_std(win0_d, (2 + g) * 512)
                for c in range(4):
                    hp = g * 4 + c
                    b = nbank()
                    proj_fm(s, c, b)
                    TS("dve", qpad[0:64, 2 * hp, :], ps[b][0:64, :], 0.125, None, ALU.mult, None,
                       [psk[b]], [("qp", 2 * hp)])
                    P.op("act", lambda e, o=qpad[64:128, 2 * hp + 1, :], n=ps[b][64:128, :]:
                         e.mul(o, n, 0.125), [psk[b]], [("qp", 2 * hp + 1)])
            for g2 in range(2):
                s = wload_std(win0_d, g2 * 512)
                for c4 in range(4):
                    c = g2 * 4 + c4
                    grp = c // 2
                    w = WINS[grp]
                    b = nbank()
                    proj_fm(s, c4, b)
                    U = Fs[c % 2]
                    uk = ("F", c % 2)
                    CP("dve", U[:, 0:16], uh[:, c, :], [("uh", c)], [uk])
                    CP("act", U[:, 16:528], ps[b][:, :], [psk[b]], [uk])
                    CP("dve", uh[:, c, :], U[:, 512:528], [uk], [("uh", c)])
                    ta, tb = (2, 3) if c % 2 == 0 else (4, 5)
                    TTo("dve", Fs[ta][:, 1:528], U[:, 1:528], U[:, 0:527], ALU.add, [uk], [("F", ta)])
                    cur = ta
                    sh = 2
                    while sh < w:
                        oth = tb if cur == ta else ta
                        TTo("dve", Fs[oth][:, 2 * sh - 1:528], Fs[cur][:, 2 * sh - 1:528],
                            Fs[cur][:, sh - 1:528 - sh], ALU.add, [("F", cur)], [("F", oth)])
                        cur = oth
                        sh *= 2
                    STT("dve", pT[:, c, :], Fs[cur][:, 16:528], 1.0 / w, U[:, 16:528], ALU.mult, ALU.subtract,
                        [("F", cur), uk], [("pT", c)])
                    if i == 0:
                        oth = tb if cur == ta else ta
                        TTo("dve", Fs[oth][:, 0:16], Fs[cur][:, 16:32], invc[:, grp, :], ALU.mult,
                            [("F", cur), "invc"], [("F", oth)])
                        TTo("dve", pT[:, c, 0:16], Fs[oth][:, 0:16], U[:, 16:32], ALU.subtract,
                            [("F", oth), uk], [("pT", c)])
            src = poolw_d.rearrange("g (cc p) d -> p (g cc) d", p=128)
            s = wload(lambda w: w[:, :, 0:256], src)
            for dch in range(8):
                grp, dd = dch // 2, dch % 2
                b = nbank()
                for cc in range(2):
                    MM(ps[b][:, :], wsl[s][:, grp * 2 + cc, dd * 128:(dd + 1) * 128], pT[:, grp * 2 + cc, :],
                       cc == 0, cc == 1, [("w", s), ("pT", grp * 2 + cc)], [psk[b]])
                STT("dve", yT[:, dch, :], ps[b][:, :], pscT[:, dch:dch + 1], yT[:, dch, :], ALU.mult, ALU.mult,
                    [psk[b], "pscT", ("yT", dch)], [("yT", dch)])
            ntok = (i + 1) * TT
            nblk = 4 * (i + 1)

            def kvload(hp):
                j = hp % 2
                rk = [("kd", t) for t in range(i + 1)]
                rv = [("vd", t) for t in range(i + 1)]
                P.op("sp", lambda e: e.dma_start(out=kbuf[j][:, 0:ntok], in_=kd[hp, :, 0:ntok]), rk,
                     [("kb", j)], dma_slot="kb%d" % j)
                P.op("sp", lambda e: e.dma_start(out=vbuf[j][:, 0:nblk, :], in_=vd[hp, :, 0:nblk, :]), rv,
                     [("vb", j)], dma_slot="vb%d" % j)

            kvload(0)
            if overlap:
                ZP = [0, 0]
                CBK = [2, 3]
                accb = [4, 5]
            else:
                ZP = [0, 2]
                CBK = [4, 5]
                accb = [6, 7]
            bg = list(bg)
            bgs = {"cur": None, "idx": 0, "steps": 0, "slot": 0}
            BG_STEPS_EST = 620

            def bg_step():
                while True:
                    if bgs["cur"] is None:
                        if bgs["idx"] >= len(bg):
                            return False
                        bgs["cur"] = bg[bgs["idx"]]()
                        bgs["idx"] += 1
                    try:
                        next(bgs["cur"])
                    except StopIteration:
                        bgs["cur"] = None
                    bgs["steps"] += 1
                    return True

            def run_bg(sub):
                if not bg:
                    return
                tgt = (BG_STEPS_EST * (3 * bgs["slot"] + sub + 1) + 3 * (8 * nblk + 2) - 1) // (3 * (8 * nblk + 2))
                if bgs["steps"] < tgt:
                    bg_step()

            def flush_bg():
                while bg_step():
                    pass
            items = [(hp, kb) for hp in range(8) for kb in range(nblk - 1, -1, -1)]
            n = len(items)
            st = {}

            def fk(w):
                return [("F", 2 * w), ("F", 2 * w + 1)]

            def A_pe(p):
                hp, kb = items[p]
                j = hp % 2
                r = kb - 4 * i
                col0 = r * 128 if r >= 0 else 0
                st[p] = (hp, kb, r, col0)
                z0 = ZP[p % 2]
                for hd in range(2):
                    zb = z0 + hd
                    MM(ps[zb][:, col0:512], kbuf[j][:, kb * 128:(kb + 1) * 128], qpad[:, 2 * hp + hd, col0:512],
                       True, r < 0, [("kb", j), ("qp", 2 * hp + hd)], [psk[zb]])
                    if r >= 0:
                        MM(ps[zb][:, col0:col0 + 128], identb[:, :], maskb[:, :], False, True,
                           ["identb", "maskb"], [psk[zb]])

            def EXP1(p):
                hp, kb, r, col0 = st[p]
                z0 = ZP[p % 2]
                e_i = p % 3
                ACT(F2[e_i][:, :, col0:512], psall[:, z0:z0 + 2, col0:512], AF.Exp,
                    [psk[z0], psk[z0 + 1]], fk(e_i))

            def LN(p):
                hp, kb, r, col0 = st[p]
                e_i = p % 3
                ACT(SP2[p % 2][:, :, col0:512], F2[e_i][:, :, col0:512], AF.Ln, fk(e_i), [("SP2", p % 2)],
                    bias=1.0)

            def T_pe(p):
                hp, kb, r, col0 = st[p]
                if kb == nblk - 1:
                    for hd in range(2):
                        MM(ps[CBK[hd]][:, :], zerob[:, :], hT[:, 0, :], True, False,
                           ["zerob", ("hT", 0)], [psk[CBK[hd]]], sgc=True)
                for hd in range(2):
                    MM(ps[CBK[hd]][:, col0:512], tri[:, :], SP2[p % 2][:, hd, col0:512], False, False,
                       ["tri", ("SP2", p % 2)], [psk[CBK[hd]]], sgc=True)

            def O_pe(p):
                hp, kb, r, col0 = st[p]
                if kb > 0:
                    for hd in range(2):
                        MM(ps[CBK[hd]][:, col0:512], otri[:, :], SP2[p % 2][:, hd, col0:512], False, False,
                           ["otri", ("SP2", p % 2)], [psk[CBK[hd]]], sgc=True)

            def EXP2_F(p):
                hp, kb, r, col0 = st[p]
                e_i = p % 3
                w_i = 3 + p % 2
                ACT(F2[w_i][:, :, col0:512], psall[:, CBK[0]:CBK[0] + 2, col0:512], AF.Exp,
                    [psk[CBK[0]], psk[CBK[1]]], fk(w_i), scale=-1.0)
                TTo("dve", A2[p % 2][:, :, col0:512], F2[e_i][:, :, col0:512], F2[w_i][:, :, col0:512], ALU.mult,
                    fk(e_i) + fk(w_i), [("A2", p % 2)])

            def G_pe(p):
                hp, kb, r, col0 = st[p]
                j = hp % 2
                if kb == nblk - 1:
                    if hp + 1 < 8:
                        kvload(hp + 1)
                    for hd in range(2):
                        MM(ps[accb[hd]][:, :], zerob[:, :], hT[:, 0, :], True, False,
                           ["zerob", ("hT", 0)], [psk[accb[hd]]])
                for hd in range(2):
                    MM(ps[accb[hd]][:, col0:512], vbuf[j][:, kb, :], A2[p % 2][:, hd, col0:512], False, kb == 0,
                       [("vb", j), ("A2", p % 2)], [psk[accb[hd]]])

            def Y_evac(p):
                hp, kb, r, col0 = st[p]
                if kb == 0:
                    for hd in range(2):
                        pr = slice(hd * 64, (hd + 1) * 64)
                        TTo("dve", yT[pr, 8 + hp, :], ps[accb[hd]][pr, :], yT[pr, 8 + hp, :], ALU.mult,
                            [psk[accb[hd]], ("yT", 8 + hp)], [("yT", 8 + hp)])

            A_pe(0)
            for p in range(n + 2):
                bgs["slot"] = p
                if overlap:
                    P.group_begin("pe")
                    if 0 <= p - 1 < n:
                        T_pe(p - 1)
                    P.group_end("pe")
                    run_bg(0)
                    if p < n:
                        EXP1(p)
                    P.group_begin("pe")
                    if p + 1 < n:
                        A_pe(p + 1)
                    P.group_end("pe")
                    run_bg(1)
                else:
                    P.group_begin("pe")
                    if 0 <= p - 1 < n:
                        T_pe(p - 1)
                    if p + 1 < n:
                        A_pe(p + 1)
                    P.group_end("pe")
                    if p < n:
                        EXP1(p)
                if 0 <= p - 1 < n:
                    EXP2_F(p - 1)
                if p < n:
                    LN(p)
                P.group_begin("pe")
                if 0 <= p - 2 < n:
                    G_pe(p - 2)
                if 0 <= p - 1 < n:
                    O_pe(p - 1)
                P.group_end("pe")
                if 0 <= p - 2 < n:
                    Y_evac(p - 2)
                if overlap:
                    run_bg(2)
            flush_bg()
            stage_outproj(i, 0, X, wout0_d)

        def layer1(i):
            X = xt[i % 2]
            stage_norm(i, 1, X)
            for q in range(4):
                s = wload_std(win1_d, (12 + q) * 512)
                for c in range(4):
                    ch = q * 4 + c
                    b = nbank()
                    proj_fm(s, c, b)
                    ACT(yT[:, ch, :], ps[b][:, :], AF.Silu, [psk[b]], [("yT", ch)])
                s = wload_std(win1_d, (8 + q) * 512)
                for c in range(4):
                    b = nbank()
                    proj_fm(s, c, b)
                    CP("act", Fs[c][:, 0:512], ps[b][:, :], [psk[b]], [("F", c)])
                s = wload_std(win1_d, (4 + q) * 512)
                for c in range(4):
                    ch = q * 4 + c
                    b = nbank()
                    proj_fm(s, c, b)
                    UU = Fs[4 + c % 2]
                    uk = ("F", 4 + c % 2)
                    CP("dve", UU[:, 0:2], uuh[:, ch, :], [("uuh", ch)], [uk])
                    TTo("dve", UU[:, 2:514], ps[b][:, :], Fs[c][:, 0:512], ALU.mult, [psk[b], ("F", c)], [uk])
                    CP("dve", uuh[:, ch, :], UU[:, 512:514], [uk], [("uuh", ch)])
                    t1 = 6 + c % 2
                    TS("dve", Fs[t1][:, 0:512], UU[:, 2:514], cwT[:, ch, 2:3], cbT[:, ch:ch + 1], ALU.mult, ALU.add,
                       [uk, "cwT", "cbT"], [("F", t1)])
                    STT("dve", Fs[t1][:, 0:512], UU[:, 1:513], cwT[:, ch, 1:2], Fs[t1][:, 0:512], ALU.mult, ALU.add,
                        [uk, "cwT", ("F", t1)], [("F", t1)])
                    STT("dve", Fs[c][:, 0:512], UU[:, 0:512], cwT[:, ch, 0:1], Fs[t1][:, 0:512], ALU.mult, ALU.add,
                        [uk, "cwT", ("F", t1)], [("F", c)])
                s = wload_std(win1_d, q * 512)
                for c in range(4):
                    ch = q * 4 + c
                    b = nbank()
                    proj_fm(s, c, b)
                    t1 = 6 + c % 2
                    TTo("dve", Fs[t1][:, 0:512], ps[b][:, :], Fs[c][:, 0:512], ALU.mult, [psk[b], ("F", c)],
                        [("F", t1)])
                    TTo("dve", yT[:, ch, :], Fs[t1][:, 0:512], yT[:, ch, :], ALU.mult, [("F", t1), ("yT", ch)],
                        [("yT", ch)])
            stage_outproj(i, 1, X, wout1_d)

        def finish_blk(i, blk):
            X = xt[i % 2]
            xk = ("xt", i % 2)
            if final:
                ACT(xn[:, blk, :], X[:, blk, :], AF.Square, [xk], [("xn", blk), ("ss", blk)],
                    accum=ss[:, blk:blk + 1])
                ACT(rstd[:, blk:blk + 1], ss[:, blk:blk + 1], AF.Ln, [("ss", blk), "epsb"], [("rstd", blk)],
                    bias=epsb[:, 0:1], scale=1.0 / D)
                ACT(rstd[:, blk:blk + 1], rstd[:, blk:blk + 1], AF.Exp, [("rstd", blk)], [("rstd", blk)],
                    scale=-0.5)
                STT("dve", xn[:, blk, :], X[:, blk, :], rstd[:, blk:blk + 1], fgbc[:, :], ALU.mult, ALU.mult,
                    [xk, ("rstd", blk), "fgbc"], [("xn", blk)])
            else:
                CP("dve", xn[:, blk, :], X[:, blk, :], [xk], [("xn", blk)])
            r0 = i * TT + blk * 128
            P.op("sp", lambda e, r0=r0, blk=blk: e.dma_start(out=out_d[r0:r0 + 128, :], in_=xn[:, blk, :]),
                 [("xn", blk)], [("out", i, blk)], dma_slot="o%d" % blk)

        def finish(i):
            for blk in range(4):
                finish_blk(i, blk)

        pending = []
        for i in range(NT):
            tctx["i"] = i
            tctx["k"] = 0
            if not (overlap and do0 and do1):
                if i + 1 < NT:
                    xload(i + 1)
                if do0:
                    layer0(i)
                if do1:
                    layer1(i)
                finish(i)
                continue
            layer0(i, pending)
            if i + 1 < NT:
                xload(i + 1)
            pending = layer1_units(i) + finish_units(i)
        if pending:
            run_all(pending)

        P.emit()
    return nc


_NC_CACHE = {}


def _layout_inputs(x, c, norm_g, ada_w, ada_b, even_w_in, pool_w, pool_scale, even_w_out,
                   odd_w_in, conv_w, conv_b, odd_w_out, final_g):
    f = lambda a: np.ascontiguousarray(np.asarray(a, dtype=np.float32))
    shared = {
        "normgT": f(np.asarray(norm_g).reshape(2, 8, 128).transpose(2, 0, 1)),
        "ada_w": f(ada_w),
        "ada_b": f(np.asarray(ada_b).reshape(1, -1)),
        "w_in0": f(np.asarray(even_w_in)[0]),
        "pool_w": f(np.asarray(pool_w)[0]),
        "pscaleT": f(np.asarray(pool_scale)[0].reshape(8, 128).T),
        "w_out0": f(np.asarray(even_w_out)[0]),
        "w_in1": f(np.asarray(odd_w_in)[0]),
        "convwT": f(np.asarray(conv_w)[0].reshape(3, 16, 128).transpose(2, 1, 0)),
        "convbT": f(np.asarray(conv_b)[0].reshape(16, 128).T),
        "w_out1": f(np.asarray(odd_w_out)[0]),
        "final_g": f(np.asarray(final_g).reshape(1, -1)),
    }
    x = np.asarray(x)
    c = np.asarray(c)
    maps = []
    for b in range(x.shape[0]):
        m = dict(shared)
        m["x"] = f(x[b])
        m["cT"] = f(c[b].reshape(8, 128).T)
        maps.append(m)
    return maps


def kernel(x, c, norm_g, ada_w, ada_b, even_w_in, pool_w, pool_scale, even_w_out,
           odd_w_in, conv_w, conv_b, odd_w_out, final_g):
    x = np.asarray(x)
    B, S, _ = x.shape
    maps = _layout_inputs(x, c, norm_g, ada_w, ada_b, even_w_in, pool_w, pool_scale, even_w_out,
                          odd_w_in, conv_w, conv_b, odd_w_out, final_g)
    if S not in _NC_CACHE:
        _NC_CACHE[S] = build_nc(S)
    nc = _NC_CACHE[S]
    res = run_bass_kernel_spmd(nc, maps, core_ids=list(range(B)))
    return np.stack([np.asarray(r["out"], dtype=np.float32) for r in res.results], axis=0)
```

```python
import contextlib
import numpy as np
import concourse.bass as bass
import concourse.mybir as mybir
from concourse.bass_utils import run_bass_kernel_spmd

F32 = mybir.dt.float32
BF16 = mybir.dt.bfloat16
AF = mybir.ActivationFunctionType
ALU = mybir.AluOpType

D = 1024
TT = 512
NEG = -30000.0
EPS = 1e-6
WINS = (2, 4, 8, 16)
NWSLOT = 3
USE_WCACHE = True


class Op:
    __slots__ = ("eng", "fn", "deps", "dma", "sem", "sigval", "needs_sig", "pos", "grp")


class Prog:
    ENGS = ("pe", "act", "dve", "pool", "sp")

    def __init__(self, nc, stack):
        self.nc = nc
        self.stack = stack
        self.ops = {e: [] for e in self.ENGS}
        self.lastw = {}
        self.readers = {}
        self.esem = {e: stack.enter_context(nc.semaphore("S_" + e)) for e in self.ENGS}
        self.dsem = {}
        self.dcount = {}
        self.cur_grp = {e: None for e in self.ENGS}
        self.grp_ctr = 0

    def group_begin(self, eng):
        self.grp_ctr += 1
        self.cur_grp[eng] = self.grp_ctr

    def group_end(self, eng):
        self.cur_grp[eng] = None

    def op(self, eng, fn, reads=(), writes=(), dma_slot=None, ndma=1):
        o = Op()
        o.eng = eng
        o.fn = fn
        o.dma = dma_slot is not None
        o.needs_sig = False
        o.sem = None
        o.sigval = 0
        o.pos = len(self.ops[eng])
        o.grp = self.cur_grp[eng]
        if eng != "pe":
            extra = [k for k in reads if isinstance(k, tuple) and k[0] == "ps" and k not in writes]
            if extra:
                writes = list(writes) + extra
        deps = []
        for k in reads:
            w = self.lastw.get(k)
            if w is not None:
                deps.append(w)
        for k in writes:
            w = self.lastw.get(k)
            if w is not None:
                deps.append(w)
            r = self.readers.get(k)
            if r:
                deps.extend(r[0].values())
                deps.extend(r[1])
        best = {}
        dl = []
        seen = set()
        for d in deps:
            if d.dma:
                if id(d) not in seen:
                    seen.add(id(d))
                    dl.append(d)
            else:
                if d.eng == "pe" and eng == "pe" and not o.dma:
                    continue
                b = best.get(d.eng)
                if b is None or d.pos > b.pos:
                    best[d.eng] = d
        for d in best.values():
            d.needs_sig = True
            dl.append(d)
        o.deps = dl
        for k in reads:
            r = self.readers.get(k)
            if r is None:
                r = self.readers[k] = ({}, [])
            if o.dma:
                r[1].append(o)
            else:
                r[0][eng] = o
        for k in writes:
            self.lastw[k] = o
            self.readers[k] = ({}, [])
        if o.dma:
            if dma_slot not in self.dsem:
                self.dsem[dma_slot] = self.stack.enter_context(self.nc.semaphore("D_" + dma_slot))
                self.dcount[dma_slot] = 0
            self.dcount[dma_slot] += 16 * ndma
            o.sem = self.dsem[dma_slot]
            o.sigval = self.dcount[dma_slot]
        self.ops[eng].append(o)
        return o

    def emit(self):
        for e in self.ENGS:
            c = 0
            for o in self.ops[e]:
                if not o.dma and o.needs_sig:
                    c += 1
                    o.sigval = c
                    o.sem = self.esem[e]
        with self.nc.Block() as block:
            decos = {"pe": block.tensor, "act": block.scalar, "dve": block.vector,
                     "pool": block.gpsimd, "sp": block.sync}
            for e in self.ENGS:
                def body(eng, e=e):
                    waited = {}
                    ops = self.ops[e]
                    n_ops = len(ops)
                    idx = 0
                    while idx < n_ops:
                        o = ops[idx]
                        j = idx + 1
                        if o.grp is not None:
                            while j < n_ops and ops[j].grp == o.grp:
                                j += 1
                        need = {}
                        for oo in ops[idx:j]:
                            for d in oo.deps:
                                if d.eng == e and (not d.dma) and d.pos >= idx:
                                    continue
                                k = id(d.sem)
                                if d.sigval > need.get(k, (None, 0))[1]:
                                    need[k] = (d.sem, d.sigval)
                        for k, (sem, val) in need.items():
                            if waited.get(k, 0) >= val:
                                continue
                            eng.wait_ge(sem, val)
                            waited[k] = val
                        for oo in ops[idx:j]:
                            ins = oo.fn(eng)
                            if oo.dma:
                                for i_ in (ins if isinstance(ins, (list, tuple)) else [ins]):
                                    i_.then_inc(oo.sem, 16)
                            elif oo.needs_sig:
                                ins.then_inc(oo.sem, 1)
                        idx = j
                    if e == "sp":
                        for slot, sem in self.dsem.items():
                            eng.wait_ge(sem, self.dcount[slot])
                decos[e](body)


def build_nc(SEQ=4096, do0=True, do1=True, final=True, overlap=True):
    NT = SEQ // TT
    NBLK = SEQ // 128
    nc = bass.Bass("TRN2", target_bir_lowering=False)

    def din(name, shape):
        return nc.dram_tensor(name, list(shape), F32, kind="ExternalInput").ap()

    x_d = din("x", [SEQ, D])
    cT_d = din("cT", [128, 8])
    ngT_d = din("normgT", [128, 2, 8])
    adaw_d = din("ada_w", [2, D, 3 * D])
    adab_d = din("ada_b", [1, 2 * 3 * D])
    win0_d = din("w_in0", [D, 6144])
    poolw_d = din("pool_w", [4, 256, 256])
    pscT_d = din("pscaleT", [128, 8])
    wout0_d = din("w_out0", [2048, D])
    win1_d = din("w_in1", [D, 8192])
    cwT_d = din("convwT", [128, 16, 3])
    cbT_d = din("convbT", [128, 16])
    wout1_d = din("w_out1", [2048, D])
    fg_d = din("final_g", [1, D])
    out_d = nc.dram_tensor("out", [SEQ, D], F32, kind="ExternalOutput").ap()
    kd = nc.dram_tensor("kd", [8, 128, SEQ], BF16).ap()
    vd = nc.dram_tensor("vd", [8, 128, NBLK, 128], BF16).ap()
    NPIECE = 64
    wc = nc.dram_tensor("wcache", [NPIECE, 128, 8, 512], BF16).ap()

    with contextlib.ExitStack() as stack:
        P = Prog(nc, stack)

        def sb(name, shape, dtype):
            return stack.enter_context(nc.sbuf_tensor("s_" + name, list(shape), dtype))

        psall = stack.enter_context(nc.psum_tensor("psall", [128, 8, 512], F32))
        ps = [psall[:, b, :] for b in range(8)]
        psk = [("ps", b) for b in range(8)]
        bank_rr = [0]

        def nbank():
            b = bank_rr[0]
            bank_rr[0] = (b + 1) % 8
            return b

        onesf = sb("onesf", [128, 128], F32)
        identf = sb("identf", [128, 128], F32)
        tmpf = sb("tmpf", [128, 128], F32)
        negf = sb("negf", [128, 128], F32)
        tri = sb("tri", [128, 128], BF16)
        onesb = sb("onesb", [128, 128], BF16)
        identb = sb("identb", [128, 128], BF16)
        maskb = sb("maskb", [128, 128], BF16)
        zerob = sb("zerob", [128, 128], BF16)
        invc = sb("invc", [128, 4, 16], F32)
        cT = sb("cT", [128, 8], F32)
        scb = sb("scb", [128, 8], BF16)
        ngT = sb("ngT", [128, 2, 8], F32)
        GT = sb("GT", [128, 2, 8], F32)
        shT = sb("shT", [128, 2, 8], F32)
        G1bc = sb("G1bc", [128, 2, D], F32)
        fgbc = sb("fgbc", [128, D], F32)
        pscT = sb("pscT", [128, 8], F32)
        cwT = sb("cwT", [128, 16, 3], F32)
        cbT = sb("cbT", [128, 16], F32)
        uh = sb("uh", [128, 8, 16], F32)
        uuh = sb("uuh", [128, 16, 2], F32)
        ss = sb("ss", [128, 8], F32)
        rstd = sb("rstd", [128, 8], F32)
        epsb = sb("epsb", [128, 1], F32)
        xt = [sb("xt%d" % j, [128, 4, D], F32) for j in range(2)]
        xn = sb("xn", [128, 4, D], F32)
        MSBK = [("xn", 0), ("xn", 1), ("xn", 2)]
        ADBK = [("xt", 1)]

        def msb_ap(c0, c1):
            b = c0 // D
            assert (c1 - 1) // D == b
            return xn[0:1, b, c0 - b * D:c1 - b * D]

        def adab_ap(c0, c1):
            b = c0 // D
            assert (c1 - 1) // D == b
            return xt[1][0:1, b, c0 - b * D:c1 - b * D]
        fgrow = xn[0:1, 3, :]
        hT = sb("hT", [128, 8, TT], BF16)
        qpad = sb("qpad", [128, 16, TT], BF16)
        pT = sb("pT", [128, 8, TT], BF16)
        yT = sb("yT", [128, 16, TT], BF16)
        kst = pT
        vst = sb("vst", [128, 4, D], BF16)
        kbuf = [sb("kbuf%d" % j, [128, SEQ], BF16) for j in range(2)]
        vbuf = [sb("vbuf%d" % j, [128, NBLK, 128], BF16) for j in range(2)]
        F2 = [sb("F2_%d" % j, [128, 2, 528], F32) for j in range(5)]
        Fs = [F2[j // 2][:, j % 2, :] for j in range(10)]
        SP2 = [sb("SP2_%d" % j, [128, 2, 512], BF16) for j in range(2)]
        A2 = [sb("A2_%d" % j, [128, 2, 512], BF16) for j in range(2)]
        otri = sb("otri", [128, 128], BF16)
        wsl = [sb("w%d" % j, [128, 8, 512], BF16) for j in range(NWSLOT)]
        wcnt = [0]

        def MM(out, lhsT, rhs, start, stop, reads, writes, sgc=False):
            if sgc:
                P.op("pe", lambda e: e.matmul(out, lhsT, rhs, start=start, stop=stop, skip_group_check=True),
                     reads, writes)
            else:
                P.op("pe", lambda e: e.matmul(out, lhsT, rhs, start=start, stop=stop), reads, writes)

        def TR(out, in_, reads, writes):
            P.op("pe", lambda e: e.transpose(out, in_, identf[:]), list(reads) + ["identf"], writes)

        def ACT(out, in_, func, reads, writes, bias=None, scale=None, accum=None):
            kw = {}
            if bias is not None:
                kw["bias"] = bias
            if scale is not None:
                kw["scale"] = scale
            if accum is not None:
                kw["accum_out"] = accum
            P.op("act", lambda e: e.activation(out, in_, func, **kw), reads, writes)

        def TTo(eng, out, in0, in1, op, reads, writes):
            P.op(eng, lambda e: e.tensor_tensor(out, in0, in1, op), reads, writes)

        def TS(eng, out, in0, s1, s2, op0, op1, reads, writes):
            if s2 is None:
                P.op(eng, lambda e: e.tensor_scalar(out, in0, s1, None, op0), reads, writes)
            else:
                P.op(eng, lambda e: e.tensor_scalar(out, in0, s1, s2, op0, op1), reads, writes)

        def STT(eng, out, in0, sc, in1, op0, op1, reads, writes):
            P.op(eng, lambda e: e.scalar_tensor_tensor(out, in0, sc, in1, op0, op1), reads, writes)

        def CP(eng, out, in_, reads, writes):
            if eng == "act":
                P.op("act", lambda e: e.activation(out, in_, AF.Copy), reads, writes)
            else:
                P.op(eng, lambda e: e.tensor_copy(out, in_), reads, writes)

        def MS(eng, ap, val, writes):
            P.op(eng, lambda e: e.memset(ap, val), (), writes)

        tctx = {"i": -1, "k": 0}

        def wload(dst_fn, src, piece=None, tile=None):
            s = wcnt[0] % NWSLOT
            wcnt[0] += 1
            dst = dst_fn(wsl[s])
            if piece is None:
                tile = tctx["i"]
                if tile >= 0:
                    piece = tctx["k"]
                    tctx["k"] += 1
            if piece is None or tile < 0 or not USE_WCACHE:
                P.op("pool", lambda e: e.dma_start(out=dst, in_=src), (), [("w", s)], dma_slot="w%d" % s)
                return s
            assert piece < NPIECE
            cv = dst_fn(wc[piece])
            if tile == 0:
                P.op("pool", lambda e: e.dma_start(out=dst, in_=src), (), [("w", s)], dma_slot="w%d" % s)
                P.op("sp", lambda e: e.dma_start(out=cv, in_=dst), [("w", s)], [("wc", piece)], dma_slot="wst%d" % s)
            else:
                P.op("pool", lambda e: e.dma_start(out=dst, in_=cv), [("wc", piece)], [("w", s)], dma_slot="w%d" % s)
            return s

        def wload_std(wd, col0, piece=None, tile=None):
            src = wd[:, col0:col0 + 512].rearrange("(kc p) n -> p kc n", p=128)
            return wload(lambda w: w[:, :, :], src, piece, tile)

        MS("pool", onesf[:], 1.0, ["onesf"])
        MS("pool", epsb[:], EPS, ["epsb"])
        MS("pool", negf[:], NEG, ["negf"])
        MS("pool", zerob[:], 0.0, ["zerob"])
        MS("pool", onesb[:], 1.0, ["onesb"])
        MS("pool", qpad[:], 0.0, ["qpad"])
        MS("pool", uh[:], 0.0, ["uh"])
        MS("pool", uuh[:], 0.0, ["uuh"])
        P.op("pool", lambda e: e.affine_select(out=identf[:], in_=onesf[:], pattern=[[-1, 128]],
                                               compare_op=ALU.is_equal, fill=0.0, base=0,
                                               channel_multiplier=1), ["onesf"], ["identf"])
        P.op("pool", lambda e: e.affine_select(out=tmpf[:], in_=onesf[:], pattern=[[-1, 128]],
                                               compare_op=ALU.is_ge, fill=0.0, base=0,
                                               channel_multiplier=1), ["onesf"], ["tmpf"])
        CP("dve", tri[:], tmpf[:], ["tmpf"], ["tri"])
        TS("dve", otri[:], tmpf[:], -1.0, 1.0, ALU.mult, ALU.add, ["tmpf"], ["otri"])
        CP("dve", identb[:], identf[:], ["identf"], ["identb"])
        P.op("pool", lambda e: e.affine_select(out=tmpf[:], in_=negf[:], pattern=[[-1, 128]],
                                               compare_op=ALU.is_ge, fill=0.0, base=0,
                                               channel_multiplier=1), ["negf"], ["tmpf"])
        CP("dve", maskb[:], tmpf[:], ["tmpf"], ["maskb"])
        for g, w in enumerate(WINS):
            for t in range(w - 1):
                MS("dve", invc[:, g, t:t + 1], 1.0 / (t + 1), ["invc"])
            MS("dve", invc[:, g, w - 1:16], 1.0 / w, ["invc"])

        def const_loads(e):
            return [
                e.dma_start(out=cT[:], in_=cT_d),
                e.dma_start(out=ngT[:], in_=ngT_d),
                e.dma_start(out=pscT[:], in_=pscT_d),
                e.dma_start(out=cwT[:], in_=cwT_d),
                e.dma_start(out=cbT[:], in_=cbT_d),
                e.dma_start(out=fgrow, in_=fg_d),
            ]
        P.op("sp", const_loads, (), ["cT", "ngT", "pscT", "cwT", "cbT", ("xn", 3)],
             dma_slot="const", ndma=6)

        def xload(i):
            j = i % 2
            src = x_d[i * TT:(i + 1) * TT, :].rearrange("(b p) d -> p b d", p=128)
            P.op("sp", lambda e: e.dma_start(out=xt[j][:], in_=src), (), [("xt", j)], dma_slot="x%d" % j)

        xload(0)

        ACT(scb[:], cT[:], AF.Silu, ["cT"], ["scb"])
        for l in range(2):
            if (l == 0 and not do0) or (l == 1 and not do1):
                continue
            P.op("sp", lambda e, l=l: e.dma_start(
                out=xt[1][0:1, 0:3, :], in_=adab_d[0:1, l * 3072:(l + 1) * 3072].rearrange("o (b n) -> o b n", n=D)),
                (), ADBK, dma_slot="ab")
            for kc in range(8):
                src = adaw_d[l, kc * 128:(kc + 1) * 128, :].rearrange("p (g n) -> p g n", n=512)
                s = wload(lambda w: w[:, 0:6, :], src)
                for g in range(6):
                    MM(ps[g][0:1, :], scb[:, kc:kc + 1], wsl[s][:, g, :], kc == 0, kc == 7,
                       ["scb", ("w", s)], [psk[g]])
            for g in range(6):
                TTo("dve", msb_ap(g * 512, (g + 1) * 512), ps[g][0:1, :], adab_ap(g * 512, (g + 1) * 512), ALU.add,
                    [psk[g]] + ADBK, MSBK)
            for b_ in (1, 2):
                TS("dve", xn[0:1, b_, :], xn[0:1, b_, :], 1.0, None, ALU.add, None, MSBK, MSBK)
            for h in range(2):
                MM(ps[6][:, :], onesf[0:1, 0:128], msb_ap(2048 + h * 512, 2048 + (h + 1) * 512),
                   True, True, ["onesf"] + MSBK, [psk[6]])
                CP("act", G1bc[:, l, h * 512:(h + 1) * 512], ps[6][:, :], [psk[6]], [("G1bc", l)])
            for j in range(16):
                MM(ps[7][:, j:j + 1], msb_ap(j * 128, (j + 1) * 128), onesf[0:1, 0:1], True, True,
                   MSBK + ["onesf"], [psk[7]])
            CP("dve", shT[:, l, :], ps[7][:, 0:8], [psk[7]], [("shT", l)])
            TTo("dve", GT[:, l, :], ps[7][:, 8:16], ngT[:, l, :], ALU.mult, [psk[7], "ngT"], [("GT", l)])
        if final:
            for h in range(2):
                MM(ps[6][:, :], onesf[0:1, 0:128], fgrow[:, h * 512:(h + 1) * 512], True, True,
                   ["onesf", ("xn", 3)], [psk[6]])
                CP("act", fgbc[:, h * 512:(h + 1) * 512], ps[6][:, :], [psk[6]], ["fgbc"])

        def stage_norm(i, l, X):
            xk = ("xt", i % 2)
            for blk in range(4):
                ACT(xn[:, blk, :], X[:, blk, :], AF.Square, [xk], [("xn", blk), ("ss", blk)],
                    accum=ss[:, blk:blk + 1])
                ACT(rstd[:, blk:blk + 1], ss[:, blk:blk + 1], AF.Ln, [("ss", blk), "epsb"], [("rstd", blk)],
                    bias=epsb[:, 0:1], scale=1.0 / D)
                ACT(rstd[:, blk:blk + 1], rstd[:, blk:blk + 1], AF.Exp, [("rstd", blk)], [("rstd", blk)],
                    scale=-0.5)
                TS("dve", xn[:, blk, :], X[:, blk, :], rstd[:, blk:blk + 1], None, ALU.mult, None,
                   [xk, ("rstd", blk)], [("xn", blk)])
            for kc in range(8):
                for blk in range(4):
                    TR(ps[kc][:, blk * 128:(blk + 1) * 128], xn[:, blk, kc * 128:(kc + 1) * 128],
                       [("xn", blk)], [psk[kc]])
            for kc in range(8):
                ACT(hT[:, kc, :], ps[kc][:, :], AF.Identity, [psk[kc], ("GT", l), ("shT", l)], [("hT", kc)],
                    bias=shT[:, l, kc:kc + 1], scale=GT[:, l, kc:kc + 1])

        def proj_fm(s, c, b):
            for kc in range(8):
                MM(ps[b][:, :], wsl[s][:, kc, c * 128:(c + 1) * 128], hT[:, kc, :], kc == 0, kc == 7,
                   [("w", s), ("hT", kc)], [psk[b]])

        def stage_outproj(i, l, X, wd):
            xk = ("xt", i % 2)
            for half in range(2):
                banks = [nbank() for _ in range(4)]
                for j in range(2):
                    src = wd[j * 1024:(j + 1) * 1024, half * 512:(half + 1) * 512].rearrange(
                        "(kc p) n -> p kc n", p=128)
                    s = wload(lambda w: w[:, :, :], src)
                    for blk in range(4):
                        for kc in range(8):
                            ch = j * 8 + kc
                            MM(ps[banks[blk]][:, :], yT[:, ch, blk * 128:(blk + 1) * 128], wsl[s][:, kc, :],
                               j == 0 and kc == 0, j == 1 and kc == 7,
                               [("yT", ch), ("w", s)], [psk[banks[blk]]])
                for blk in range(4):
                    f = 6 + (blk % 2)
                    TTo("dve", Fs[f][:, 0:512], ps[banks[blk]][:, :], G1bc[:, l, half * 512:(half + 1) * 512],
                        ALU.mult, [psk[banks[blk]], ("G1bc", l)], [("F", f)])
                    TTo("dve", X[:, blk, half * 512:(half + 1) * 512], X[:, blk, half * 512:(half + 1) * 512],
                        Fs[f][:, 0:512], ALU.add, [("F", f), xk], [xk])

        BGB = [6, 7]
        bgb_ctr = [0]
        bgmode = {"alone": False}

        def bg_bank():
            b = BGB[bgb_ctr[0] % 2]
            bgb_ctr[0] += 1
            return b

        def run_all(units):
            for u in units:
                for _ in u():
                    pass

        def y4ap(yb, c, a=0, b=512):
            return vst[:, 2 * yb + c // 2, (c % 2) * 512 + a:(c % 2) * 512 + b]

        def norm_units_l1(i):
            X = xt[i % 2]
            xk = ("xt", i % 2)
            units = []
            for blk in range(4):
                for half in range(2):
                    def u(blk=blk, half=half):
                        if half == 0:
                            ACT(xn[:, blk, :], X[:, blk, :], AF.Square, [xk], [("xn", blk), ("ss", blk)],
                                accum=ss[:, blk:blk + 1])
                            ACT(rstd[:, blk:blk + 1], ss[:, blk:blk + 1], AF.Ln, [("ss", blk), "epsb"],
                                [("rstd", blk)], bias=epsb[:, 0:1], scale=1.0 / D)
                            ACT(rstd[:, blk:blk + 1], rstd[:, blk:blk + 1], AF.Exp, [("rstd", blk)],
                                [("rstd", blk)], scale=-0.5)
                            TS("dve", xn[:, blk, :], X[:, blk, :], rstd[:, blk:blk + 1], None, ALU.mult, None,
                               [xk, ("rstd", blk)], [("xn", blk)])
                            yield
                        b = bg_bank()
                        for k4 in range(4):
                            kc = half * 4 + k4
                            TR(ps[b][:, k4 * 128:(k4 + 1) * 128], xn[:, blk, kc * 128:(kc + 1) * 128],
                               [("xn", blk)], [psk[b]])
                            if k4 == 1:
                                yield
                        yield
                        for k4 in range(4):
                            kc = half * 4 + k4
                            if bgmode["alone"]:
                                ACT(hT[:, kc, blk * 128:(blk + 1) * 128], ps[b][:, k4 * 128:(k4 + 1) * 128],
                                    AF.Identity, [psk[b], ("GT", 1), ("shT", 1)], [("hT", kc)],
                                    bias=shT[:, 1, kc:kc + 1], scale=GT[:, 1, kc:kc + 1])
                            else:
                                TS("dve", hT[:, kc, blk * 128:(blk + 1) * 128], ps[b][:, k4 * 128:(k4 + 1) * 128],
                                   GT[:, 1, kc:kc + 1], shT[:, 1, kc:kc + 1], ALU.mult, ALU.add,
                                   [psk[b], ("GT", 1), ("shT", 1)], [("hT", kc)])
                    units.append(u)
            return units

        def proj_steps(s, c, b):
            for kc in range(8):
                MM(ps[b][:, :], wsl[s][:, kc, c * 128:(c + 1) * 128], hT[:, kc, :], kc == 0, kc == 7,
                   [("w", s), ("hT", kc)], [psk[b]])
                if kc % 2 == 1 and kc < 7:
                    yield

        def layer1_units(i):
            X = xt[i % 2]
            xk = ("xt", i % 2)
            units = list(norm_units_l1(i))
            first_norm = units[0]

            def kick():
                need(0)
                yield from first_norm()
            units[0] = kick
            USB = [xn[:, 0, 0:512], xn[:, 0, 512:1024], xn[:, 1, 0:512], xn[:, 1, 512:1024]]
            USK = [("xn", 0), ("xn", 0), ("xn", 1), ("xn", 1)]
            UU, UK = xn[:, 2, 0:514], ("xn", 2)
            T1, TK = xn[:, 3, 0:512], ("xn", 3)
            T2 = xn[:, 3, 512:1024]
            PB = 40
            slots = {}

            def do_load(k):
                q, t = k // 5, k % 5
                if t < 4:
                    col = [(12 + q), (8 + q), (4 + q), q][t] * 512
                    return wload_std(win1_d, col, PB + k, i)
                src = wout1_d[q * 512:(q + 1) * 512, :].rearrange("(kc p) n -> p kc n", p=128)
                return wload(lambda w: w[:, :, :].rearrange("p a b -> p (a b)").rearrange(
                    "p (a b) -> p a b", b=1024), src, PB + k, i)

            def need(k):
                for kk in (k, k + 1):
                    if kk < 20 and kk not in slots:
                        slots[kk] = do_load(kk)
                return slots[k]

            for q in range(4):
                yb = q % 2
                hg, hu, hc, hb, ho = [None], [None], [None], [None], [None]
                for c in range(4):
                    def u(c=c, hg=hg, yb=yb, q=q):
                        if c == 0:
                            hg[0] = need(q * 5 + 0)
                        b = bg_bank()
                        yield from proj_steps(hg[0], c, b)
                        yield
                        if bgmode["alone"]:
                            ACT(y4ap(yb, c), ps[b][:, :], AF.Silu, [psk[b]], ["vst"])
                        else:
                            f = T1 if c % 2 == 0 else T2
                            ACT(f, ps[b][:, :], AF.Exp, [psk[b]], [TK], scale=-1.0)
                            TS("dve", f, f, 1.0, None, ALU.add, None, [TK], [TK])
                            P.op("dve", lambda e, o=f: e.reciprocal(o, o), [TK], [TK])
                            TTo("dve", y4ap(yb, c), ps[b][:, :], f, ALU.mult, [psk[b], TK], ["vst"])
                    units.append(u)
                for c in range(4):
                    def u(c=c, hu=hu, q=q):
                        if c == 0:
                            hu[0] = need(q * 5 + 1)
                        b = bg_bank()
                        yield from proj_steps(hu[0], c, b)
                        yield
                        CP("act" if bgmode["alone"] else "dve", USB[c], ps[b][:, :], [psk[b]], [USK[c]])
                    units.append(u)
                for c in range(4):
                    def u(c=c, hc=hc, q=q):
                        if c == 0:
                            hc[0] = need(q * 5 + 2)
                        ch = q * 4 + c
                        b = bg_bank()
                        yield from proj_steps(hc[0], c, b)
                        yield
                        CP("dve", UU[:, 0:2], uuh[:, ch, :], [("uuh", ch)], [UK])
                        TTo("dve", UU[:, 2:514], ps[b][:, :], USB[c], ALU.mult, [psk[b], USK[c]], [UK])
                        CP("dve", uuh[:, ch, :], UU[:, 512:514], [UK], [("uuh", ch)])
                        TS("dve", T1, UU[:, 2:514], cwT[:, ch, 2:3], cbT[:, ch:ch + 1],
                           ALU.mult, ALU.add, [UK, "cwT", "cbT"], [TK])
                        STT("dve", T1, UU[:, 1:513], cwT[:, ch, 1:2], T1, ALU.mult, ALU.add, [UK, "cwT", TK], [TK])
                        STT("dve", USB[c], UU[:, 0:512], cwT[:, ch, 0:1], T1, ALU.mult, ALU.add,
                            [UK, "cwT", TK], [USK[c]])
                    units.append(u)
                for c in range(4):
                    def u(c=c, hb=hb, yb=yb, q=q):
                        if c == 0:
                            hb[0] = need(q * 5 + 3)
                        b = bg_bank()
                        yield from proj_steps(hb[0], c, b)
                        yield
                        TTo("dve", T1, ps[b][:, :], USB[c], ALU.mult, [psk[b], USK[c]], [TK])
                        TTo("dve", y4ap(yb, c), T1, y4ap(yb, c), ALU.mult, [TK, "vst"], ["vst"])
                    units.append(u)
                for half in range(2):
                    for blk in range(4):
                        def u(half=half, blk=blk, ho=ho, yb=yb, q=q):
                            if half == 0 and blk == 0:
                                ho[0] = need(q * 5 + 4)
                            b = bg_bank()
                            wv = wsl[ho[0]][:, :, :].rearrange("p a b -> p (a b)").rearrange("p (a b) -> p a b", b=1024)
                            for c in range(4):
                                MM(ps[b][:, :], y4ap(yb, c, blk * 128, (blk + 1) * 128),
                                   wv[:, c, half * 512:(half + 1) * 512], c == 0, c == 3,
                                   ["vst", ("w", ho[0])], [psk[b]])
                                if c == 1:
                                    yield
                            yield
                            f = T1 if blk % 2 == 0 else T2
                            TTo("dve", f, ps[b][:, :], G1bc[:, 1, half * 512:(half + 1) * 512],
                                ALU.mult, [psk[b], ("G1bc", 1)], [TK])
                            TTo("dve", X[:, blk, half * 512:(half + 1) * 512], X[:, blk, half * 512:(half + 1) * 512],
                                f, ALU.add, [TK, xk], [xk])
                        units.append(u)
            return units

        def finish_units(i):
            def mk(blk):
                def u():
                    finish_blk(i, blk)
                    return
                    yield
                return u
            return [mk(blk) for blk in range(4)]

        def layer0(i, bg=()):
            X = xt[i % 2]
            tok0 = i * TT
            stage_norm(i, 0, X)
            for g in range(2):
                s = wload_std(win0_d, (4 + g) * 512)
                for c in range(4):
                    hp = g * 4 + c
                    b = nbank()
                    proj_fm(s, c, b)
                    CP("act" if c % 2 == 0 else "dve", kst[:, hp, :], ps[b][:, :], [psk[b]], [("pT", hp)])
            P.op("sp", lambda e: e.dma_start(out=kd[:, :, tok0:tok0 + TT].rearrange("h p t -> p h t"),
                                             in_=kst[:]), [("pT", c_) for c_ in range(8)], [("kd", i)], dma_slot="ks")
            for vh in range(2):
                s = wload_std(win0_d, (6 + vh) * 512)
                for blk in range(4):
                    b = nbank()
                    for kc in range(8):
                        MM(ps[b][:, :], hT[:, kc, blk * 128:(blk + 1) * 128], wsl[s][:, kc, :], kc == 0, kc == 7,
                           [("hT", kc), ("w", s)], [psk[b]])
                    CP("act" if blk % 2 == 0 else "dve", vst[:, blk, vh * 512:(vh + 1) * 512], ps[b][:, :],
                       [psk[b]], ["vst"])

            def vstore(e):
                return [e.dma_start(out=vd[hp, :, 4 * i:4 * i + 4, :], in_=vst[:, :, hp * 128:(hp + 1) * 128])
                        for hp in range(8)]
            P.op("sp", vstore, ["vst"], [("vd", i)], dma_slot="vs", ndma=8)
            for g in range(4):
                s = wload_std(win0_d, (8 + g) * 512)
                for c in range(4):
                    ch = g * 4 + c
                    b = nbank()
                    proj_fm(s, c, b)
                    ACT(yT[:, ch, :], ps[b][:, :], AF.Silu, [psk[b]], [("yT", ch)])
            for g in range(2):
                s = wload_std(win0_d, (2 + g) * 512)
                for c in range(4):
                    hp = g * 4 + c
                    b = nbank()
                    proj_fm(s, c, b)
                    TS("dve", qpad[0:64, 2 * hp, :], ps[b][0:64, :], 0.125, None, ALU.mult, None,
                       [psk[b]], [("qp", 2 * hp)])
                    P.op("act", lambda e, o=qpad[64:128, 2 * hp + 1, :], n=ps[b][64:128, :]:
                         e.mul(o, n, 0.125), [psk[b]], [("qp", 2 * hp + 1)])
            for g2 in range(2):
                s = wload_std(win0_d, g2 * 512)
                for c4 in range(4):
                    c = g2 * 4 + c4
                    grp = c // 2
                    w = WINS[grp]
                    b = nbank()
                    proj_fm(s, c4, b)
                    U = Fs[c % 2]
                    uk = ("F", c % 2)
                    CP("dve", U[:, 0:16], uh[:, c, :], [("uh", c)], [uk])
                    CP("act", U[:, 16:528], ps[b][:, :], [psk[b]], [uk])
                    CP("dve", uh[:, c, :], U[:, 512:528], [uk], [("uh", c)])
                    ta, tb = (2, 3) if c % 2 == 0 else (4, 5)
                    TTo("dve", Fs[ta][:, 1:528], U[:, 1:528], U[:, 0:527], ALU.add, [uk], [("F", ta)])
                    cur = ta
                    sh = 2
                    while sh < w:
                        oth = tb if cur == ta else ta
                        TTo("dve", Fs[oth][:, 2 * sh - 1:528], Fs[cur][:, 2 * sh - 1:528],
                            Fs[cur][:, sh - 1:528 - sh], ALU.add, [("F", cur)], [("F", oth)])
                        cur = oth
                        sh *= 2
                    STT("dve", pT[:, c, :], Fs[cur][:, 16:528], 1.0 / w, U[:, 16:528], ALU.mult, ALU.subtract,
                        [("F", cur), uk], [("pT", c)])
                    if i == 0:
                        oth = tb if cur == ta else ta
                        TTo("dve", Fs[oth][:, 0:16], Fs[cur][:, 16:32], invc[:, grp, :], ALU.mult,
                            [("F", cur), "invc"], [("F", oth)])
                        TTo("dve", pT[:, c, 0:16], Fs[oth][:, 0:16], U[:, 16:32], ALU.subtract,
                            [("F", oth), uk], [("pT", c)])
            src = poolw_d.rearrange("g (cc p) d -> p (g cc) d", p=128)
            s = wload(lambda w: w[:, :, 0:256], src)
            for dch in range(8):
                grp, dd = dch // 2, dch % 2
                b = nbank()
                for cc in range(2):
                    MM(ps[b][:, :], wsl[s][:, grp * 2 + cc, dd * 128:(dd + 1) * 128], pT[:, grp * 2 + cc, :],
                       cc == 0, cc == 1, [("w", s), ("pT", grp * 2 + cc)], [psk[b]])
                STT("dve", yT[:, dch, :], ps[b][:, :], pscT[:, dch:dch + 1], yT[:, dch, :], ALU.mult, ALU.mult,
                    [psk[b], "pscT", ("yT", dch)], [("yT", dch)])
            ntok = (i + 1) * TT
            nblk = 4 * (i + 1)

            def kvload(hp):
                j = hp % 2
                rk = [("kd", t) for t in range(i + 1)]
                rv = [("vd", t) for t in range(i + 1)]
                P.op("sp", lambda e: e.dma_start(out=kbuf[j][:, 0:ntok], in_=kd[hp, :, 0:ntok]), rk,
                     [("kb", j)], dma_slot="kb%d" % j)
                P.op("sp", lambda e: e.dma_start(out=vbuf[j][:, 0:nblk, :], in_=vd[hp, :, 0:nblk, :]), rv,
                     [("vb", j)], dma_slot="vb%d" % j)

            kvload(0)
            if overlap:
                ZP = [0, 0]
                CBK = [2, 3]
                accb = [4, 5]
            else:
                ZP = [0, 2]
                CBK = [4, 5]
                accb = [6, 7]
            bg = list(bg)
            bgs = {"cur": None, "idx": 0, "steps": 0, "slot": 0}
            BG_STEPS_EST = 620

            def bg_step():
                while True:
                    if bgs["cur"] is None:
                        if bgs["idx"] >= len(bg):
                            return False
                        bgs["cur"] = bg[bgs["idx"]]()
                        bgs["idx"] += 1
                    try:
                        next(bgs["cur"])
                    except StopIteration:
                        bgs["cur"] = None
                    bgs["steps"] += 1
                    return True

            def run_bg(sub):
                if not bg:
                    return
                tgt = (BG_STEPS_EST * (3 * bgs["slot"] + sub + 1) + 3 * (8 * nblk + 2) - 1) // (3 * (8 * nblk + 2))
                if bgs["steps"] < tgt:
                    bg_step()

            def flush_bg():
                bgmode["alone"] = True
                while bg_step():
                    pass
                bgmode["alone"] = False
            items = [(hp, kb) for hp in range(8) for kb in range(nblk - 1, -1, -1)]
            n = len(items)
            st = {}

            def fk(w):
                return [("F", 2 * w), ("F", 2 * w + 1)]

            def A_pe(p):
                hp, kb = items[p]
                j = hp % 2
                r = kb - 4 * i
                col0 = r * 128 if r >= 0 else 0
                st[p] = (hp, kb, r, col0)
                z0 = ZP[p % 2]
                for hd in range(2):
                    zb = z0 + hd
                    MM(ps[zb][:, col0:512], kbuf[j][:, kb * 128:(kb + 1) * 128], qpad[:, 2 * hp + hd, col0:512],
                       True, r < 0, [("kb", j), ("qp", 2 * hp + hd)], [psk[zb]])
                    if r >= 0:
                        MM(ps[zb][:, col0:col0 + 128], identb[:, :], maskb[:, :], False, True,
                           ["identb", "maskb"], [psk[zb]])

            def EXP1(p):
                hp, kb, r, col0 = st[p]
                z0 = ZP[p % 2]
                e_i = p % 3
                ACT(F2[e_i][:, :, col0:512], psall[:, z0:z0 + 2, col0:512], AF.Exp,
                    [psk[z0], psk[z0 + 1]], fk(e_i))

            def LN(p):
                hp, kb, r, col0 = st[p]
                e_i = p % 3
                ACT(SP2[p % 2][:, :, col0:512], F2[e_i][:, :, col0:512], AF.Ln, fk(e_i), [("SP2", p % 2)],
                    bias=1.0)

            def T_pe(p):
                hp, kb, r, col0 = st[p]
                if kb == nblk - 1:
                    for hd in range(2):
                        MM(ps[CBK[hd]][:, :], zerob[:, :], hT[:, 0, :], True, False,
                           ["zerob", ("hT", 0)], [psk[CBK[hd]]], sgc=True)
                for hd in range(2):
                    MM(ps[CBK[hd]][:, col0:512], tri[:, :], SP2[p % 2][:, hd, col0:512], False, False,
                       ["tri", ("SP2", p % 2)], [psk[CBK[hd]]], sgc=True)

            def O_pe(p):
                hp, kb, r, col0 = st[p]
                if kb > 0:
                    for hd in range(2):
                        MM(ps[CBK[hd]][:, col0:512], otri[:, :], SP2[p % 2][:, hd, col0:512], False, False,
                           ["otri", ("SP2", p % 2)], [psk[CBK[hd]]], sgc=True)

            def EXP2_F(p):
                hp, kb, r, col0 = st[p]
                e_i = p % 3
                w_i = 3 + p % 2
                ACT(F2[w_i][:, :, col0:512], psall[:, CBK[0]:CBK[0] + 2, col0:512], AF.Exp,
                    [psk[CBK[0]], psk[CBK[1]]], fk(w_i), scale=-1.0)
                TTo("dve", A2[p % 2][:, :, col0:512], F2[e_i][:, :, col0:512], F2[w_i][:, :, col0:512], ALU.mult,
                    fk(e_i) + fk(w_i), [("A2", p % 2)])

            def G_pe(p):
                hp, kb, r, col0 = st[p]
                j = hp % 2
                if kb == nblk - 1:
                    if hp + 1 < 8:
                        kvload(hp + 1)
                    for hd in range(2):
                        MM(ps[accb[hd]][:, :], zerob[:, :], hT[:, 0, :], True, False,
                           ["zerob", ("hT", 0)], [psk[accb[hd]]])
                for hd in range(2):
                    MM(ps[accb[hd]][:, col0:512], vbuf[j][:, kb, :], A2[p % 2][:, hd, col0:512], False, kb == 0,
                       [("vb", j), ("A2", p % 2)], [psk[accb[hd]]])

            def Y_evac(p):
                hp, kb, r, col0 = st[p]
                if kb == 0:
                    for hd in range(2):
                        pr = slice(hd * 64, (hd + 1) * 64)
                        TTo("dve", yT[pr, 8 + hp, :], ps[accb[hd]][pr, :], yT[pr, 8 + hp, :], ALU.mult,
                            [psk[accb[hd]], ("yT", 8 + hp)], [("yT", 8 + hp)])

            A_pe(0)
            for p in range(n + 2):
                bgs["slot"] = p
                if overlap:
                    P.group_begin("pe")
                    if 0 <= p - 1 < n:
                        T_pe(p - 1)
                    P.group_end("pe")
                    run_bg(0)
                    if p < n:
                        EXP1(p)
                    P.group_begin("pe")
                    if p + 1 < n:
                        A_pe(p + 1)
                    P.group_end("pe")
                    run_bg(1)
                else:
                    P.group_begin("pe")
                    if 0 <= p - 1 < n:
                        T_pe(p - 1)
                    if p + 1 < n:
                        A_pe(p + 1)
                    P.group_end("pe")
                    if p < n:
                        EXP1(p)
                if 0 <= p - 1 < n:
                    EXP2_F(p - 1)
                if p < n:
                    LN(p)
                P.group_begin("pe")
                if 0 <= p - 2 < n:
                    G_pe(p - 2)
                if 0 <= p - 1 < n:
                    O_pe(p - 1)
                P.group_end("pe")
                if 0 <= p - 2 < n:
                    Y_evac(p - 2)
                if overlap:
                    run_bg(2)
            flush_bg()
            stage_outproj(i, 0, X, wout0_d)

        def layer1(i):
            X = xt[i % 2]
            stage_norm(i, 1, X)
            for q in range(4):
                s = wload_std(win1_d, (12 + q) * 512)
                for c in range(4):
                    ch = q * 4 + c
                    b = nbank()
                    proj_fm(s, c, b)
                    ACT(yT[:, ch, :], ps[b][:, :], AF.Silu, [psk[b]], [("yT", ch)])
                s = wload_std(win1_d, (8 + q) * 512)
                for c in range(4):
                    b = nbank()
                    proj_fm(s, c, b)
                    CP("act", Fs[c][:, 0:512], ps[b][:, :], [psk[b]], [("F", c)])
                s = wload_std(win1_d, (4 + q) * 512)
                for c in range(4):
                    ch = q * 4 + c
                    b = nbank()
                    proj_fm(s, c, b)
                    UU = Fs[4 + c % 2]
                    uk = ("F", 4 + c % 2)
                    CP("dve", UU[:, 0:2], uuh[:, ch, :], [("uuh", ch)], [uk])
                    TTo("dve", UU[:, 2:514], ps[b][:, :], Fs[c][:, 0:512], ALU.mult, [psk[b], ("F", c)], [uk])
                    CP("dve", uuh[:, ch, :], UU[:, 512:514], [uk], [("uuh", ch)])
                    t1 = 6 + c % 2
                    TS("dve", Fs[t1][:, 0:512], UU[:, 2:514], cwT[:, ch, 2:3], cbT[:, ch:ch + 1], ALU.mult, ALU.add,
                       [uk, "cwT", "cbT"], [("F", t1)])
                    STT("dve", Fs[t1][:, 0:512], UU[:, 1:513], cwT[:, ch, 1:2], Fs[t1][:, 0:512], ALU.mult, ALU.add,
                        [uk, "cwT", ("F", t1)], [("F", t1)])
                    STT("dve", Fs[c][:, 0:512], UU[:, 0:512], cwT[:, ch, 0:1], Fs[t1][:, 0:512], ALU.mult, ALU.add,
                        [uk, "cwT", ("F", t1)], [("F", c)])
                s = wload_std(win1_d, q * 512)
                for c in range(4):
                    ch = q * 4 + c
                    b = nbank()
                    proj_fm(s, c, b)
                    t1 = 6 + c % 2
                    TTo("dve", Fs[t1][:, 0:512], ps[b][:, :], Fs[c][:, 0:512], ALU.mult, [psk[b], ("F", c)],
                        [("F", t1)])
                    TTo("dve", yT[:, ch, :], Fs[t1][:, 0:512], yT[:, ch, :], ALU.mult, [("F", t1), ("yT", ch)],
                        [("yT", ch)])
            stage_outproj(i, 1, X, wout1_d)

        def finish_blk(i, blk):
            X = xt[i % 2]
            xk = ("xt", i % 2)
            if final:
                ACT(xn[:, blk, :], X[:, blk, :], AF.Square, [xk], [("xn", blk), ("ss", blk)],
                    accum=ss[:, blk:blk + 1])
                ACT(rstd[:, blk:blk + 1], ss[:, blk:blk + 1], AF.Ln, [("ss", blk), "epsb"], [("rstd", blk)],
                    bias=epsb[:, 0:1], scale=1.0 / D)
                ACT(rstd[:, blk:blk + 1], rstd[:, blk:blk + 1], AF.Exp, [("rstd", blk)], [("rstd", blk)],
                    scale=-0.5)
                STT("dve", xn[:, blk, :], X[:, blk, :], rstd[:, blk:blk + 1], fgbc[:, :], ALU.mult, ALU.mult,
                    [xk, ("rstd", blk), "fgbc"], [("xn", blk)])
            else:
                CP("dve", xn[:, blk, :], X[:, blk, :], [xk], [("xn", blk)])
            r0 = i * TT + blk * 128
            P.op("sp", lambda e, r0=r0, blk=blk: e.dma_start(out=out_d[r0:r0 + 128, :], in_=xn[:, blk, :]),
                 [("xn", blk)], [("out", i, blk)], dma_slot="o%d" % blk)

        def finish(i):
            for blk in range(4):
                finish_blk(i, blk)

        pending = []
        for i in range(NT):
            tctx["i"] = i
            tctx["k"] = 0
            if not (overlap and do0 and do1):
                if i + 1 < NT:
                    xload(i + 1)
                if do0:
                    layer0(i)
                if do1:
                    layer1(i)
                finish(i)
                continue
            layer0(i, pending)
            if i + 1 < NT:
                xload(i + 1)
            pending = layer1_units(i) + finish_units(i)
        if pending:
            bgmode["alone"] = True
            run_all(pending)

        P.emit()
    return nc


_NC_CACHE = {}


def _layout_inputs(x, c, norm_g, ada_w, ada_b, even_w_in, pool_w, pool_scale, even_w_out,
                   odd_w_in, conv_w, conv_b, odd_w_out, final_g):
    f = lambda a: np.ascontiguousarray(np.asarray(a, dtype=np.float32))
    shared = {
        "normgT": f(np.asarray(norm_g).reshape(2, 8, 128).transpose(2, 0, 1)),
        "ada_w": f(ada_w),
        "ada_b": f(np.asarray(ada_b).reshape(1, -1)),
        "w_in0": f(np.asarray(even_w_in)[0]),
        "pool_w": f(np.asarray(pool_w)[0]),
        "pscaleT": f(np.asarray(pool_scale)[0].reshape(8, 128).T),
        "w_out0": f(np.asarray(even_w_out)[0]),
        "w_in1": f(np.asarray(odd_w_in)[0]),
        "convwT": f(np.asarray(conv_w)[0].reshape(3, 16, 128).transpose(2, 1, 0)),
        "convbT": f(np.asarray(conv_b)[0].reshape(16, 128).T),
        "w_out1": f(np.asarray(odd_w_out)[0]),
        "final_g": f(np.asarray(final_g).reshape(1, -1)),
    }
    x = np.asarray(x)
    c = np.asarray(c)
    maps = []
    for b in range(x.shape[0]):
        m = dict(shared)
        m["x"] = f(x[b])
        m["cT"] = f(c[b].reshape(8, 128).T)
        maps.append(m)
    return maps


def kernel(x, c, norm_g, ada_w, ada_b, even_w_in, pool_w, pool_scale, even_w_out,
           odd_w_in, conv_w, conv_b, odd_w_out, final_g):
    x = np.asarray(x)
    B, S, _ = x.shape
    maps = _layout_inputs(x, c, norm_g, ada_w, ada_b, even_w_in, pool_w, pool_scale, even_w_out,
                          odd_w_in, conv_w, conv_b, odd_w_out, final_g)
    if S not in _NC_CACHE:
        _NC_CACHE[S] = build_nc(S)
    nc = _NC_CACHE[S]
    res = run_bass_kernel_spmd(nc, maps, core_ids=list(range(B)))
    return np.stack([np.asarray(r["out"], dtype=np.float32) for r in res.results], axis=0)
```

```python
import contextlib
import numpy as np
import concourse.bass as bass
import concourse.mybir as mybir
from concourse.bass_utils import run_bass_kernel_spmd

F32 = mybir.dt.float32
BF16 = mybir.dt.bfloat16
AF = mybir.ActivationFunctionType
ALU = mybir.AluOpType

D = 1024
TT = 512
NEG = -30000.0
EPS = 1e-6
WINS = (2, 4, 8, 16)
NWSLOT = 3
USE_WCACHE = True


class Op:
    __slots__ = ("eng", "fn", "deps", "dma", "sem", "sigval", "needs_sig", "pos", "grp")


class Prog:
    ENGS = ("pe", "act", "dve", "pool", "sp")

    def __init__(self, nc, stack):
        self.nc = nc
        self.stack = stack
        self.ops = {e: [] for e in self.ENGS}
        self.lastw = {}
        self.readers = {}
        self.esem = {e: stack.enter_context(nc.semaphore("S_" + e)) for e in self.ENGS}
        self.dsem = {}
        self.dcount = {}
        self.cur_grp = {e: None for e in self.ENGS}
        self.grp_ctr = 0

    def group_begin(self, eng):
        self.grp_ctr += 1
        self.cur_grp[eng] = self.grp_ctr

    def group_end(self, eng):
        self.cur_grp[eng] = None

    def op(self, eng, fn, reads=(), writes=(), dma_slot=None, ndma=1):
        o = Op()
        o.eng = eng
        o.fn = fn
        o.dma = dma_slot is not None
        o.needs_sig = False
        o.sem = None
        o.sigval = 0
        o.pos = len(self.ops[eng])
        o.grp = self.cur_grp[eng]
        if eng != "pe":
            extra = [k for k in reads if isinstance(k, tuple) and k[0] == "ps" and k not in writes]
            if extra:
                writes = list(writes) + extra
        deps = []
        for k in reads:
            w = self.lastw.get(k)
            if w is not None:
                deps.append(w)
        for k in writes:
            w = self.lastw.get(k)
            if w is not None:
                deps.append(w)
            r = self.readers.get(k)
            if r:
                deps.extend(r[0].values())
                deps.extend(r[1])
        best = {}
        dl = []
        seen = set()
        for d in deps:
            if d.dma:
                if id(d) not in seen:
                    seen.add(id(d))
                    dl.append(d)
            else:
                if d.eng == "pe" and eng == "pe" and not o.dma:
                    continue
                b = best.get(d.eng)
                if b is None or d.pos > b.pos:
                    best[d.eng] = d
        for d in best.values():
            d.needs_sig = True
            dl.append(d)
        o.deps = dl
        for k in reads:
            r = self.readers.get(k)
            if r is None:
                r = self.readers[k] = ({}, [])
            if o.dma:
                r[1].append(o)
            else:
                r[0][eng] = o
        for k in writes:
            self.lastw[k] = o
            self.readers[k] = ({}, [])
        if o.dma:
            if dma_slot not in self.dsem:
                self.dsem[dma_slot] = self.stack.enter_context(self.nc.semaphore("D_" + dma_slot))
                self.dcount[dma_slot] = 0
            self.dcount[dma_slot] += 16 * ndma
            o.sem = self.dsem[dma_slot]
            o.sigval = self.dcount[dma_slot]
        self.ops[eng].append(o)
        return o

    def emit(self):
        for e in self.ENGS:
            c = 0
            for o in self.ops[e]:
                if not o.dma and o.needs_sig:
                    c += 1
                    o.sigval = c
                    o.sem = self.esem[e]
        with self.nc.Block() as block:
            decos = {"pe": block.tensor, "act": block.scalar, "dve": block.vector,
                     "pool": block.gpsimd, "sp": block.sync}
            for e in self.ENGS:
                def body(eng, e=e):
                    waited = {}
                    ops = self.ops[e]
                    n_ops = len(ops)
                    idx = 0
                    while idx < n_ops:
                        o = ops[idx]
                        j = idx + 1
                        if o.grp is not None:
                            while j < n_ops and ops[j].grp == o.grp:
                                j += 1
                        need = {}
                        for oo in ops[idx:j]:
                            for d in oo.deps:
                                if d.eng == e and (not d.dma) and d.pos >= idx:
                                    continue
                                k = id(d.sem)
                                if d.sigval > need.get(k, (None, 0))[1]:
                                    need[k] = (d.sem, d.sigval)
                        for k, (sem, val) in need.items():
                            if waited.get(k, 0) >= val:
                                continue
                            eng.wait_ge(sem, val)
                            waited[k] = val
                        for oo in ops[idx:j]:
                            ins = oo.fn(eng)
                            if oo.dma:
                                for i_ in (ins if isinstance(ins, (list, tuple)) else [ins]):
                                    i_.then_inc(oo.sem, 16)
                            elif oo.needs_sig:
                                ins.then_inc(oo.sem, 1)
                        idx = j
                    if e == "sp":
                        for slot, sem in self.dsem.items():
                            eng.wait_ge(sem, self.dcount[slot])
                decos[e](body)


def build_nc(SEQ=4096, do0=True, do1=True, final=True, overlap=True):
    NT = SEQ // TT
    NBLK = SEQ // 128
    nc = bass.Bass("TRN2", target_bir_lowering=False)

    def din(name, shape):
        return nc.dram_tensor(name, list(shape), F32, kind="ExternalInput").ap()

    x_d = din("x", [SEQ, D])
    cT_d = din("cT", [128, 8])
    ngT_d = din("normgT", [128, 2, 8])
    adaw_d = din("ada_w", [2, D, 3 * D])
    adab_d = din("ada_b", [1, 2 * 3 * D])
    win0_d = din("w_in0", [D, 6144])
    poolw_d = din("pool_w", [4, 256, 256])
    pscT_d = din("pscaleT", [128, 8])
    wout0_d = din("w_out0", [2048, D])
    win1_d = din("w_in1", [D, 8192])
    cwT_d = din("convwT", [128, 16, 3])
    cbT_d = din("convbT", [128, 16])
    wout1_d = din("w_out1", [2048, D])
    fg_d = din("final_g", [1, D])
    out_d = nc.dram_tensor("out", [SEQ, D], F32, kind="ExternalOutput").ap()
    kd = nc.dram_tensor("kd", [8, 128, SEQ], BF16).ap()
    vd = nc.dram_tensor("vd", [8, 128, NBLK, 128], BF16).ap()
    NPIECE = 64
    wc = nc.dram_tensor("wcache", [NPIECE, 128, 8, 512], BF16).ap()

    with contextlib.ExitStack() as stack:
        P = Prog(nc, stack)

        def sb(name, shape, dtype):
            return stack.enter_context(nc.sbuf_tensor("s_" + name, list(shape), dtype))

        psall = stack.enter_context(nc.psum_tensor("psall", [128, 8, 512], F32))
        ps = [psall[:, b, :] for b in range(8)]
        psk = [("ps", b) for b in range(8)]
        bank_rr = [0]

        def nbank():
            b = bank_rr[0]
            bank_rr[0] = (b + 1) % 8
            return b

        onesf = sb("onesf", [128, 128], F32)
        identf = sb("identf", [128, 128], F32)
        tmpf = sb("tmpf", [128, 128], F32)
        negf = sb("negf", [128, 128], F32)
        tri = sb("tri", [128, 128], BF16)
        onesb = sb("onesb", [128, 128], BF16)
        identb = sb("identb", [128, 128], BF16)
        maskb = sb("maskb", [128, 128], BF16)
        zerob = sb("zerob", [128, 128], BF16)
        invc = sb("invc", [128, 4, 16], F32)
        cT = sb("cT", [128, 8], F32)
        scb = sb("scb", [128, 8], BF16)
        ngT = sb("ngT", [128, 2, 8], F32)
        GT = sb("GT", [128, 2, 8], F32)
        shT = sb("shT", [128, 2, 8], F32)
        G1bc = sb("G1bc", [128, 2, D], F32)
        fgbc = sb("fgbc", [128, D], F32)
        pscT = sb("pscT", [128, 8], F32)
        cwT = sb("cwT", [128, 16, 3], F32)
        cbT = sb("cbT", [128, 16], F32)
        uh = sb("uh", [128, 8, 16], F32)
        uuh = sb("uuh", [128, 16, 2], F32)
        ss = sb("ss", [128, 8], F32)
        rstd = sb("rstd", [128, 8], F32)
        epsb = sb("epsb", [128, 1], F32)
        xt = [sb("xt%d" % j, [128, 4, D], F32) for j in range(2)]
        xn = sb("xn", [128, 4, D], F32)
        MSBK = [("xn", 0), ("xn", 1), ("xn", 2)]
        ADBK = [("xt", 1)]

        def msb_ap(c0, c1):
            b = c0 // D
            assert (c1 - 1) // D == b
            return xn[0:1, b, c0 - b * D:c1 - b * D]

        def adab_ap(c0, c1):
            b = c0 // D
            assert (c1 - 1) // D == b
            return xt[1][0:1, b, c0 - b * D:c1 - b * D]
        fgrow = xn[0:1, 3, :]
        hT = sb("hT", [128, 8, TT], BF16)
        qpad = sb("qpad", [128, 16, TT], BF16)
        pT = sb("pT", [128, 8, TT], BF16)
        yT = sb("yT", [128, 16, TT], BF16)
        kst = pT
        vst = sb("vst", [128, 4, D], BF16)
        kbuf = [sb("kbuf%d" % j, [128, SEQ], BF16) for j in range(2)]
        vbuf = [sb("vbuf%d" % j, [128, NBLK, 128], BF16) for j in range(2)]
        F2 = [sb("F2_%d" % j, [128, 2, 528], F32) for j in range(5)]
        Fs = [F2[j // 2][:, j % 2, :] for j in range(10)]
        SP2 = [sb("SP2_%d" % j, [128, 2, 512], BF16) for j in range(2)]
        A2 = [sb("A2_%d" % j, [128, 2, 512], BF16) for j in range(2)]
        otri = sb("otri", [128, 128], BF16)
        wsl = [sb("w%d" % j, [128, 8, 512], BF16) for j in range(NWSLOT)]
        wcnt = [0]

        def MM(out, lhsT, rhs, start, stop, reads, writes, sgc=False):
            if sgc:
                P.op("pe", lambda e: e.matmul(out, lhsT, rhs, start=start, stop=stop, skip_group_check=True),
                     reads, writes)
            else:
                P.op("pe", lambda e: e.matmul(out, lhsT, rhs, start=start, stop=stop), reads, writes)

        def TR(out, in_, reads, writes):
            P.op("pe", lambda e: e.transpose(out, in_, identf[:]), list(reads) + ["identf"], writes)

        def ACT(out, in_, func, reads, writes, bias=None, scale=None, accum=None):
            kw = {}
            if bias is not None:
                kw["bias"] = bias
            if scale is not None:
                kw["scale"] = scale
            if accum is not None:
                kw["accum_out"] = accum
            P.op("act", lambda e: e.activation(out, in_, func, **kw), reads, writes)

        def TTo(eng, out, in0, in1, op, reads, writes):
            P.op(eng, lambda e: e.tensor_tensor(out, in0, in1, op), reads, writes)

        def TS(eng, out, in0, s1, s2, op0, op1, reads, writes):
            if s2 is None:
                P.op(eng, lambda e: e.tensor_scalar(out, in0, s1, None, op0), reads, writes)
            else:
                P.op(eng, lambda e: e.tensor_scalar(out, in0, s1, s2, op0, op1), reads, writes)

        def STT(eng, out, in0, sc, in1, op0, op1, reads, writes):
            P.op(eng, lambda e: e.scalar_tensor_tensor(out, in0, sc, in1, op0, op1), reads, writes)

        def CP(eng, out, in_, reads, writes):
            if eng == "act":
                P.op("act", lambda e: e.activation(out, in_, AF.Copy), reads, writes)
            else:
                P.op(eng, lambda e: e.tensor_copy(out, in_), reads, writes)

        def MS(eng, ap, val, writes):
            P.op(eng, lambda e: e.memset(ap, val), (), writes)

        tctx = {"i": -1, "k": 0}

        def wload(dst_fn, src, piece=None, tile=None):
            s = wcnt[0] % NWSLOT
            wcnt[0] += 1
            dst = dst_fn(wsl[s])
            if piece is None:
                tile = tctx["i"]
                if tile >= 0:
                    piece = tctx["k"]
                    tctx["k"] += 1
            if piece is None or tile < 0 or not USE_WCACHE:
                P.op("pool", lambda e: e.dma_start(out=dst, in_=src), (), [("w", s)], dma_slot="w%d" % s)
                return s
            assert piece < NPIECE
            cv = dst_fn(wc[piece])
            if tile == 0:
                P.op("pool", lambda e: e.dma_start(out=dst, in_=src), (), [("w", s)], dma_slot="w%d" % s)
                P.op("sp", lambda e: e.dma_start(out=cv, in_=dst), [("w", s)], [("wc", piece)], dma_slot="wst%d" % s)
            else:
                P.op("pool", lambda e: e.dma_start(out=dst, in_=cv), [("wc", piece)], [("w", s)], dma_slot="w%d" % s)
            return s

        def wload_std(wd, col0, piece=None, tile=None):
            src = wd[:, col0:col0 + 512].rearrange("(kc p) n -> p kc n", p=128)
            return wload(lambda w: w[:, :, :], src, piece, tile)

        MS("pool", onesf[:], 1.0, ["onesf"])
        MS("pool", epsb[:], EPS, ["epsb"])
        MS("pool", negf[:], NEG, ["negf"])
        MS("pool", zerob[:], 0.0, ["zerob"])
        MS("pool", onesb[:], 1.0, ["onesb"])
        MS("pool", qpad[:], 0.0, ["qpad"])
        MS("pool", uh[:], 0.0, ["uh"])
        MS("pool", uuh[:], 0.0, ["uuh"])
        P.op("pool", lambda e: e.affine_select(out=identf[:], in_=onesf[:], pattern=[[-1, 128]],
                                               compare_op=ALU.is_equal, fill=0.0, base=0,
                                               channel_multiplier=1), ["onesf"], ["identf"])
        P.op("pool", lambda e: e.affine_select(out=tmpf[:], in_=onesf[:], pattern=[[-1, 128]],
                                               compare_op=ALU.is_ge, fill=0.0, base=0,
                                               channel_multiplier=1), ["onesf"], ["tmpf"])
        CP("dve", tri[:], tmpf[:], ["tmpf"], ["tri"])
        TS("dve", otri[:], tmpf[:], -1.0, 1.0, ALU.mult, ALU.add, ["tmpf"], ["otri"])
        CP("dve", identb[:], identf[:], ["identf"], ["identb"])
        P.op("pool", lambda e: e.affine_select(out=tmpf[:], in_=negf[:], pattern=[[-1, 128]],
                                               compare_op=ALU.is_ge, fill=0.0, base=0,
                                               channel_multiplier=1), ["negf"], ["tmpf"])
        CP("dve", maskb[:], tmpf[:], ["tmpf"], ["maskb"])
        for g, w in enumerate(WINS):
            for t in range(w - 1):
                MS("dve", invc[:, g, t:t + 1], 1.0 / (t + 1), ["invc"])
            MS("dve", invc[:, g, w - 1:16], 1.0 / w, ["invc"])

        def const_loads(e):
            return [
                e.dma_start(out=cT[:], in_=cT_d),
                e.dma_start(out=ngT[:], in_=ngT_d),
                e.dma_start(out=pscT[:], in_=pscT_d),
                e.dma_start(out=cwT[:], in_=cwT_d),
                e.dma_start(out=cbT[:], in_=cbT_d),
                e.dma_start(out=fgrow, in_=fg_d),
            ]
        P.op("sp", const_loads, (), ["cT", "ngT", "pscT", "cwT", "cbT", ("xn", 3)],
             dma_slot="const", ndma=6)

        def xload(i):
            j = i % 2
            src = x_d[i * TT:(i + 1) * TT, :].rearrange("(b p) d -> p b d", p=128)
            P.op("sp", lambda e: e.dma_start(out=xt[j][:], in_=src), (), [("xt", j)], dma_slot="x%d" % j)

        xload(0)

        ACT(scb[:], cT[:], AF.Silu, ["cT"], ["scb"])
        for l in range(2):
            if (l == 0 and not do0) or (l == 1 and not do1):
                continue
            P.op("sp", lambda e, l=l: e.dma_start(
                out=xt[1][0:1, 0:3, :], in_=adab_d[0:1, l * 3072:(l + 1) * 3072].rearrange("o (b n) -> o b n", n=D)),
                (), ADBK, dma_slot="ab")
            for kc in range(8):
                src = adaw_d[l, kc * 128:(kc + 1) * 128, :].rearrange("p (g n) -> p g n", n=512)
                s = wload(lambda w: w[:, 0:6, :], src)
                for g in range(6):
                    MM(ps[g][0:1, :], scb[:, kc:kc + 1], wsl[s][:, g, :], kc == 0, kc == 7,
                       ["scb", ("w", s)], [psk[g]])
            for g in range(6):
                TTo("dve", msb_ap(g * 512, (g + 1) * 512), ps[g][0:1, :], adab_ap(g * 512, (g + 1) * 512), ALU.add,
                    [psk[g]] + ADBK, MSBK)
            for b_ in (1, 2):
                TS("dve", xn[0:1, b_, :], xn[0:1, b_, :], 1.0, None, ALU.add, None, MSBK, MSBK)
            for h in range(2):
                MM(ps[6][:, :], onesf[0:1, 0:128], msb_ap(2048 + h * 512, 2048 + (h + 1) * 512),
                   True, True, ["onesf"] + MSBK, [psk[6]])
                CP("act", G1bc[:, l, h * 512:(h + 1) * 512], ps[6][:, :], [psk[6]], [("G1bc", l)])
            for j in range(16):
                MM(ps[7][:, j:j + 1], msb_ap(j * 128, (j + 1) * 128), onesf[0:1, 0:1], True, True,
                   MSBK + ["onesf"], [psk[7]])
            CP("dve", shT[:, l, :], ps[7][:, 0:8], [psk[7]], [("shT", l)])
            TTo("dve", GT[:, l, :], ps[7][:, 8:16], ngT[:, l, :], ALU.mult, [psk[7], "ngT"], [("GT", l)])
        if final:
            for h in range(2):
                MM(ps[6][:, :], onesf[0:1, 0:128], fgrow[:, h * 512:(h + 1) * 512], True, True,
                   ["onesf", ("xn", 3)], [psk[6]])
                CP("act", fgbc[:, h * 512:(h + 1) * 512], ps[6][:, :], [psk[6]], ["fgbc"])

        def stage_norm(i, l, X):
            xk = ("xt", i % 2)
            for blk in range(4):
                ACT(xn[:, blk, :], X[:, blk, :], AF.Square, [xk], [("xn", blk), ("ss", blk)],
                    accum=ss[:, blk:blk + 1])
                ACT(rstd[:, blk:blk + 1], ss[:, blk:blk + 1], AF.Ln, [("ss", blk), "epsb"], [("rstd", blk)],
                    bias=epsb[:, 0:1], scale=1.0 / D)
                ACT(rstd[:, blk:blk + 1], rstd[:, blk:blk + 1], AF.Exp, [("rstd", blk)], [("rstd", blk)],
                    scale=-0.5)
                TS("dve", xn[:, blk, :], X[:, blk, :], rstd[:, blk:blk + 1], None, ALU.mult, None,
                   [xk, ("rstd", blk)], [("xn", blk)])
            for kc in range(8):
                for blk in range(4):
                    TR(ps[kc][:, blk * 128:(blk + 1) * 128], xn[:, blk, kc * 128:(kc + 1) * 128],
                       [("xn", blk)], [psk[kc]])
            for kc in range(8):
                ACT(hT[:, kc, :], ps[kc][:, :], AF.Identity, [psk[kc], ("GT", l), ("shT", l)], [("hT", kc)],
                    bias=shT[:, l, kc:kc + 1], scale=GT[:, l, kc:kc + 1])

        def proj_fm(s, c, b):
            for kc in range(8):
                MM(ps[b][:, :], wsl[s][:, kc, c * 128:(c + 1) * 128], hT[:, kc, :], kc == 0, kc == 7,
                   [("w", s), ("hT", kc)], [psk[b]])

        def stage_outproj(i, l, X, wd):
            xk = ("xt", i % 2)
            for half in range(2):
                banks = [nbank() for _ in range(4)]
                for j in range(2):
                    src = wd[j * 1024:(j + 1) * 1024, half * 512:(half + 1) * 512].rearrange(
                        "(kc p) n -> p kc n", p=128)
                    s = wload(lambda w: w[:, :, :], src)
                    for blk in range(4):
                        for kc in range(8):
                            ch = j * 8 + kc
                            MM(ps[banks[blk]][:, :], yT[:, ch, blk * 128:(blk + 1) * 128], wsl[s][:, kc, :],
                               j == 0 and kc == 0, j == 1 and kc == 7,
                               [("yT", ch), ("w", s)], [psk[banks[blk]]])
                for blk in range(4):
                    f = 6 + (blk % 2)
                    TTo("dve", Fs[f][:, 0:512], ps[banks[blk]][:, :], G1bc[:, l, half * 512:(half + 1) * 512],
                        ALU.mult, [psk[banks[blk]], ("G1bc", l)], [("F", f)])
                    TTo("dve", X[:, blk, half * 512:(half + 1) * 512], X[:, blk, half * 512:(half + 1) * 512],
                        Fs[f][:, 0:512], ALU.add, [("F", f), xk], [xk])

        BGB = [6, 7]
        bgb_ctr = [0]
        bgmode = {"alone": False}

        def bg_bank():
            b = BGB[bgb_ctr[0] % 2]
            bgb_ctr[0] += 1
            return b

        def run_all(units):
            for u in units:
                for _ in u():
                    pass

        def y4ap(yb, c, a=0, b=512):
            return vst[:, 2 * yb + c // 2, (c % 2) * 512 + a:(c % 2) * 512 + b]

        def norm_units_l1(i):
            X = xt[i % 2]
            xk = ("xt", i % 2)
            units = []
            for blk in range(4):
                for half in range(2):
                    def u(blk=blk, half=half):
                        if half == 0:
                            ACT(xn[:, blk, :], X[:, blk, :], AF.Square, [xk], [("xn", blk), ("ss", blk)],
                                accum=ss[:, blk:blk + 1])
                            ACT(rstd[:, blk:blk + 1], ss[:, blk:blk + 1], AF.Ln, [("ss", blk), "epsb"],
                                [("rstd", blk)], bias=epsb[:, 0:1], scale=1.0 / D)
                            ACT(rstd[:, blk:blk + 1], rstd[:, blk:blk + 1], AF.Exp, [("rstd", blk)],
                                [("rstd", blk)], scale=-0.5)
                            TS("dve", xn[:, blk, :], X[:, blk, :], rstd[:, blk:blk + 1], None, ALU.mult, None,
                               [xk, ("rstd", blk)], [("xn", blk)])
                            yield
                        b = bg_bank()
                        for k4 in range(4):
                            kc = half * 4 + k4
                            TR(ps[b][:, k4 * 128:(k4 + 1) * 128], xn[:, blk, kc * 128:(kc + 1) * 128],
                               [("xn", blk)], [psk[b]])
                            if k4 == 1:
                                yield
                        yield
                        for k4 in range(4):
                            kc = half * 4 + k4
                            if bgmode["alone"]:
                                ACT(hT[:, kc, blk * 128:(blk + 1) * 128], ps[b][:, k4 * 128:(k4 + 1) * 128],
                                    AF.Identity, [psk[b], ("GT", 1), ("shT", 1)], [("hT", kc)],
                                    bias=shT[:, 1, kc:kc + 1], scale=GT[:, 1, kc:kc + 1])
                            else:
                                TS("dve", hT[:, kc, blk * 128:(blk + 1) * 128], ps[b][:, k4 * 128:(k4 + 1) * 128],
                                   GT[:, 1, kc:kc + 1], shT[:, 1, kc:kc + 1], ALU.mult, ALU.add,
                                   [psk[b], ("GT", 1), ("shT", 1)], [("hT", kc)])
                    units.append(u)
            return units

        def proj_steps(s, c, b):
            for kc in range(8):
                MM(ps[b][:, :], wsl[s][:, kc, c * 128:(c + 1) * 128], hT[:, kc, :], kc == 0, kc == 7,
                   [("w", s), ("hT", kc)], [psk[b]])
                if kc % 2 == 1 and kc < 7:
                    yield

        def layer1_units(i):
            X = xt[i % 2]
            xk = ("xt", i % 2)
            units = list(norm_units_l1(i))
            first_norm = units[0]

            def kick():
                need(0)
                yield from first_norm()
            units[0] = kick
            USB = [xn[:, 0, 0:512], xn[:, 0, 512:1024], xn[:, 1, 0:512], xn[:, 1, 512:1024]]
            USK = [("xn", 0), ("xn", 0), ("xn", 1), ("xn", 1)]
            UU, UK = xn[:, 2, 0:514], ("xn", 2)
            T1, TK = xn[:, 3, 0:512], ("xn", 3)
            T2 = xn[:, 3, 512:1024]
            PB = 40
            slots = {}

            def do_load(k):
                q, t = k // 5, k % 5
                if t < 4:
                    col = [(12 + q), (8 + q), (4 + q), q][t] * 512
                    return wload_std(win1_d, col, PB + k, i)
                src = wout1_d[q * 512:(q + 1) * 512, :].rearrange("(kc p) n -> p kc n", p=128)
                return wload(lambda w: w[:, :, :].rearrange("p a b -> p (a b)").rearrange(
                    "p (a b) -> p a b", b=1024), src, PB + k, i)

            def need(k):
                for kk in (k, k + 1):
                    if kk < 20 and kk not in slots:
                        slots[kk] = do_load(kk)
                return slots[k]

            for q in range(4):
                yb = q % 2
                hg, hu, hc, hb, ho = [None], [None], [None], [None], [None]
                for c in range(4):
                    def u(c=c, hg=hg, yb=yb, q=q):
                        if c == 0:
                            hg[0] = need(q * 5 + 0)
                        b = bg_bank()
                        yield from proj_steps(hg[0], c, b)
                        yield
                        if bgmode["alone"]:
                            ACT(y4ap(yb, c), ps[b][:, :], AF.Silu, [psk[b]], ["vst"])
                        else:
                            f = T1 if c % 2 == 0 else T2
                            ACT(f, ps[b][:, :], AF.Exp, [psk[b]], [TK], scale=-1.0)
                            TS("dve", f, f, 1.0, None, ALU.add, None, [TK], [TK])
                            P.op("dve", lambda e, o=f: e.reciprocal(o, o), [TK], [TK])
                            TTo("dve", y4ap(yb, c), ps[b][:, :], f, ALU.mult, [psk[b], TK], ["vst"])
                    units.append(u)
                for c in range(4):
                    def u(c=c, hu=hu, q=q):
                        if c == 0:
                            hu[0] = need(q * 5 + 1)
                        b = bg_bank()
                        yield from proj_steps(hu[0], c, b)
                        yield
                        CP("act" if bgmode["alone"] else "dve", USB[c], ps[b][:, :], [psk[b]], [USK[c]])
                    units.append(u)
                for c in range(4):
                    def u(c=c, hc=hc, q=q):
                        if c == 0:
                            hc[0] = need(q * 5 + 2)
                        ch = q * 4 + c
                        b = bg_bank()
                        yield from proj_steps(hc[0], c, b)
                        yield
                        CP("dve", UU[:, 0:2], uuh[:, ch, :], [("uuh", ch)], [UK])
                        TTo("dve", UU[:, 2:514], ps[b][:, :], USB[c], ALU.mult, [psk[b], USK[c]], [UK])
                        CP("dve", uuh[:, ch, :], UU[:, 512:514], [UK], [("uuh", ch)])
                        TS("dve", T1, UU[:, 2:514], cwT[:, ch, 2:3], cbT[:, ch:ch + 1],
                           ALU.mult, ALU.add, [UK, "cwT", "cbT"], [TK])
                        STT("dve", T1, UU[:, 1:513], cwT[:, ch, 1:2], T1, ALU.mult, ALU.add, [UK, "cwT", TK], [TK])
                        STT("dve", USB[c], UU[:, 0:512], cwT[:, ch, 0:1], T1, ALU.mult, ALU.add,
                            [UK, "cwT", TK], [USK[c]])
                    units.append(u)
                for c in range(4):
                    def u(c=c, hb=hb, yb=yb, q=q):
                        if c == 0:
                            hb[0] = need(q * 5 + 3)
                        b = bg_bank()
                        yield from proj_steps(hb[0], c, b)
                        yield
                        TTo("dve", T1, ps[b][:, :], USB[c], ALU.mult, [psk[b], USK[c]], [TK])
                        TTo("dve", y4ap(yb, c), T1, y4ap(yb, c), ALU.mult, [TK, "vst"], ["vst"])
                    units.append(u)
                for half in range(2):
                    for blk in range(4):
                        def u(half=half, blk=blk, ho=ho, yb=yb, q=q):
                            if half == 0 and blk == 0:
                                ho[0] = need(q * 5 + 4)
                            b = bg_bank()
                            wv = wsl[ho[0]][:, :, :].rearrange("p a b -> p (a b)").rearrange("p (a b) -> p a b", b=1024)
                            for c in range(4):
                                MM(ps[b][:, :], y4ap(yb, c, blk * 128, (blk + 1) * 128),
                                   wv[:, c, half * 512:(half + 1) * 512], c == 0, c == 3,
                                   ["vst", ("w", ho[0])], [psk[b]])
                                if c == 1:
                                    yield
                            yield
                            f = T1 if blk % 2 == 0 else T2
                            TTo("dve", f, ps[b][:, :], G1bc[:, 1, half * 512:(half + 1) * 512],
                                ALU.mult, [psk[b], ("G1bc", 1)], [TK])
                            TTo("dve", X[:, blk, half * 512:(half + 1) * 512], X[:, blk, half * 512:(half + 1) * 512],
                                f, ALU.add, [TK, xk], [xk])
                        units.append(u)
            return units

        def finish_units(i):
            def mk(blk):
                def u():
                    finish_blk(i, blk)
                    return
                    yield
                return u
            return [mk(blk) for blk in range(4)]

        def layer0(i, bg=()):
            X = xt[i % 2]
            tok0 = i * TT
            stage_norm(i, 0, X)
            for g in range(2):
                s = wload_std(win0_d, (4 + g) * 512)
                for c in range(4):
                    hp = g * 4 + c
                    b = nbank()
                    proj_fm(s, c, b)
                    CP("act" if c % 2 == 0 else "dve", kst[:, hp, :], ps[b][:, :], [psk[b]], [("pT", hp)])
            P.op("sp", lambda e: e.dma_start(out=kd[:, :, tok0:tok0 + TT].rearrange("h p t -> p h t"),
                                             in_=kst[:]), [("pT", c_) for c_ in range(8)], [("kd", i)], dma_slot="ks")
            for vh in range(2):
                s = wload_std(win0_d, (6 + vh) * 512)
                for blk in range(4):
                    b = nbank()
                    for kc in range(8):
                        MM(ps[b][:, :], hT[:, kc, blk * 128:(blk + 1) * 128], wsl[s][:, kc, :], kc == 0, kc == 7,
                           [("hT", kc), ("w", s)], [psk[b]])
                    CP("act" if blk % 2 == 0 else "dve", vst[:, blk, vh * 512:(vh + 1) * 512], ps[b][:, :],
                       [psk[b]], ["vst"])

            def vstore(e):
                return [e.dma_start(out=vd[hp, :, 4 * i:4 * i + 4, :], in_=vst[:, :, hp * 128:(hp + 1) * 128])
                        for hp in range(8)]
            P.op("sp", vstore, ["vst"], [("vd", i)], dma_slot="vs", ndma=8)
            for g in range(4):
                s = wload_std(win0_d, (8 + g) * 512)
                for c in range(4):
                    ch = g * 4 + c
                    b = nbank()
                    proj_fm(s, c, b)
                    ACT(yT[:, ch, :], ps[b][:, :], AF.Silu, [psk[b]], [("yT", ch)])
            for g in range(2):
                s = wload_std(win0_d, (2 + g) * 512)
                for c in range(4):
                    hp = g * 4 + c
                    b = nbank()
                    proj_fm(s, c, b)
                    TS("dve", qpad[0:64, 2 * hp, :], ps[b][0:64, :], 0.125, None, ALU.mult, None,
                       [psk[b]], [("qp", 2 * hp)])
                    P.op("act", lambda e, o=qpad[64:128, 2 * hp + 1, :], n=ps[b][64:128, :]:
                         e.mul(o, n, 0.125), [psk[b]], [("qp", 2 * hp + 1)])
            for g2 in range(2):
                s = wload_std(win0_d, g2 * 512)
                for c4 in range(4):
                    c = g2 * 4 + c4
                    grp = c // 2
                    w = WINS[grp]
                    b = nbank()
                    proj_fm(s, c4, b)
                    U = Fs[c % 2]
                    uk = ("F", c % 2)
                    CP("dve", U[:, 0:16], uh[:, c, :], [("uh", c)], [uk])
                    CP("act", U[:, 16:528], ps[b][:, :], [psk[b]], [uk])
                    CP("dve", uh[:, c, :], U[:, 512:528], [uk], [("uh", c)])
                    ta, tb = (2, 3) if c % 2 == 0 else (4, 5)
                    TTo("dve", Fs[ta][:, 1:528], U[:, 1:528], U[:, 0:527], ALU.add, [uk], [("F", ta)])
                    cur = ta
                    sh = 2
                    while sh < w:
                        oth = tb if cur == ta else ta
                        TTo("dve", Fs[oth][:, 2 * sh - 1:528], Fs[cur][:, 2 * sh - 1:528],
                            Fs[cur][:, sh - 1:528 - sh], ALU.add, [("F", cur)], [("F", oth)])
                        cur = oth
                        sh *= 2
                    STT("dve", pT[:, c, :], Fs[cur][:, 16:528], 1.0 / w, U[:, 16:528], ALU.mult, ALU.subtract,
                        [("F", cur), uk], [("pT", c)])
                    if i == 0:
                        oth = tb if cur == ta else ta
                        TTo("dve", Fs[oth][:, 0:16], Fs[cur][:, 16:32], invc[:, grp, :], ALU.mult,
                            [("F", cur), "invc"], [("F", oth)])
                        TTo("dve", pT[:, c, 0:16], Fs[oth][:, 0:16], U[:, 16:32], ALU.subtract,
                            [("F", oth), uk], [("pT", c)])
            src = poolw_d.rearrange("g (cc p) d -> p (g cc) d", p=128)
            s = wload(lambda w: w[:, :, 0:256], src)
            for dch in range(8):
                grp, dd = dch // 2, dch % 2
                b = nbank()
                for cc in range(2):
                    MM(ps[b][:, :], wsl[s][:, grp * 2 + cc, dd * 128:(dd + 1) * 128], pT[:, grp * 2 + cc, :],
                       cc == 0, cc == 1, [("w", s), ("pT", grp * 2 + cc)], [psk[b]])
                STT("dve", yT[:, dch, :], ps[b][:, :], pscT[:, dch:dch + 1], yT[:, dch, :], ALU.mult, ALU.mult,
                    [psk[b], "pscT", ("yT", dch)], [("yT", dch)])
            ntok = (i + 1) * TT
            nblk = 4 * (i + 1)

            def kvload(hp):
                j = hp % 2
                rk = [("kd", t) for t in range(i + 1)]
                rv = [("vd", t) for t in range(i + 1)]
                P.op("sp", lambda e: e.dma_start(out=kbuf[j][:, 0:ntok], in_=kd[hp, :, 0:ntok]), rk,
                     [("kb", j)], dma_slot="kb%d" % j)
                P.op("sp", lambda e: e.dma_start(out=vbuf[j][:, 0:nblk, :], in_=vd[hp, :, 0:nblk, :]), rv,
                     [("vb", j)], dma_slot="vb%d" % j)

            kvload(0)
            if overlap:
                ZP = [0, 0]
                CBK = [2, 3]
                accb = [4, 5]
            else:
                ZP = [0, 2]
                CBK = [4, 5]
                accb = [6, 7]
            bg = list(bg)
            bgs = {"cur": None, "idx": 0, "steps": 0, "slot": 0}
            BG_STEPS_EST = 448

            def bg_step():
                while True:
                    if bgs["cur"] is None:
                        if bgs["idx"] >= len(bg):
                            return False
                        bgs["cur"] = bg[bgs["idx"]]()
                        bgs["idx"] += 1
                    try:
                        next(bgs["cur"])
                    except StopIteration:
                        bgs["cur"] = None
                    bgs["steps"] += 1
                    return True

            def run_bg(sub):
                if not bg:
                    return
                tgt = (BG_STEPS_EST * (3 * bgs["slot"] + sub + 1) + 3 * (8 * nblk + 2) - 1) // (3 * (8 * nblk + 2))
                if bgs["steps"] < tgt:
                    bg_step()

            def flush_bg():
                bgmode["alone"] = True
                while bg_step():
                    pass
                bgmode["alone"] = False
            items = [(hp, kb) for hp in range(8) for kb in range(nblk - 1, -1, -1)]
            n = len(items)
            st = {}

            def fk(w):
                return [("F", 2 * w), ("F", 2 * w + 1)]

            def A_pe(p):
                hp, kb = items[p]
                j = hp % 2
                r = kb - 4 * i
                col0 = r * 128 if r >= 0 else 0
                st[p] = (hp, kb, r, col0)
                z0 = ZP[p % 2]
                for hd in range(2):
                    zb = z0 + hd
                    MM(ps[zb][:, col0:512], kbuf[j][:, kb * 128:(kb + 1) * 128], qpad[:, 2 * hp + hd, col0:512],
                       True, r < 0, [("kb", j), ("qp", 2 * hp + hd)], [psk[zb]])
                    if r >= 0:
                        MM(ps[zb][:, col0:col0 + 128], identb[:, :], maskb[:, :], False, True,
                           ["identb", "maskb"], [psk[zb]])

            def EXP1(p):
                hp, kb, r, col0 = st[p]
                z0 = ZP[p % 2]
                e_i = p % 3
                ACT(F2[e_i][:, :, col0:512], psall[:, z0:z0 + 2, col0:512], AF.Exp,
                    [psk[z0], psk[z0 + 1]], fk(e_i))

            def LN(p):
                hp, kb, r, col0 = st[p]
                e_i = p % 3
                ACT(SP2[p % 2][:, :, col0:512], F2[e_i][:, :, col0:512], AF.Ln, fk(e_i), [("SP2", p % 2)],
                    bias=1.0)

            def T_pe(p):
                hp, kb, r, col0 = st[p]
                if kb == nblk - 1:
                    for hd in range(2):
                        MM(ps[CBK[hd]][:, :], zerob[:, :], hT[:, 0, :], True, False,
                           ["zerob", ("hT", 0)], [psk[CBK[hd]]], sgc=True)
                for hd in range(2):
                    MM(ps[CBK[hd]][:, col0:512], tri[:, :], SP2[p % 2][:, hd, col0:512], False, False,
                       ["tri", ("SP2", p % 2)], [psk[CBK[hd]]], sgc=True)

            def O_pe(p):
                hp, kb, r, col0 = st[p]
                if kb > 0:
                    for hd in range(2):
                        MM(ps[CBK[hd]][:, col0:512], otri[:, :], SP2[p % 2][:, hd, col0:512], False, False,
                           ["otri", ("SP2", p % 2)], [psk[CBK[hd]]], sgc=True)

            def EXP2_F(p):
                hp, kb, r, col0 = st[p]
                e_i = p % 3
                w_i = 3 + p % 2
                ACT(F2[w_i][:, :, col0:512], psall[:, CBK[0]:CBK[0] + 2, col0:512], AF.Exp,
                    [psk[CBK[0]], psk[CBK[1]]], fk(w_i), scale=-1.0)
                TTo("dve", A2[p % 2][:, :, col0:512], F2[e_i][:, :, col0:512], F2[w_i][:, :, col0:512], ALU.mult,
                    fk(e_i) + fk(w_i), [("A2", p % 2)])

            def G_pe(p):
                hp, kb, r, col0 = st[p]
                j = hp % 2
                if kb == nblk - 1:
                    if hp + 1 < 8:
                        kvload(hp + 1)
                    for hd in range(2):
                        MM(ps[accb[hd]][:, :], zerob[:, :], hT[:, 0, :], True, False,
                           ["zerob", ("hT", 0)], [psk[accb[hd]]])
                for hd in range(2):
                    MM(ps[accb[hd]][:, col0:512], vbuf[j][:, kb, :], A2[p % 2][:, hd, col0:512], False, kb == 0,
                       [("vb", j), ("A2", p % 2)], [psk[accb[hd]]])

            def Y_evac(p):
                hp, kb, r, col0 = st[p]
                if kb == 0:
                    for hd in range(2):
                        pr = slice(hd * 64, (hd + 1) * 64)
                        TTo("dve", yT[pr, 8 + hp, :], ps[accb[hd]][pr, :], yT[pr, 8 + hp, :], ALU.mult,
                            [psk[accb[hd]], ("yT", 8 + hp)], [("yT", 8 + hp)])

            A_pe(0)
            for p in range(n + 2):
                bgs["slot"] = p
                if overlap:
                    P.group_begin("pe")
                    if 0 <= p - 1 < n:
                        T_pe(p - 1)
                    P.group_end("pe")
                    run_bg(0)
                    if p < n:
                        EXP1(p)
                    P.group_begin("pe")
                    if p + 1 < n:
                        A_pe(p + 1)
                    P.group_end("pe")
                    run_bg(1)
                else:
                    P.group_begin("pe")
                    if 0 <= p - 1 < n:
                        T_pe(p - 1)
                    if p + 1 < n:
                        A_pe(p + 1)
                    P.group_end("pe")
                    if p < n:
                        EXP1(p)
                if 0 <= p - 1 < n:
                    EXP2_F(p - 1)
                if p < n:
                    LN(p)
                P.group_begin("pe")
                if 0 <= p - 2 < n:
                    G_pe(p - 2)
                if 0 <= p - 1 < n:
                    O_pe(p - 1)
                P.group_end("pe")
                if 0 <= p - 2 < n:
                    Y_evac(p - 2)
                if overlap:
                    run_bg(2)
            flush_bg()
            stage_outproj(i, 0, X, wout0_d)

        def layer1(i):
            X = xt[i % 2]
            stage_norm(i, 1, X)
            for q in range(4):
                s = wload_std(win1_d, (12 + q) * 512)
                for c in range(4):
                    ch = q * 4 + c
                    b = nbank()
                    proj_fm(s, c, b)
                    ACT(yT[:, ch, :], ps[b][:, :], AF.Silu, [psk[b]], [("yT", ch)])
                s = wload_std(win1_d, (8 + q) * 512)
                for c in range(4):
                    b = nbank()
                    proj_fm(s, c, b)
                    CP("act", Fs[c][:, 0:512], ps[b][:, :], [psk[b]], [("F", c)])
                s = wload_std(win1_d, (4 + q) * 512)
                for c in range(4):
                    ch = q * 4 + c
                    b = nbank()
                    proj_fm(s, c, b)
                    UU = Fs[4 + c % 2]
                    uk = ("F", 4 + c % 2)
                    CP("dve", UU[:, 0:2], uuh[:, ch, :], [("uuh", ch)], [uk])
                    TTo("dve", UU[:, 2:514], ps[b][:, :], Fs[c][:, 0:512], ALU.mult, [psk[b], ("F", c)], [uk])
                    CP("dve", uuh[:, ch, :], UU[:, 512:514], [uk], [("uuh", ch)])
                    t1 = 6 + c % 2
                    TS("dve", Fs[t1][:, 0:512], UU[:, 2:514], cwT[:, ch, 2:3], cbT[:, ch:ch + 1], ALU.mult, ALU.add,
                       [uk, "cwT", "cbT"], [("F", t1)])
                    STT("dve", Fs[t1][:, 0:512], UU[:, 1:513], cwT[:, ch, 1:2], Fs[t1][:, 0:512], ALU.mult, ALU.add,
                        [uk, "cwT", ("F", t1)], [("F", t1)])
                    STT("dve", Fs[c][:, 0:512], UU[:, 0:512], cwT[:, ch, 0:1], Fs[t1][:, 0:512], ALU.mult, ALU.add,
                        [uk, "cwT", ("F", t1)], [("F", c)])
                s = wload_std(win1_d, q * 512)
                for c in range(4):
                    ch = q * 4 + c
                    b = nbank()
                    proj_fm(s, c, b)
                    t1 = 6 + c % 2
                    TTo("dve", Fs[t1][:, 0:512], ps[b][:, :], Fs[c][:, 0:512], ALU.mult, [psk[b], ("F", c)],
                        [("F", t1)])
                    TTo("dve", yT[:, ch, :], Fs[t1][:, 0:512], yT[:, ch, :], ALU.mult, [("F", t1), ("yT", ch)],
                        [("yT", ch)])
            stage_outproj(i, 1, X, wout1_d)

        def finish_blk(i, blk):
            X = xt[i % 2]
            xk = ("xt", i % 2)
            if final:
                ACT(xn[:, blk, :], X[:, blk, :], AF.Square, [xk], [("xn", blk), ("ss", blk)],
                    accum=ss[:, blk:blk + 1])
                ACT(rstd[:, blk:blk + 1], ss[:, blk:blk + 1], AF.Ln, [("ss", blk), "epsb"], [("rstd", blk)],
                    bias=epsb[:, 0:1], scale=1.0 / D)
                ACT(rstd[:, blk:blk + 1], rstd[:, blk:blk + 1], AF.Exp, [("rstd", blk)], [("rstd", blk)],
                    scale=-0.5)
                STT("dve", xn[:, blk, :], X[:, blk, :], rstd[:, blk:blk + 1], fgbc[:, :], ALU.mult, ALU.mult,
                    [xk, ("rstd", blk), "fgbc"], [("xn", blk)])
            else:
                CP("dve", xn[:, blk, :], X[:, blk, :], [xk], [("xn", blk)])
            r0 = i * TT + blk * 128
            P.op("sp", lambda e, r0=r0, blk=blk: e.dma_start(out=out_d[r0:r0 + 128, :], in_=xn[:, blk, :]),
                 [("xn", blk)], [("out", i, blk)], dma_slot="o%d" % blk)

        def finish(i):
            for blk in range(4):
                finish_blk(i, blk)

        pending = []
        for i in range(NT):
            tctx["i"] = i
            tctx["k"] = 0
            if not (overlap and do0 and do1):
                if i + 1 < NT:
                    xload(i + 1)
                if do0:
                    layer0(i)
                if do1:
                    layer1(i)
                finish(i)
                continue
            layer0(i, pending)
            if i + 1 < NT:
                xload(i + 1)
            pending = layer1_units(i) + finish_units(i)
        if pending:
            bgmode["alone"] = True
            run_all(pending)

        P.emit()
    return nc


_NC_CACHE = {}


def _layout_inputs(x, c, norm_g, ada_w, ada_b, even_w_in, pool_w, pool_scale, even_w_out,
                   odd_w_in, conv_w, conv_b, odd_w_out, final_g):
    f = lambda a: np.ascontiguousarray(np.asarray(a, dtype=np.float32))
    shared = {
        "normgT": f(np.asarray(norm_g).reshape(2, 8, 128).transpose(2, 0, 1)),
        "ada_w": f(ada_w),
        "ada_b": f(np.asarray(ada_b).reshape(1, -1)),
        "w_in0": f(np.asarray(even_w_in)[0]),
        "pool_w": f(np.asarray(pool_w)[0]),
        "pscaleT": f(np.asarray(pool_scale)[0].reshape(8, 128).T),
        "w_out0": f(np.asarray(even_w_out)[0]),
        "w_in1": f(np.asarray(odd_w_in)[0]),
        "convwT": f(np.asarray(conv_w)[0].reshape(3, 16, 128).transpose(2, 1, 0)),
        "convbT": f(np.asarray(conv_b)[0].reshape(16, 128).T),
        "w_out1": f(np.asarray(odd_w_out)[0]),
        "final_g": f(np.asarray(final_g).reshape(1, -1)),
    }
    x = np.asarray(x)
    c = np.asarray(c)
    maps = []
    for b in range(x.shape[0]):
        m = dict(shared)
        m["x"] = f(x[b])
        m["cT"] = f(c[b].reshape(8, 128).T)
        maps.append(m)
    return maps


def kernel(x, c, norm_g, ada_w, ada_b, even_w_in, pool_w, pool_scale, even_w_out,
           odd_w_in, conv_w, conv_b, odd_w_out, final_g):
    x = np.asarray(x)
    B, S, _ = x.shape
    maps = _layout_inputs(x, c, norm_g, ada_w, ada_b, even_w_in, pool_w, pool_scale, even_w_out,
                          odd_w_in, conv_w, conv_b, odd_w_out, final_g)
    if S not in _NC_CACHE:
        _NC_CACHE[S] = build_nc(S)
    nc = _NC_CACHE[S]
    res = run_bass_kernel_spmd(nc, maps, core_ids=list(range(B)))
    return np.stack([np.asarray(r["out"], dtype=np.float32) for r in res.results], axis=0)
```

```python
import contextlib
import numpy as np
import concourse.bass as bass
import concourse.mybir as mybir
from concourse.bass_utils import run_bass_kernel_spmd

F32 = mybir.dt.float32
BF16 = mybir.dt.bfloat16
AF = mybir.ActivationFunctionType
ALU = mybir.AluOpType

D = 1024
TT = 512
NEG = -30000.0
EPS = 1e-6
WINS = (2, 4, 8, 16)
NWSLOT = 3
USE_WCACHE = True


class Op:
    __slots__ = ("eng", "fn", "deps", "dma", "sem", "sigval", "needs_sig", "pos", "grp")


class Prog:
    ENGS = ("pe", "act", "dve", "pool", "sp")

    def __init__(self, nc, stack):
        self.nc = nc
        self.stack = stack
        self.ops = {e: [] for e in self.ENGS}
        self.lastw = {}
        self.readers = {}
        self.esem = {e: stack.enter_context(nc.semaphore("S_" + e)) for e in self.ENGS}
        self.dsem = {}
        self.dcount = {}
        self.cur_grp = {e: None for e in self.ENGS}
        self.grp_ctr = 0

    def group_begin(self, eng):
        self.grp_ctr += 1
        self.cur_grp[eng] = self.grp_ctr

    def group_end(self, eng):
        self.cur_grp[eng] = None

    def op(self, eng, fn, reads=(), writes=(), dma_slot=None, ndma=1):
        o = Op()
        o.eng = eng
        o.fn = fn
        o.dma = dma_slot is not None
        o.needs_sig = False
        o.sem = None
        o.sigval = 0
        o.pos = len(self.ops[eng])
        o.grp = self.cur_grp[eng]
        if eng != "pe":
            extra = [k for k in reads if isinstance(k, tuple) and k[0] == "ps" and k not in writes]
            if extra:
                writes = list(writes) + extra
        deps = []
        for k in reads:
            w = self.lastw.get(k)
            if w is not None:
                deps.append(w)
        for k in writes:
            w = self.lastw.get(k)
            if w is not None:
                deps.append(w)
            r = self.readers.get(k)
            if r:
                deps.extend(r[0].values())
                deps.extend(r[1])
        best = {}
        dl = []
        seen = set()
        for d in deps:
            if d.dma:
                if id(d) not in seen:
                    seen.add(id(d))
                    dl.append(d)
            else:
                if d.eng == "pe" and eng == "pe" and not o.dma:
                    continue
                b = best.get(d.eng)
                if b is None or d.pos > b.pos:
                    best[d.eng] = d
        for d in best.values():
            d.needs_sig = True
            dl.append(d)
        o.deps = dl
        for k in reads:
            r = self.readers.get(k)
            if r is None:
                r = self.readers[k] = ({}, [])
            if o.dma:
                r[1].append(o)
            else:
                r[0][eng] = o
        for k in writes:
            self.lastw[k] = o
            self.readers[k] = ({}, [])
        if o.dma:
            if dma_slot not in self.dsem:
                self.dsem[dma_slot] = self.stack.enter_context(self.nc.semaphore("D_" + dma_slot))
                self.dcount[dma_slot] = 0
            self.dcount[dma_slot] += 16 * ndma
            o.sem = self.dsem[dma_slot]
            o.sigval = self.dcount[dma_slot]
        self.ops[eng].append(o)
        return o

    def emit(self):
        for e in self.ENGS:
            c = 0
            for o in self.ops[e]:
                if not o.dma and o.needs_sig:
                    c += 1
                    o.sigval = c
                    o.sem = self.esem[e]
        with self.nc.Block() as block:
            decos = {"pe": block.tensor, "act": block.scalar, "dve": block.vector,
                     "pool": block.gpsimd, "sp": block.sync}
            for e in self.ENGS:
                def body(eng, e=e):
                    waited = {}
                    ops = self.ops[e]
                    n_ops = len(ops)
                    idx = 0
                    while idx < n_ops:
                        o = ops[idx]
                        j = idx + 1
                        if o.grp is not None:
                            while j < n_ops and ops[j].grp == o.grp:
                                j += 1
                        need = {}
                        for oo in ops[idx:j]:
                            for d in oo.deps:
                                if d.eng == e and (not d.dma) and d.pos >= idx:
                                    continue
                                k = id(d.sem)
                                if d.sigval > need.get(k, (None, 0))[1]:
                                    need[k] = (d.sem, d.sigval)
                        for k, (sem, val) in need.items():
                            if waited.get(k, 0) >= val:
                                continue
                            eng.wait_ge(sem, val)
                            waited[k] = val
                        for oo in ops[idx:j]:
                            ins = oo.fn(eng)
                            if oo.dma:
                                for i_ in (ins if isinstance(ins, (list, tuple)) else [ins]):
                                    i_.then_inc(oo.sem, 16)
                            elif oo.needs_sig:
                                ins.then_inc(oo.sem, 1)
                        idx = j
                    if e == "sp":
                        for slot, sem in self.dsem.items():
                            eng.wait_ge(sem, self.dcount[slot])
                decos[e](body)


def build_nc(SEQ=4096, do0=True, do1=True, final=True, overlap=True):
    NT = SEQ // TT
    NBLK = SEQ // 128
    nc = bass.Bass("TRN2", target_bir_lowering=False)

    def din(name, shape):
        return nc.dram_tensor(name, list(shape), F32, kind="ExternalInput").ap()

    x_d = din("x", [SEQ, D])
    cT_d = din("cT", [128, 8])
    ngT_d = din("normgT", [128, 2, 8])
    adaw_d = din("ada_w", [2, D, 3 * D])
    adab_d = din("ada_b", [1, 2 * 3 * D])
    win0_d = din("w_in0", [D, 6144])
    poolw_d = din("pool_w", [4, 256, 256])
    pscT_d = din("pscaleT", [128, 8])
    wout0_d = din("w_out0", [2048, D])
    win1_d = din("w_in1", [D, 8192])
    cwT_d = din("convwT", [128, 16, 3])
    cbT_d = din("convbT", [128, 16])
    wout1_d = din("w_out1", [2048, D])
    fg_d = din("final_g", [1, D])
    out_d = nc.dram_tensor("out", [SEQ, D], F32, kind="ExternalOutput").ap()
    kd = nc.dram_tensor("kd", [8, 128, SEQ], BF16).ap()
    vd = nc.dram_tensor("vd", [8, 128, NBLK, 128], BF16).ap()
    NPIECE = 37
    wc = nc.dram_tensor("wcache", [NPIECE, 128, 8, 512], BF16).ap()

    with contextlib.ExitStack() as stack:
        P = Prog(nc, stack)

        def sb(name, shape, dtype):
            return stack.enter_context(nc.sbuf_tensor("s_" + name, list(shape), dtype))

        psall = stack.enter_context(nc.psum_tensor("psall", [128, 8, 512], F32))
        ps = [psall[:, b, :] for b in range(8)]
        psk = [("ps", b) for b in range(8)]
        bank_rr = [0]

        def nbank():
            b = bank_rr[0]
            bank_rr[0] = (b + 1) % 8
            return b

        onesf = sb("onesf", [128, 128], F32)
        identf = sb("identf", [128, 128], F32)
        tmpf = sb("tmpf", [128, 128], F32)
        negf = sb("negf", [128, 128], F32)
        tri = sb("tri", [128, 128], BF16)
        onesb = sb("onesb", [128, 128], BF16)
        identb = sb("identb", [128, 128], BF16)
        maskb = sb("maskb", [128, 128], BF16)
        zerob = sb("zerob", [128, 128], BF16)
        invc = sb("invc", [128, 4, 16], F32)
        cT = sb("cT", [128, 8], F32)
        scb = sb("scb", [128, 8], BF16)
        ngT = sb("ngT", [128, 2, 8], F32)
        GT = sb("GT", [128, 2, 8], F32)
        shT = sb("shT", [128, 2, 8], F32)
        G1bc = sb("G1bc", [128, 2, D], F32)
        fgbc = sb("fgbc", [128, D], F32)
        pscT = sb("pscT", [128, 8], F32)
        cwT = sb("cwT", [128, 16, 3], F32)
        cbT = sb("cbT", [128, 16], F32)
        uh = sb("uh", [128, 8, 16], F32)
        uuh = sb("uuh", [128, 16, 2], F32)
        ss = sb("ss", [128, 8], F32)
        rstd = sb("rstd", [128, 8], F32)
        epsb = sb("epsb", [128, 1], F32)
        xt = [sb("xt%d" % j, [128, 4, D], F32) for j in range(2)]
        xn = sb("xn", [128, 4, D], F32)
        MSBK = [("xn", 0), ("xn", 1), ("xn", 2)]
        ADBK = [("xt", 1)]

        def msb_ap(c0, c1):
            b = c0 // D
            assert (c1 - 1) // D == b
            return xn[0:1, b, c0 - b * D:c1 - b * D]

        def adab_ap(c0, c1):
            b = c0 // D
            assert (c1 - 1) // D == b
            return xt[1][0:1, b, c0 - b * D:c1 - b * D]
        fgrow = xn[0:1, 3, :]
        hT = sb("hT", [128, 8, TT], BF16)
        qpad = sb("qpad", [128, 16, TT], BF16)
        pT = sb("pT", [128, 8, TT], BF16)
        yT = sb("yT", [128, 16, TT], BF16)
        kst = pT
        vst = sb("vst", [128, 4, D], BF16)
        kbuf = [sb("kbuf%d" % j, [128, SEQ], BF16) for j in range(2)]
        vbuf = [sb("vbuf%d" % j, [128, NBLK, 128], BF16) for j in range(2)]
        F2 = [sb("F2_%d" % j, [128, 2, 528], F32) for j in range(5)]
        Fs = [F2[j // 2][:, j % 2, :] for j in range(10)]
        SP2 = [sb("SP2_%d" % j, [128, 2, 512], BF16) for j in range(2)]
        A2 = [sb("A2_%d" % j, [128, 2, 512], BF16) for j in range(2)]
        otri = sb("otri", [128, 128], BF16)
        wsl = [sb("w%d" % j, [128, 8, 512], BF16) for j in range(NWSLOT)]
        wcnt = [0]

        def MM(out, lhsT, rhs, start, stop, reads, writes, sgc=False):
            if sgc:
                P.op("pe", lambda e: e.matmul(out, lhsT, rhs, start=start, stop=stop, skip_group_check=True),
                     reads, writes)
            else:
                P.op("pe", lambda e: e.matmul(out, lhsT, rhs, start=start, stop=stop), reads, writes)

        def TR(out, in_, reads, writes):
            P.op("pe", lambda e: e.transpose(out, in_, identf[:]), list(reads) + ["identf"], writes)

        def ACT(out, in_, func, reads, writes, bias=None, scale=None, accum=None):
            kw = {}
            if bias is not None:
                kw["bias"] = bias
            if scale is not None:
                kw["scale"] = scale
            if accum is not None:
                kw["accum_out"] = accum
            P.op("act", lambda e: e.activation(out, in_, func, **kw), reads, writes)

        def TTo(eng, out, in0, in1, op, reads, writes):
            P.op(eng, lambda e: e.tensor_tensor(out, in0, in1, op), reads, writes)

        def TS(eng, out, in0, s1, s2, op0, op1, reads, writes):
            if s2 is None:
                P.op(eng, lambda e: e.tensor_scalar(out, in0, s1, None, op0), reads, writes)
            else:
                P.op(eng, lambda e: e.tensor_scalar(out, in0, s1, s2, op0, op1), reads, writes)

        def STT(eng, out, in0, sc, in1, op0, op1, reads, writes):
            P.op(eng, lambda e: e.scalar_tensor_tensor(out, in0, sc, in1, op0, op1), reads, writes)

        def CP(eng, out, in_, reads, writes):
            if eng == "act":
                P.op("act", lambda e: e.activation(out, in_, AF.Copy), reads, writes)
            else:
                P.op(eng, lambda e: e.tensor_copy(out, in_), reads, writes)

        def MS(eng, ap, val, writes):
            P.op(eng, lambda e: e.memset(ap, val), (), writes)

        tctx = {"i": -1, "k": 0}

        def wload(dst_fn, src, piece=None, tile=None):
            s = wcnt[0] % NWSLOT
            wcnt[0] += 1
            dst = dst_fn(wsl[s])
            if piece is None:
                tile = tctx["i"]
                if tile >= 0:
                    piece = tctx["k"]
                    tctx["k"] += 1
            if piece is None or tile < 0 or not USE_WCACHE:
                P.op("pool", lambda e: e.dma_start(out=dst, in_=src), (), [("w", s)], dma_slot="w%d" % s)
                return s
            assert piece < NPIECE
            cv = dst_fn(wc[piece])
            if tile == 0:
                P.op("pool", lambda e: e.dma_start(out=dst, in_=src), (), [("w", s)], dma_slot="w%d" % s)
                P.op("sp", lambda e: e.dma_start(out=cv, in_=dst), [("w", s)], [("wc", piece)], dma_slot="wst%d" % s)
            else:
                P.op("pool", lambda e: e.dma_start(out=dst, in_=cv), [("wc", piece)], [("w", s)], dma_slot="w%d" % s)
            return s

        def wload_std(wd, col0, piece=None, tile=None):
            src = wd[:, col0:col0 + 512].rearrange("(kc p) n -> p kc n", p=128)
            return wload(lambda w: w[:, :, :], src, piece, tile)

        MS("pool", onesf[:], 1.0, ["onesf"])
        MS("pool", epsb[:], EPS, ["epsb"])
        MS("pool", negf[:], NEG, ["negf"])
        MS("pool", zerob[:], 0.0, ["zerob"])
        MS("pool", onesb[:], 1.0, ["onesb"])
        MS("pool", qpad[:], 0.0, ["qpad"])
        MS("pool", uh[:], 0.0, ["uh"])
        MS("pool", uuh[:], 0.0, ["uuh"])
        P.op("pool", lambda e: e.affine_select(out=identf[:], in_=onesf[:], pattern=[[-1, 128]],
                                               compare_op=ALU.is_equal, fill=0.0, base=0,
                                               channel_multiplier=1), ["onesf"], ["identf"])
        P.op("pool", lambda e: e.affine_select(out=tmpf[:], in_=onesf[:], pattern=[[-1, 128]],
                                               compare_op=ALU.is_ge, fill=0.0, base=0,
                                               channel_multiplier=1), ["onesf"], ["tmpf"])
        CP("dve", tri[:], tmpf[:], ["tmpf"], ["tri"])
        TS("dve", otri[:], tmpf[:], -1.0, 1.0, ALU.mult, ALU.add, ["tmpf"], ["otri"])
        CP("dve", identb[:], identf[:], ["identf"], ["identb"])
        P.op("pool", lambda e: e.affine_select(out=tmpf[:], in_=negf[:], pattern=[[-1, 128]],
                                               compare_op=ALU.is_ge, fill=0.0, base=0,
                                               channel_multiplier=1), ["negf"], ["tmpf"])
        CP("dve", maskb[:], tmpf[:], ["tmpf"], ["maskb"])
        for g, w in enumerate(WINS):
            for t in range(w - 1):
                MS("dve", invc[:, g, t:t + 1], 1.0 / (t + 1), ["invc"])
            MS("dve", invc[:, g, w - 1:16], 1.0 / w, ["invc"])

        def const_loads(e):
            return [
                e.dma_start(out=cT[:], in_=cT_d),
                e.dma_start(out=ngT[:], in_=ngT_d),
                e.dma_start(out=pscT[:], in_=pscT_d),
                e.dma_start(out=cwT[:], in_=cwT_d),
                e.dma_start(out=cbT[:], in_=cbT_d),
                e.dma_start(out=fgrow, in_=fg_d),
            ]
        P.op("sp", const_loads, (), ["cT", "ngT", "pscT", "cwT", "cbT", ("xn", 3)],
             dma_slot="const", ndma=6)

        def xload(i):
            j = i % 2
            src = x_d[i * TT:(i + 1) * TT, :].rearrange("(b p) d -> p b d", p=128)
            P.op("sp", lambda e: e.dma_start(out=xt[j][:], in_=src), (), [("xt", j)], dma_slot="x%d" % j)

        xload(0)

        ACT(scb[:], cT[:], AF.Silu, ["cT"], ["scb"])
        for l in range(2):
            if (l == 0 and not do0) or (l == 1 and not do1):
                continue
            P.op("sp", lambda e, l=l: e.dma_start(
                out=xt[1][0:1, 0:3, :], in_=adab_d[0:1, l * 3072:(l + 1) * 3072].rearrange("o (b n) -> o b n", n=D)),
                (), ADBK, dma_slot="ab")
            for kc in range(8):
                src = adaw_d[l, kc * 128:(kc + 1) * 128, :].rearrange("p (g n) -> p g n", n=512)
                s = wload(lambda w: w[:, 0:6, :], src)
                for g in range(6):
                    MM(ps[g][0:1, :], scb[:, kc:kc + 1], wsl[s][:, g, :], kc == 0, kc == 7,
                       ["scb", ("w", s)], [psk[g]])
            for g in range(6):
                TTo("dve", msb_ap(g * 512, (g + 1) * 512), ps[g][0:1, :], adab_ap(g * 512, (g + 1) * 512), ALU.add,
                    [psk[g]] + ADBK, MSBK)
            for b_ in (1, 2):
                TS("dve", xn[0:1, b_, :], xn[0:1, b_, :], 1.0, None, ALU.add, None, MSBK, MSBK)
            for h in range(2):
                MM(ps[6][:, :], onesf[0:1, 0:128], msb_ap(2048 + h * 512, 2048 + (h + 1) * 512),
                   True, True, ["onesf"] + MSBK, [psk[6]])
                CP("act", G1bc[:, l, h * 512:(h + 1) * 512], ps[6][:, :], [psk[6]], [("G1bc", l)])
            for j in range(16):
                MM(ps[7][:, j:j + 1], msb_ap(j * 128, (j + 1) * 128), onesf[0:1, 0:1], True, True,
                   MSBK + ["onesf"], [psk[7]])
            CP("dve", shT[:, l, :], ps[7][:, 0:8], [psk[7]], [("shT", l)])
            TTo("dve", GT[:, l, :], ps[7][:, 8:16], ngT[:, l, :], ALU.mult, [psk[7], "ngT"], [("GT", l)])
        if final:
            for h in range(2):
                MM(ps[6][:, :], onesf[0:1, 0:128], fgrow[:, h * 512:(h + 1) * 512], True, True,
                   ["onesf", ("xn", 3)], [psk[6]])
                CP("act", fgbc[:, h * 512:(h + 1) * 512], ps[6][:, :], [psk[6]], ["fgbc"])

        def stage_norm(i, l, X):
            xk = ("xt", i % 2)
            for blk in range(4):
                ACT(xn[:, blk, :], X[:, blk, :], AF.Square, [xk], [("xn", blk), ("ss", blk)],
                    accum=ss[:, blk:blk + 1])
                ACT(rstd[:, blk:blk + 1], ss[:, blk:blk + 1], AF.Ln, [("ss", blk), "epsb"], [("rstd", blk)],
                    bias=epsb[:, 0:1], scale=1.0 / D)
                ACT(rstd[:, blk:blk + 1], rstd[:, blk:blk + 1], AF.Exp, [("rstd", blk)], [("rstd", blk)],
                    scale=-0.5)
                TS("dve", xn[:, blk, :], X[:, blk, :], rstd[:, blk:blk + 1], None, ALU.mult, None,
                   [xk, ("rstd", blk)], [("xn", blk)])
            for kc in range(8):
                for blk in range(4):
                    TR(ps[kc][:, blk * 128:(blk + 1) * 128], xn[:, blk, kc * 128:(kc + 1) * 128],
                       [("xn", blk)], [psk[kc]])
            for kc in range(8):
                ACT(hT[:, kc, :], ps[kc][:, :], AF.Identity, [psk[kc], ("GT", l), ("shT", l)], [("hT", kc)],
                    bias=shT[:, l, kc:kc + 1], scale=GT[:, l, kc:kc + 1])

        def proj_fm(s, c, b):
            for kc in range(8):
                MM(ps[b][:, :], wsl[s][:, kc, c * 128:(c + 1) * 128], hT[:, kc, :], kc == 0, kc == 7,
                   [("w", s), ("hT", kc)], [psk[b]])

        def stage_outproj(i, l, X, wd):
            xk = ("xt", i % 2)
            for half in range(2):
                banks = [nbank() for _ in range(4)]
                for j in range(2):
                    src = wd[j * 1024:(j + 1) * 1024, half * 512:(half + 1) * 512].rearrange(
                        "(kc p) n -> p kc n", p=128)
                    s = wload(lambda w: w[:, :, :], src)
                    for blk in range(4):
                        for kc in range(8):
                            ch = j * 8 + kc
                            MM(ps[banks[blk]][:, :], yT[:, ch, blk * 128:(blk + 1) * 128], wsl[s][:, kc, :],
                               j == 0 and kc == 0, j == 1 and kc == 7,
                               [("yT", ch), ("w", s)], [psk[banks[blk]]])
                for blk in range(4):
                    f = 6 + (blk % 2)
                    TTo("dve", Fs[f][:, 0:512], ps[banks[blk]][:, :], G1bc[:, l, half * 512:(half + 1) * 512],
                        ALU.mult, [psk[banks[blk]], ("G1bc", l)], [("F", f)])
                    TTo("dve", X[:, blk, half * 512:(half + 1) * 512], X[:, blk, half * 512:(half + 1) * 512],
                        Fs[f][:, 0:512], ALU.add, [("F", f), xk], [xk])

        BGB = [6, 7]
        bgb_ctr = [0]
        bgmode = {"alone": False}

        def bg_bank():
            b = BGB[bgb_ctr[0] % 2]
            bgb_ctr[0] += 1
            return b

        def run_all(units):
            for u in units:
                for _ in u():
                    pass

        def y4ap(yb, c, a=0, b=512):
            return vst[:, 2 * yb + c // 2, (c % 2) * 512 + a:(c % 2) * 512 + b]

        def norm_units_l1(i):
            X = xt[i % 2]
            xk = ("xt", i % 2)
            units = []
            for blk in range(4):
                for half in range(2):
                    def u(blk=blk, half=half):
                        if half == 0:
                            ACT(xn[:, blk, :], X[:, blk, :], AF.Square, [xk], [("xn", blk), ("ss", blk)],
                                accum=ss[:, blk:blk + 1])
                            ACT(rstd[:, blk:blk + 1], ss[:, blk:blk + 1], AF.Ln, [("ss", blk), "epsb"],
                                [("rstd", blk)], bias=epsb[:, 0:1], scale=1.0 / D)
                            ACT(rstd[:, blk:blk + 1], rstd[:, blk:blk + 1], AF.Exp, [("rstd", blk)],
                                [("rstd", blk)], scale=-0.5)
                            TS("dve", xn[:, blk, :], X[:, blk, :], rstd[:, blk:blk + 1], None, ALU.mult, None,
                               [xk, ("rstd", blk)], [("xn", blk)])
                            yield
                        b = bg_bank()
                        for k4 in range(4):
                            kc = half * 4 + k4
                            TR(ps[b][:, k4 * 128:(k4 + 1) * 128], xn[:, blk, kc * 128:(kc + 1) * 128],
                               [("xn", blk)], [psk[b]])
                            if k4 == 1:
                                yield
                        yield
                        for k4 in range(4):
                            kc = half * 4 + k4
                            if bgmode["alone"]:
                                ACT(hT[:, kc, blk * 128:(blk + 1) * 128], ps[b][:, k4 * 128:(k4 + 1) * 128],
                                    AF.Identity, [psk[b], ("GT", 1), ("shT", 1)], [("hT", kc)],
                                    bias=shT[:, 1, kc:kc + 1], scale=GT[:, 1, kc:kc + 1])
                            else:
                                TS("dve", hT[:, kc, blk * 128:(blk + 1) * 128], ps[b][:, k4 * 128:(k4 + 1) * 128],
                                   GT[:, 1, kc:kc + 1], shT[:, 1, kc:kc + 1], ALU.mult, ALU.add,
                                   [psk[b], ("GT", 1), ("shT", 1)], [("hT", kc)])
                    units.append(u)
            return units

        def proj_steps(s, c, b):
            for kc in range(8):
                MM(ps[b][:, :], wsl[s][:, kc, c * 128:(c + 1) * 128], hT[:, kc, :], kc == 0, kc == 7,
                   [("w", s), ("hT", kc)], [psk[b]])
                if kc % 2 == 1 and kc < 7:
                    yield

        def layer1_units(i):
            X = xt[i % 2]
            xk = ("xt", i % 2)
            units = list(norm_units_l1(i))
            first_norm = units[0]

            def kick():
                need(0)
                yield from first_norm()
            units[0] = kick
            USB = [xn[:, 0, 0:512], xn[:, 0, 512:1024], xn[:, 1, 0:512], xn[:, 1, 512:1024]]
            USK = [("xn", 0), ("xn", 0), ("xn", 1), ("xn", 1)]
            UU, UK = xn[:, 2, 0:514], ("xn", 2)
            T1, TK = xn[:, 3, 0:512], ("xn", 3)
            T2 = xn[:, 3, 512:1024]
            PB = 17
            slots = {}

            def do_load(k):
                q, t = k // 5, k % 5
                if t < 4:
                    col = [(12 + q), (8 + q), (4 + q), q][t] * 512
                    return wload_std(win1_d, col, PB + k, i)
                src = wout1_d[q * 512:(q + 1) * 512, :].rearrange("(kc p) n -> p kc n", p=128)
                return wload(lambda w: w[:, :, :].rearrange("p a b -> p (a b)").rearrange(
                    "p (a b) -> p a b", b=1024), src, PB + k, i)

            def need(k):
                for kk in (k, k + 1):
                    if kk < 20 and kk not in slots:
                        slots[kk] = do_load(kk)
                return slots[k]

            for q in range(4):
                yb = q % 2
                hg, hu, hc, hb, ho = [None], [None], [None], [None], [None]
                for c in range(4):
                    def u(c=c, hg=hg, yb=yb, q=q):
                        if c == 0:
                            hg[0] = need(q * 5 + 0)
                        b = bg_bank()
                        yield from proj_steps(hg[0], c, b)
                        yield
                        if bgmode["alone"]:
                            ACT(y4ap(yb, c), ps[b][:, :], AF.Silu, [psk[b]], ["vst"])
                        else:
                            f = T1 if c % 2 == 0 else T2
                            ACT(f, ps[b][:, :], AF.Exp, [psk[b]], [TK], scale=-1.0)
                            TS("dve", f, f, 1.0, None, ALU.add, None, [TK], [TK])
                            P.op("dve", lambda e, o=f: e.reciprocal(o, o), [TK], [TK])
                            TTo("dve", y4ap(yb, c), ps[b][:, :], f, ALU.mult, [psk[b], TK], ["vst"])
                    units.append(u)
                for c in range(4):
                    def u(c=c, hu=hu, q=q):
                        if c == 0:
                            hu[0] = need(q * 5 + 1)
                        b = bg_bank()
                        yield from proj_steps(hu[0], c, b)
                        yield
                        CP("act" if bgmode["alone"] else "dve", USB[c], ps[b][:, :], [psk[b]], [USK[c]])
                    units.append(u)
                for c in range(4):
                    def u(c=c, hc=hc, q=q):
                        if c == 0:
                            hc[0] = need(q * 5 + 2)
                        ch = q * 4 + c
                        b = bg_bank()
                        yield from proj_steps(hc[0], c, b)
                        yield
                        CP("dve", UU[:, 0:2], uuh[:, ch, :], [("uuh", ch)], [UK])
                        TTo("dve", UU[:, 2:514], ps[b][:, :], USB[c], ALU.mult, [psk[b], USK[c]], [UK])
                        CP("dve", uuh[:, ch, :], UU[:, 512:514], [UK], [("uuh", ch)])
                        TS("dve", T1, UU[:, 2:514], cwT[:, ch, 2:3], cbT[:, ch:ch + 1],
                           ALU.mult, ALU.add, [UK, "cwT", "cbT"], [TK])
                        STT("dve", T1, UU[:, 1:513], cwT[:, ch, 1:2], T1, ALU.mult, ALU.add, [UK, "cwT", TK], [TK])
                        STT("dve", USB[c], UU[:, 0:512], cwT[:, ch, 0:1], T1, ALU.mult, ALU.add,
                            [UK, "cwT", TK], [USK[c]])
                    units.append(u)
                for c in range(4):
                    def u(c=c, hb=hb, yb=yb, q=q):
                        if c == 0:
                            hb[0] = need(q * 5 + 3)
                        b = bg_bank()
                        yield from proj_steps(hb[0], c, b)
                        yield
                        TTo("dve", T1, ps[b][:, :], USB[c], ALU.mult, [psk[b], USK[c]], [TK])
                        TTo("dve", y4ap(yb, c), T1, y4ap(yb, c), ALU.mult, [TK, "vst"], ["vst"])
                    units.append(u)
                for half in range(2):
                    for blk in range(4):
                        def u(half=half, blk=blk, ho=ho, yb=yb, q=q):
                            if half == 0 and blk == 0:
                                ho[0] = need(q * 5 + 4)
                            b = bg_bank()
                            wv = wsl[ho[0]][:, :, :].rearrange("p a b -> p (a b)").rearrange("p (a b) -> p a b", b=1024)
                            for c in range(4):
                                MM(ps[b][:, :], y4ap(yb, c, blk * 128, (blk + 1) * 128),
                                   wv[:, c, half * 512:(half + 1) * 512], c == 0, c == 3,
                                   ["vst", ("w", ho[0])], [psk[b]])
                                if c == 1:
                                    yield
                            yield
                            f = T1 if blk % 2 == 0 else T2
                            TTo("dve", f, ps[b][:, :], G1bc[:, 1, half * 512:(half + 1) * 512],
                                ALU.mult, [psk[b], ("G1bc", 1)], [TK])
                            TTo("dve", X[:, blk, half * 512:(half + 1) * 512], X[:, blk, half * 512:(half + 1) * 512],
                                f, ALU.add, [TK, xk], [xk])
                        units.append(u)
            return units

        def finish_units(i):
            def mk(blk):
                def u():
                    finish_blk(i, blk)
                    return
                    yield
                return u
            return [mk(blk) for blk in range(4)]

        def layer0(i, bg=()):
            X = xt[i % 2]
            tok0 = i * TT
            stage_norm(i, 0, X)
            for g in range(2):
                s = wload_std(win0_d, (4 + g) * 512)
                for c in range(4):
                    hp = g * 4 + c
                    b = nbank()
                    proj_fm(s, c, b)
                    CP("act" if c % 2 == 0 else "dve", kst[:, hp, :], ps[b][:, :], [psk[b]], [("pT", hp)])
            P.op("sp", lambda e: e.dma_start(out=kd[:, :, tok0:tok0 + TT].rearrange("h p t -> p h t"),
                                             in_=kst[:]), [("pT", c_) for c_ in range(8)], [("kd", i)], dma_slot="ks")
            for vh in range(2):
                s = wload_std(win0_d, (6 + vh) * 512)
                for blk in range(4):
                    b = nbank()
                    for kc in range(8):
                        MM(ps[b][:, :], hT[:, kc, blk * 128:(blk + 1) * 128], wsl[s][:, kc, :], kc == 0, kc == 7,
                           [("hT", kc), ("w", s)], [psk[b]])
                    CP("act" if blk % 2 == 0 else "dve", vst[:, blk, vh * 512:(vh + 1) * 512], ps[b][:, :],
                       [psk[b]], ["vst"])

            def vstore(e):
                return [e.dma_start(out=vd[hp, :, 4 * i:4 * i + 4, :], in_=vst[:, :, hp * 128:(hp + 1) * 128])
                        for hp in range(8)]
            P.op("sp", vstore, ["vst"], [("vd", i)], dma_slot="vs", ndma=8)
            for g in range(4):
                s = wload_std(win0_d, (8 + g) * 512)
                for c in range(4):
                    ch = g * 4 + c
                    b = nbank()
                    proj_fm(s, c, b)
                    ACT(yT[:, ch, :], ps[b][:, :], AF.Silu, [psk[b]], [("yT", ch)])
            for g in range(2):
                s = wload_std(win0_d, (2 + g) * 512)
                for c in range(4):
                    hp = g * 4 + c
                    b = nbank()
                    proj_fm(s, c, b)
                    TS("dve", qpad[0:64, 2 * hp, :], ps[b][0:64, :], 0.125, None, ALU.mult, None,
                       [psk[b]], [("qp", 2 * hp)])
                    P.op("act", lambda e, o=qpad[64:128, 2 * hp + 1, :], n=ps[b][64:128, :]:
                         e.mul(o, n, 0.125), [psk[b]], [("qp", 2 * hp + 1)])
            for g2 in range(2):
                s = wload_std(win0_d, g2 * 512)
                for c4 in range(4):
                    c = g2 * 4 + c4
                    grp = c // 2
                    w = WINS[grp]
                    b = nbank()
                    proj_fm(s, c4, b)
                    U = Fs[c % 2]
                    uk = ("F", c % 2)
                    CP("dve", U[:, 0:16], uh[:, c, :], [("uh", c)], [uk])
                    CP("act", U[:, 16:528], ps[b][:, :], [psk[b]], [uk])
                    CP("dve", uh[:, c, :], U[:, 512:528], [uk], [("uh", c)])
                    ta, tb = (2, 3) if c % 2 == 0 else (4, 5)
                    TTo("dve", Fs[ta][:, 1:528], U[:, 1:528], U[:, 0:527], ALU.add, [uk], [("F", ta)])
                    cur = ta
                    sh = 2
                    while sh < w:
                        oth = tb if cur == ta else ta
                        TTo("dve", Fs[oth][:, 2 * sh - 1:528], Fs[cur][:, 2 * sh - 1:528],
                            Fs[cur][:, sh - 1:528 - sh], ALU.add, [("F", cur)], [("F", oth)])
                        cur = oth
                        sh *= 2
                    STT("dve", pT[:, c, :], Fs[cur][:, 16:528], 1.0 / w, U[:, 16:528], ALU.mult, ALU.subtract,
                        [("F", cur), uk], [("pT", c)])
                    if i == 0:
                        oth = tb if cur == ta else ta
                        TTo("dve", Fs[oth][:, 0:16], Fs[cur][:, 16:32], invc[:, grp, :], ALU.mult,
                            [("F", cur), "invc"], [("F", oth)])
                        TTo("dve", pT[:, c, 0:16], Fs[oth][:, 0:16], U[:, 16:32], ALU.subtract,
                            [("F", oth), uk], [("pT", c)])
            src = poolw_d.rearrange("g (cc p) d -> p (g cc) d", p=128)
            s = wload(lambda w: w[:, :, 0:256], src)
            for dch in range(8):
                grp, dd = dch // 2, dch % 2
                b = nbank()
                for cc in range(2):
                    MM(ps[b][:, :], wsl[s][:, grp * 2 + cc, dd * 128:(dd + 1) * 128], pT[:, grp * 2 + cc, :],
                       cc == 0, cc == 1, [("w", s), ("pT", grp * 2 + cc)], [psk[b]])
                STT("dve", yT[:, dch, :], ps[b][:, :], pscT[:, dch:dch + 1], yT[:, dch, :], ALU.mult, ALU.mult,
                    [psk[b], "pscT", ("yT", dch)], [("yT", dch)])
            ntok = (i + 1) * TT
            nblk = 4 * (i + 1)

            def kvload(hp):
                j = hp % 2
                rk = [("kd", t) for t in range(i + 1)]
                rv = [("vd", t) for t in range(i + 1)]
                P.op("sp", lambda e: e.dma_start(out=kbuf[j][:, 0:ntok], in_=kd[hp, :, 0:ntok]), rk,
                     [("kb", j)], dma_slot="kb%d" % j)
                P.op("sp", lambda e: e.dma_start(out=vbuf[j][:, 0:nblk, :], in_=vd[hp, :, 0:nblk, :]), rv,
                     [("vb", j)], dma_slot="vb%d" % j)

            kvload(0)
            if overlap:
                ZP = [0, 0]
                CBK = [2, 3]
                accb = [4, 5]
            else:
                ZP = [0, 2]
                CBK = [4, 5]
                accb = [6, 7]
            bg = list(bg)
            bgs = {"cur": None, "idx": 0, "steps": 0, "slot": 0}
            BG_STEPS_EST = 448

            def bg_step():
                while True:
                    if bgs["cur"] is None:
                        if bgs["idx"] >= len(bg):
                            return False
                        bgs["cur"] = bg[bgs["idx"]]()
                        bgs["idx"] += 1
                    try:
                        next(bgs["cur"])
                    except StopIteration:
                        bgs["cur"] = None
                    bgs["steps"] += 1
                    return True

            def run_bg(sub):
                if not bg:
                    return
                tgt = (BG_STEPS_EST * (3 * bgs["slot"] + sub + 1) + 3 * (8 * nblk + 2) - 1) // (3 * (8 * nblk + 2))
                if bgs["steps"] < tgt:
                    bg_step()

            def flush_bg():
                bgmode["alone"] = True
                while bg_step():
                    pass
                bgmode["alone"] = False
            items = [(hp, kb) for hp in range(8) for kb in range(nblk - 1, -1, -1)]
            n = len(items)
            st = {}

            def fk(w):
                return [("F", 2 * w), ("F", 2 * w + 1)]

            def A_pe(p):
                hp, kb = items[p]
                j = hp % 2
                r = kb - 4 * i
                col0 = r * 128 if r >= 0 else 0
                st[p] = (hp, kb, r, col0)
                z0 = ZP[p % 2]
                for hd in range(2):
                    zb = z0 + hd
                    MM(ps[zb][:, col0:512], kbuf[j][:, kb * 128:(kb + 1) * 128], qpad[:, 2 * hp + hd, col0:512],
                       True, r < 0, [("kb", j), ("qp", 2 * hp + hd)], [psk[zb]])
                    if r >= 0:
                        MM(ps[zb][:, col0:col0 + 128], identb[:, :], maskb[:, :], False, True,
                           ["identb", "maskb"], [psk[zb]])

            def EXP1(p):
                hp, kb, r, col0 = st[p]
                z0 = ZP[p % 2]
                e_i = p % 3
                ACT(F2[e_i][:, :, col0:512], psall[:, z0:z0 + 2, col0:512], AF.Exp,
                    [psk[z0], psk[z0 + 1]], fk(e_i))

            def LN(p):
                hp, kb, r, col0 = st[p]
                e_i = p % 3
                ACT(SP2[p % 2][:, :, col0:512], F2[e_i][:, :, col0:512], AF.Ln, fk(e_i), [("SP2", p % 2)],
                    bias=1.0)

            def T_pe(p):
                hp, kb, r, col0 = st[p]
                if kb == nblk - 1:
                    for hd in range(2):
                        MM(ps[CBK[hd]][:, :], zerob[:, :], hT[:, 0, :], True, False,
                           ["zerob", ("hT", 0)], [psk[CBK[hd]]], sgc=True)
                for hd in range(2):
                    MM(ps[CBK[hd]][:, col0:512], tri[:, :], SP2[p % 2][:, hd, col0:512], False, False,
                       ["tri", ("SP2", p % 2)], [psk[CBK[hd]]], sgc=True)

            def O_pe(p):
                hp, kb, r, col0 = st[p]
                if kb > 0:
                    for hd in range(2):
                        MM(ps[CBK[hd]][:, col0:512], otri[:, :], SP2[p % 2][:, hd, col0:512], False, False,
                           ["otri", ("SP2", p % 2)], [psk[CBK[hd]]], sgc=True)

            def EXP2_F(p):
                hp, kb, r, col0 = st[p]
                e_i = p % 3
                w_i = 3 + p % 2
                ACT(F2[w_i][:, :, col0:512], psall[:, CBK[0]:CBK[0] + 2, col0:512], AF.Exp,
                    [psk[CBK[0]], psk[CBK[1]]], fk(w_i), scale=-1.0)
                TTo("dve", A2[p % 2][:, :, col0:512], F2[e_i][:, :, col0:512], F2[w_i][:, :, col0:512], ALU.mult,
                    fk(e_i) + fk(w_i), [("A2", p % 2)])

            def G_pe(p):
                hp, kb, r, col0 = st[p]
                j = hp % 2
                if kb == nblk - 1:
                    if hp + 1 < 8:
                        kvload(hp + 1)
                    for hd in range(2):
                        MM(ps[accb[hd]][:, :], zerob[:, :], hT[:, 0, :], True, False,
                           ["zerob", ("hT", 0)], [psk[accb[hd]]])
                for hd in range(2):
                    MM(ps[accb[hd]][:, col0:512], vbuf[j][:, kb, :], A2[p % 2][:, hd, col0:512], False, kb == 0,
                       [("vb", j), ("A2", p % 2)], [psk[accb[hd]]])

            def Y_evac(p):
                hp, kb, r, col0 = st[p]
                if kb == 0:
                    for hd in range(2):
                        pr = slice(hd * 64, (hd + 1) * 64)
                        TTo("dve", yT[pr, 8 + hp, :], ps[accb[hd]][pr, :], yT[pr, 8 + hp, :], ALU.mult,
                            [psk[accb[hd]], ("yT", 8 + hp)], [("yT", 8 + hp)])

            A_pe(0)
            for p in range(n + 2):
                bgs["slot"] = p
                if overlap:
                    P.group_begin("pe")
                    if 0 <= p - 1 < n:
                        T_pe(p - 1)
                    P.group_end("pe")
                    run_bg(0)
                    if p < n:
                        EXP1(p)
                    P.group_begin("pe")
                    if p + 1 < n:
                        A_pe(p + 1)
                    P.group_end("pe")
                    run_bg(1)
                else:
                    P.group_begin("pe")
                    if 0 <= p - 1 < n:
                        T_pe(p - 1)
                    if p + 1 < n:
                        A_pe(p + 1)
                    P.group_end("pe")
                    if p < n:
                        EXP1(p)
                if 0 <= p - 1 < n:
                    EXP2_F(p - 1)
                if p < n:
                    LN(p)
                P.group_begin("pe")
                if 0 <= p - 2 < n:
                    G_pe(p - 2)
                if 0 <= p - 1 < n:
                    O_pe(p - 1)
                P.group_end("pe")
                if 0 <= p - 2 < n:
                    Y_evac(p - 2)
                if overlap:
                    run_bg(2)
            flush_bg()
            stage_outproj(i, 0, X, wout0_d)

        def layer1(i):
            X = xt[i % 2]
            stage_norm(i, 1, X)
            for q in range(4):
                s = wload_std(win1_d, (12 + q) * 512)
                for c in range(4):
                    ch = q * 4 + c
                    b = nbank()
                    proj_fm(s, c, b)
                    ACT(yT[:, ch, :], ps[b][:, :], AF.Silu, [psk[b]], [("yT", ch)])
                s = wload_std(win1_d, (8 + q) * 512)
                for c in range(4):
                    b = nbank()
                    proj_fm(s, c, b)
                    CP("act", Fs[c][:, 0:512], ps[b][:, :], [psk[b]], [("F", c)])
                s = wload_std(win1_d, (4 + q) * 512)
                for c in range(4):
                    ch = q * 4 + c
                    b = nbank()
                    proj_fm(s, c, b)
                    UU = Fs[4 + c % 2]
                    uk = ("F", 4 + c % 2)
                    CP("dve", UU[:, 0:2], uuh[:, ch, :], [("uuh", ch)], [uk])
                    TTo("dve", UU[:, 2:514], ps[b][:, :], Fs[c][:, 0:512], ALU.mult, [psk[b], ("F", c)], [uk])
                    CP("dve", uuh[:, ch, :], UU[:, 512:514], [uk], [("uuh", ch)])
                    t1 = 6 + c % 2
                    TS("dve", Fs[t1][:, 0:512], UU[:, 2:514], cwT[:, ch, 2:3], cbT[:, ch:ch + 1], ALU.mult, ALU.add,
                       [uk, "cwT", "cbT"], [("F", t1)])
                    STT("dve", Fs[t1][:, 0:512], UU[:, 1:513], cwT[:, ch, 1:2], Fs[t1][:, 0:512], ALU.mult, ALU.add,
                        [uk, "cwT", ("F", t1)], [("F", t1)])
                    STT("dve", Fs[c][:, 0:512], UU[:, 0:512], cwT[:, ch, 0:1], Fs[t1][:, 0:512], ALU.mult, ALU.add,
                        [uk, "cwT", ("F", t1)], [("F", c)])
                s = wload_std(win1_d, q * 512)
                for c in range(4):
                    ch = q * 4 + c
                    b = nbank()
                    proj_fm(s, c, b)
                    t1 = 6 + c % 2
                    TTo("dve", Fs[t1][:, 0:512], ps[b][:, :], Fs[c][:, 0:512], ALU.mult, [psk[b], ("F", c)],
                        [("F", t1)])
                    TTo("dve", yT[:, ch, :], Fs[t1][:, 0:512], yT[:, ch, :], ALU.mult, [("F", t1), ("yT", ch)],
                        [("yT", ch)])
            stage_outproj(i, 1, X, wout1_d)

        def finish_blk(i, blk):
            X = xt[i % 2]
            xk = ("xt", i % 2)
            if final:
                ACT(xn[:, blk, :], X[:, blk, :], AF.Square, [xk], [("xn", blk), ("ss", blk)],
                    accum=ss[:, blk:blk + 1])
                ACT(rstd[:, blk:blk + 1], ss[:, blk:blk + 1], AF.Ln, [("ss", blk), "epsb"], [("rstd", blk)],
                    bias=epsb[:, 0:1], scale=1.0 / D)
                ACT(rstd[:, blk:blk + 1], rstd[:, blk:blk + 1], AF.Exp, [("rstd", blk)], [("rstd", blk)],
                    scale=-0.5)
                STT("dve", xn[:, blk, :], X[:, blk, :], rstd[:, blk:blk + 1], fgbc[:, :], ALU.mult, ALU.mult,
                    [xk, ("rstd", blk), "fgbc"], [("xn", blk)])
            else:
                CP("dve", xn[:, blk, :], X[:, blk, :], [xk], [("xn", blk)])
            r0 = i * TT + blk * 128
            P.op("sp", lambda e, r0=r0, blk=blk: e.dma_start(out=out_d[r0:r0 + 128, :], in_=xn[:, blk, :]),
                 [("xn", blk)], [("out", i, blk)], dma_slot="o%d" % blk)

        def finish(i):
            for blk in range(4):
                finish_blk(i, blk)

        pending = []
        for i in range(NT):
            tctx["i"] = i
            tctx["k"] = 0
            if not (overlap and do0 and do1):
                if i + 1 < NT:
                    xload(i + 1)
                if do0:
                    layer0(i)
                if do1:
                    layer1(i)
                finish(i)
                continue
            layer0(i, pending)
            if i + 1 < NT:
                xload(i + 1)
            pending = layer1_units(i) + finish_units(i)
        if pending:
            bgmode["alone"] = True
            run_all(pending)

        P.emit()
    return nc


_NC_CACHE = {}


def _layout_inputs(x, c, norm_g, ada_w, ada_b, even_w_in, pool_w, pool_scale, even_w_out,
                   odd_w_in, conv_w, conv_b, odd_w_out, final_g):
    f = lambda a: np.ascontiguousarray(np.asarray(a, dtype=np.float32))
    shared = {
        "normgT": f(np.asarray(norm_g).reshape(2, 8, 128).transpose(2, 0, 1)),
        "ada_w": f(ada_w),
        "ada_b": f(np.asarray(ada_b).reshape(1, -1)),
        "w_in0": f(np.asarray(even_w_in)[0]),
        "pool_w": f(np.asarray(pool_w)[0]),
        "pscaleT": f(np.asarray(pool_scale)[0].reshape(8, 128).T),
        "w_out0": f(np.asarray(even_w_out)[0]),
        "w_in1": f(np.asarray(odd_w_in)[0]),
        "convwT": f(np.asarray(conv_w)[0].reshape(3, 16, 128).transpose(2, 1, 0)),
        "convbT": f(np.asarray(conv_b)[0].reshape(16, 128).T),
        "w_out1": f(np.asarray(odd_w_out)[0]),
        "final_g": f(np.asarray(final_g).reshape(1, -1)),
    }
    x = np.asarray(x)
    c = np.asarray(c)
    maps = []
    for b in range(x.shape[0]):
        m = dict(shared)
        m["x"] = f(x[b])
        m["cT"] = f(c[b].reshape(8, 128).T)
        maps.append(m)
    return maps


def kernel(x, c, norm_g, ada_w, ada_b, even_w_in, pool_w, pool_scale, even_w_out,
           odd_w_in, conv_w, conv_b, odd_w_out, final_g):
    x = np.asarray(x)
    B, S, _ = x.shape
    maps = _layout_inputs(x, c, norm_g, ada_w, ada_b, even_w_in, pool_w, pool_scale, even_w_out,
                          odd_w_in, conv_w, conv_b, odd_w_out, final_g)
    if S not in _NC_CACHE:
        _NC_CACHE[S] = build_nc(S)
    nc = _NC_CACHE[S]
    res = run_bass_kernel_spmd(nc, maps, core_ids=list(range(B)))
    return np.stack([np.asarray(r["out"], dtype=np.float32) for r in res.results], axis=0)
```

```python
import contextlib
import numpy as np
import concourse.bass as bass
import concourse.mybir as mybir
from concourse.bass_utils import run_bass_kernel_spmd

F32 = mybir.dt.float32
BF16 = mybir.dt.bfloat16
AF = mybir.ActivationFunctionType
ALU = mybir.AluOpType

D = 1024
TT = 512
NEG = -30000.0
EPS = 1e-6
WINS = (2, 4, 8, 16)
NWSLOT = 3
USE_WCACHE = True


class Op:
    __slots__ = ("eng", "fn", "deps", "dma", "sem", "sigval", "needs_sig", "pos", "grp")


class Prog:
    ENGS = ("pe", "act", "dve", "pool", "sp")

    def __init__(self, nc, stack):
        self.nc = nc
        self.stack = stack
        self.ops = {e: [] for e in self.ENGS}
        self.lastw = {}
        self.readers = {}
        self.esem = {e: stack.enter_context(nc.semaphore("S_" + e)) for e in self.ENGS}
        self.dsem = {}
        self.dcount = {}
        self.cur_grp = {e: None for e in self.ENGS}
        self.grp_ctr = 0

    def group_begin(self, eng):
        self.grp_ctr += 1
        self.cur_grp[eng] = self.grp_ctr

    def group_end(self, eng):
        self.cur_grp[eng] = None

    def op(self, eng, fn, reads=(), writes=(), dma_slot=None, ndma=1):
        o = Op()
        o.eng = eng
        o.fn = fn
        o.dma = dma_slot is not None
        o.needs_sig = False
        o.sem = None
        o.sigval = 0
        o.pos = len(self.ops[eng])
        o.grp = self.cur_grp[eng]
        if eng != "pe":
            extra = [k for k in reads if isinstance(k, tuple) and k[0] == "ps" and k not in writes]
            if extra:
                writes = list(writes) + extra
        deps = []
        for k in reads:
            w = self.lastw.get(k)
            if w is not None:
                deps.append(w)
        for k in writes:
            w = self.lastw.get(k)
            if w is not None:
                deps.append(w)
            r = self.readers.get(k)
            if r:
                deps.extend(r[0].values())
                deps.extend(r[1])
        best = {}
        dl = []
        seen = set()
        for d in deps:
            if d.dma:
                if id(d) not in seen:
                    seen.add(id(d))
                    dl.append(d)
            else:
                if d.eng == "pe" and eng == "pe" and not o.dma:
                    continue
                b = best.get(d.eng)
                if b is None or d.pos > b.pos:
                    best[d.eng] = d
        for d in best.values():
            d.needs_sig = True
            dl.append(d)
        o.deps = dl
        for k in reads:
            r = self.readers.get(k)
            if r is None:
                r = self.readers[k] = ({}, [])
            if o.dma:
                r[1].append(o)
            else:
                r[0][eng] = o
        for k in writes:
            self.lastw[k] = o
            self.readers[k] = ({}, [])
        if o.dma:
            if dma_slot not in self.dsem:
                self.dsem[dma_slot] = self.stack.enter_context(self.nc.semaphore("D_" + dma_slot))
                self.dcount[dma_slot] = 0
            self.dcount[dma_slot] += 16 * ndma
            o.sem = self.dsem[dma_slot]
            o.sigval = self.dcount[dma_slot]
        self.ops[eng].append(o)
        return o

    def emit(self):
        for e in self.ENGS:
            c = 0
            for o in self.ops[e]:
                if not o.dma and o.needs_sig:
                    c += 1
                    o.sigval = c
                    o.sem = self.esem[e]
        with self.nc.Block() as block:
            decos = {"pe": block.tensor, "act": block.scalar, "dve": block.vector,
                     "pool": block.gpsimd, "sp": block.sync}
            for e in self.ENGS:
                def body(eng, e=e):
                    waited = {}
                    ops = self.ops[e]
                    n_ops = len(ops)
                    idx = 0
                    while idx < n_ops:
                        o = ops[idx]
                        j = idx + 1
                        if o.grp is not None:
                            while j < n_ops and ops[j].grp == o.grp:
                                j += 1
                        need = {}
                        for oo in ops[idx:j]:
                            for d in oo.deps:
                                if d.eng == e and (not d.dma) and d.pos >= idx:
                                    continue
                                k = id(d.sem)
                                if d.sigval > need.get(k, (None, 0))[1]:
                                    need[k] = (d.sem, d.sigval)
                        for k, (sem, val) in need.items():
                            if waited.get(k, 0) >= val:
                                continue
                            eng.wait_ge(sem, val)
                            waited[k] = val
                        for oo in ops[idx:j]:
                            ins = oo.fn(eng)
                            if oo.dma:
                                for i_ in (ins if isinstance(ins, (list, tuple)) else [ins]):
                                    i_.then_inc(oo.sem, 16)
                            elif oo.needs_sig:
                                ins.then_inc(oo.sem, 1)
                        idx = j
                    if e == "sp":
                        for slot, sem in self.dsem.items():
                            eng.wait_ge(sem, self.dcount[slot])
                decos[e](body)


def build_nc(SEQ=4096, do0=True, do1=True, final=True, overlap=True):
    NT = SEQ // TT
    NBLK = SEQ // 128
    nc = bass.Bass("TRN2", target_bir_lowering=False)

    def din(name, shape):
        return nc.dram_tensor(name, list(shape), F32, kind="ExternalInput").ap()

    x_d = din("x", [SEQ, D])
    cT_d = din("cT", [128, 8])
    ngT_d = din("normgT", [128, 2, 8])
    adaw_d = din("ada_w", [2, D, 3 * D])
    adab_d = din("ada_b", [1, 2 * 3 * D])
    win0_d = din("w_in0", [D, 6144])
    poolw_d = din("pool_w", [4, 256, 256])
    pscT_d = din("pscaleT", [128, 8])
    wout0_d = din("w_out0", [2048, D])
    win1_d = din("w_in1", [D, 8192])
    cwT_d = din("convwT", [128, 16, 3])
    cbT_d = din("convbT", [128, 16])
    wout1_d = din("w_out1", [2048, D])
    fg_d = din("final_g", [1, D])
    out_d = nc.dram_tensor("out", [SEQ, D], F32, kind="ExternalOutput").ap()
    kd = nc.dram_tensor("kd", [8, 128, SEQ], BF16).ap()
    vd = nc.dram_tensor("vd", [8, 128, NBLK, 128], BF16).ap()
    NPIECE = 37
    wc = nc.dram_tensor("wcache", [NPIECE, 128, 8, 512], BF16).ap()

    with contextlib.ExitStack() as stack:
        P = Prog(nc, stack)

        def sb(name, shape, dtype):
            return stack.enter_context(nc.sbuf_tensor("s_" + name, list(shape), dtype))

        psall = stack.enter_context(nc.psum_tensor("psall", [128, 8, 512], F32))
        ps = [psall[:, b, :] for b in range(8)]
        psk = [("ps", b) for b in range(8)]
        bank_rr = [0]

        def nbank():
            b = bank_rr[0]
            bank_rr[0] = (b + 1) % 8
            return b

        onesf = sb("onesf", [128, 128], F32)
        identf = sb("identf", [128, 128], F32)
        tmpf = sb("tmpf", [128, 128], F32)
        negf = sb("negf", [128, 128], F32)
        tri = sb("tri", [128, 128], BF16)
        onesb = sb("onesb", [128, 128], BF16)
        identb = sb("identb", [128, 128], BF16)
        maskb = sb("maskb", [128, 128], BF16)
        zerob = sb("zerob", [128, 128], BF16)
        invc = sb("invc", [128, 4, 16], F32)
        cT = sb("cT", [128, 8], F32)
        scb = sb("scb", [128, 8], BF16)
        ngT = sb("ngT", [128, 2, 8], F32)
        GT = sb("GT", [128, 2, 8], F32)
        shT = sb("shT", [128, 2, 8], F32)
        G1bc = sb("G1bc", [128, 2, D], F32)
        fgbc = sb("fgbc", [128, D], F32)
        pscT = sb("pscT", [128, 8], F32)
        cwT = sb("cwT", [128, 16, 3], F32)
        cbT = sb("cbT", [128, 16], F32)
        uh = sb("uh", [128, 8, 16], F32)
        uuh = sb("uuh", [128, 16, 2], F32)
        ss = sb("ss", [128, 8], F32)
        rstd = sb("rstd", [128, 8], F32)
        epsb = sb("epsb", [128, 1], F32)
        xt = [sb("xt%d" % j, [128, 4, D], F32) for j in range(2)]
        xn = sb("xn", [128, 4, D], F32)
        MSBK = [("xn", 0), ("xn", 1), ("xn", 2)]
        ADBK = [("xt", 1)]

        def msb_ap(c0, c1):
            b = c0 // D
            assert (c1 - 1) // D == b
            return xn[0:1, b, c0 - b * D:c1 - b * D]

        def adab_ap(c0, c1):
            b = c0 // D
            assert (c1 - 1) // D == b
            return xt[1][0:1, b, c0 - b * D:c1 - b * D]
        fgrow = xn[0:1, 3, :]
        hT = sb("hT", [128, 8, TT], BF16)
        qpad = sb("qpad", [128, 16, TT], BF16)
        pT = sb("pT", [128, 8, TT], BF16)
        yT = sb("yT", [128, 16, TT], BF16)
        kst = pT
        vst = sb("vst", [128, 4, D], BF16)
        kbuf = [sb("kbuf%d" % j, [128, SEQ], BF16) for j in range(2)]
        vbuf = [sb("vbuf%d" % j, [128, NBLK, 128], BF16) for j in range(2)]
        F2 = [sb("F2_%d" % j, [128, 2, 528], F32) for j in range(5)]
        Fs = [F2[j // 2][:, j % 2, :] for j in range(10)]
        SP2 = [sb("SP2_%d" % j, [128, 2, 512], BF16) for j in range(2)]
        A2 = [sb("A2_%d" % j, [128, 2, 512], BF16) for j in range(2)]
        otri = sb("otri", [128, 128], BF16)
        wsl = [sb("w%d" % j, [128, 8, 512], BF16) for j in range(NWSLOT)]
        wcnt = [0]

        def MM(out, lhsT, rhs, start, stop, reads, writes, sgc=False):
            if sgc:
                P.op("pe", lambda e: e.matmul(out, lhsT, rhs, start=start, stop=stop, skip_group_check=True),
                     reads, writes)
            else:
                P.op("pe", lambda e: e.matmul(out, lhsT, rhs, start=start, stop=stop), reads, writes)

        def TR(out, in_, reads, writes):
            P.op("pe", lambda e: e.transpose(out, in_, identf[:]), list(reads) + ["identf"], writes)

        def ACT(out, in_, func, reads, writes, bias=None, scale=None, accum=None):
            kw = {}
            if bias is not None:
                kw["bias"] = bias
            if scale is not None:
                kw["scale"] = scale
            if accum is not None:
                kw["accum_out"] = accum
            P.op("act", lambda e: e.activation(out, in_, func, **kw), reads, writes)

        def TTo(eng, out, in0, in1, op, reads, writes):
            P.op(eng, lambda e: e.tensor_tensor(out, in0, in1, op), reads, writes)

        def TS(eng, out, in0, s1, s2, op0, op1, reads, writes):
            if s2 is None:
                P.op(eng, lambda e: e.tensor_scalar(out, in0, s1, None, op0), reads, writes)
            else:
                P.op(eng, lambda e: e.tensor_scalar(out, in0, s1, s2, op0, op1), reads, writes)

        def STT(eng, out, in0, sc, in1, op0, op1, reads, writes):
            P.op(eng, lambda e: e.scalar_tensor_tensor(out, in0, sc, in1, op0, op1), reads, writes)

        def CP(eng, out, in_, reads, writes):
            if eng == "act":
                P.op("act", lambda e: e.activation(out, in_, AF.Copy), reads, writes)
            else:
                P.op(eng, lambda e: e.tensor_copy(out, in_), reads, writes)

        def MS(eng, ap, val, writes):
            P.op(eng, lambda e: e.memset(ap, val), (), writes)

        tctx = {"i": -1, "k": 0}

        def wload(dst_fn, src, piece=None, tile=None):
            s = wcnt[0] % NWSLOT
            wcnt[0] += 1
            dst = dst_fn(wsl[s])
            if piece is None:
                tile = tctx["i"]
                if tile >= 0:
                    piece = tctx["k"]
                    tctx["k"] += 1
            if piece is None or tile < 0 or not USE_WCACHE:
                P.op("pool", lambda e: e.dma_start(out=dst, in_=src), (), [("w", s)], dma_slot="w%d" % s)
                return s
            assert piece < NPIECE
            cv = dst_fn(wc[piece])
            if tile == 0:
                P.op("pool", lambda e: e.dma_start(out=dst, in_=src), (), [("w", s)], dma_slot="w%d" % s)
                P.op("sp", lambda e: e.dma_start(out=cv, in_=dst), [("w", s)], [("wc", piece)], dma_slot="wst%d" % s)
            else:
                P.op("pool", lambda e: e.dma_start(out=dst, in_=cv), [("wc", piece)], [("w", s)], dma_slot="w%d" % s)
            return s

        def wload_std(wd, col0, piece=None, tile=None):
            src = wd[:, col0:col0 + 512].rearrange("(kc p) n -> p kc n", p=128)
            return wload(lambda w: w[:, :, :], src, piece, tile)

        MS("pool", onesf[:], 1.0, ["onesf"])
        MS("pool", epsb[:], EPS, ["epsb"])
        MS("pool", negf[:], NEG, ["negf"])
        MS("pool", zerob[:], 0.0, ["zerob"])
        MS("pool", onesb[:], 1.0, ["onesb"])
        MS("pool", qpad[:], 0.0, ["qpad"])
        MS("pool", uh[:], 0.0, ["uh"])
        MS("pool", uuh[:], 0.0, ["uuh"])
        P.op("pool", lambda e: e.affine_select(out=identf[:], in_=onesf[:], pattern=[[-1, 128]],
                                               compare_op=ALU.is_equal, fill=0.0, base=0,
                                               channel_multiplier=1), ["onesf"], ["identf"])
        P.op("pool", lambda e: e.affine_select(out=tmpf[:], in_=onesf[:], pattern=[[-1, 128]],
                                               compare_op=ALU.is_ge, fill=0.0, base=0,
                                               channel_multiplier=1), ["onesf"], ["tmpf"])
        CP("dve", tri[:], tmpf[:], ["tmpf"], ["tri"])
        TS("dve", otri[:], tmpf[:], -1.0, 1.0, ALU.mult, ALU.add, ["tmpf"], ["otri"])
        CP("dve", identb[:], identf[:], ["identf"], ["identb"])
        P.op("pool", lambda e: e.affine_select(out=tmpf[:], in_=negf[:], pattern=[[-1, 128]],
                                               compare_op=ALU.is_ge, fill=0.0, base=0,
                                               channel_multiplier=1), ["negf"], ["tmpf"])
        CP("dve", maskb[:], tmpf[:], ["tmpf"], ["maskb"])
        for g, w in enumerate(WINS):
            for t in range(w - 1):
                MS("dve", invc[:, g, t:t + 1], 1.0 / (t + 1), ["invc"])
            MS("dve", invc[:, g, w - 1:16], 1.0 / w, ["invc"])

        def const_loads(e):
            return [
                e.dma_start(out=cT[:], in_=cT_d),
                e.dma_start(out=ngT[:], in_=ngT_d),
                e.dma_start(out=pscT[:], in_=pscT_d),
                e.dma_start(out=cwT[:], in_=cwT_d),
                e.dma_start(out=cbT[:], in_=cbT_d),
                e.dma_start(out=fgrow, in_=fg_d),
            ]
        P.op("sp", const_loads, (), ["cT", "ngT", "pscT", "cwT", "cbT", ("xn", 3)],
             dma_slot="const", ndma=6)

        def xload(i):
            j = i % 2
            src = x_d[i * TT:(i + 1) * TT, :].rearrange("(b p) d -> p b d", p=128)
            P.op("sp", lambda e: e.dma_start(out=xt[j][:], in_=src), (), [("xt", j)], dma_slot="x%d" % j)

        xload(0)

        ACT(scb[:], cT[:], AF.Silu, ["cT"], ["scb"])
        for l in range(2):
            if (l == 0 and not do0) or (l == 1 and not do1):
                continue
            P.op("sp", lambda e, l=l: e.dma_start(
                out=xt[1][0:1, 0:3, :], in_=adab_d[0:1, l * 3072:(l + 1) * 3072].rearrange("o (b n) -> o b n", n=D)),
                (), ADBK, dma_slot="ab")
            for kc in range(8):
                src = adaw_d[l, kc * 128:(kc + 1) * 128, :].rearrange("p (g n) -> p g n", n=512)
                s = wload(lambda w: w[:, 0:6, :], src)
                for g in range(6):
                    MM(ps[g][0:1, :], scb[:, kc:kc + 1], wsl[s][:, g, :], kc == 0, kc == 7,
                       ["scb", ("w", s)], [psk[g]])
            for g in range(6):
                TTo("dve", msb_ap(g * 512, (g + 1) * 512), ps[g][0:1, :], adab_ap(g * 512, (g + 1) * 512), ALU.add,
                    [psk[g]] + ADBK, MSBK)
            for b_ in (1, 2):
                TS("dve", xn[0:1, b_, :], xn[0:1, b_, :], 1.0, None, ALU.add, None, MSBK, MSBK)
            for h in range(2):
                MM(ps[6][:, :], onesf[0:1, 0:128], msb_ap(2048 + h * 512, 2048 + (h + 1) * 512),
                   True, True, ["onesf"] + MSBK, [psk[6]])
                CP("act", G1bc[:, l, h * 512:(h + 1) * 512], ps[6][:, :], [psk[6]], [("G1bc", l)])
            for j in range(16):
                MM(ps[7][:, j:j + 1], msb_ap(j * 128, (j + 1) * 128), onesf[0:1, 0:1], True, True,
                   MSBK + ["onesf"], [psk[7]])
            CP("dve", shT[:, l, :], ps[7][:, 0:8], [psk[7]], [("shT", l)])
            TTo("dve", GT[:, l, :], ps[7][:, 8:16], ngT[:, l, :], ALU.mult, [psk[7], "ngT"], [("GT", l)])
        if final:
            for h in range(2):
                MM(ps[6][:, :], onesf[0:1, 0:128], fgrow[:, h * 512:(h + 1) * 512], True, True,
                   ["onesf", ("xn", 3)], [psk[6]])
                CP("act", fgbc[:, h * 512:(h + 1) * 512], ps[6][:, :], [psk[6]], ["fgbc"])

        def stage_norm(i, l, X):
            xk = ("xt", i % 2)
            for blk in range(4):
                ACT(xn[:, blk, :], X[:, blk, :], AF.Square, [xk], [("xn", blk), ("ss", blk)],
                    accum=ss[:, blk:blk + 1])
                ACT(rstd[:, blk:blk + 1], ss[:, blk:blk + 1], AF.Ln, [("ss", blk), "epsb"], [("rstd", blk)],
                    bias=epsb[:, 0:1], scale=1.0 / D)
                ACT(rstd[:, blk:blk + 1], rstd[:, blk:blk + 1], AF.Exp, [("rstd", blk)], [("rstd", blk)],
                    scale=-0.5)
                TS("dve", xn[:, blk, :], X[:, blk, :], rstd[:, blk:blk + 1], None, ALU.mult, None,
                   [xk, ("rstd", blk)], [("xn", blk)])
            for kc in range(8):
                for blk in range(4):
                    TR(ps[kc][:, blk * 128:(blk + 1) * 128], xn[:, blk, kc * 128:(kc + 1) * 128],
                       [("xn", blk)], [psk[kc]])
            for kc in range(8):
                ACT(hT[:, kc, :], ps[kc][:, :], AF.Identity, [psk[kc], ("GT", l), ("shT", l)], [("hT", kc)],
                    bias=shT[:, l, kc:kc + 1], scale=GT[:, l, kc:kc + 1])

        def proj_fm(s, c, b):
            for kc in range(8):
                MM(ps[b][:, :], wsl[s][:, kc, c * 128:(c + 1) * 128], hT[:, kc, :], kc == 0, kc == 7,
                   [("w", s), ("hT", kc)], [psk[b]])

        def stage_outproj(i, l, X, wd):
            xk = ("xt", i % 2)
            for half in range(2):
                banks = [nbank() for _ in range(4)]
                for j in range(2):
                    src = wd[j * 1024:(j + 1) * 1024, half * 512:(half + 1) * 512].rearrange(
                        "(kc p) n -> p kc n", p=128)
                    s = wload(lambda w: w[:, :, :], src)
                    for blk in range(4):
                        for kc in range(8):
                            ch = j * 8 + kc
                            MM(ps[banks[blk]][:, :], yT[:, ch, blk * 128:(blk + 1) * 128], wsl[s][:, kc, :],
                               j == 0 and kc == 0, j == 1 and kc == 7,
                               [("yT", ch), ("w", s)], [psk[banks[blk]]])
                for blk in range(4):
                    f = 6 + (blk % 2)
                    TTo("dve", Fs[f][:, 0:512], ps[banks[blk]][:, :], G1bc[:, l, half * 512:(half + 1) * 512],
                        ALU.mult, [psk[banks[blk]], ("G1bc", l)], [("F", f)])
                    TTo("dve", X[:, blk, half * 512:(half + 1) * 512], X[:, blk, half * 512:(half + 1) * 512],
                        Fs[f][:, 0:512], ALU.add, [("F", f), xk], [xk])

        BGB = [6, 7]
        bgb_ctr = [0]
        bgmode = {"alone": False}

        def bg_bank():
            if bgmode["alone"]:
                b = bgb_ctr[0] % 8
            else:
                b = BGB[bgb_ctr[0] % 2]
            bgb_ctr[0] += 1
            return b

        def run_all(units):
            for u in units:
                for _ in u():
                    pass

        def y4ap(yb, c, a=0, b=512):
            return vst[:, 2 * yb + c // 2, (c % 2) * 512 + a:(c % 2) * 512 + b]

        def norm_units_l1(i):
            X = xt[i % 2]
            xk = ("xt", i % 2)
            units = []
            for blk in range(4):
                for half in range(2):
                    def u(blk=blk, half=half):
                        if half == 0:
                            ACT(xn[:, blk, :], X[:, blk, :], AF.Square, [xk], [("xn", blk), ("ss", blk)],
                                accum=ss[:, blk:blk + 1])
                            ACT(rstd[:, blk:blk + 1], ss[:, blk:blk + 1], AF.Ln, [("ss", blk), "epsb"],
                                [("rstd", blk)], bias=epsb[:, 0:1], scale=1.0 / D)
                            ACT(rstd[:, blk:blk + 1], rstd[:, blk:blk + 1], AF.Exp, [("rstd", blk)],
                                [("rstd", blk)], scale=-0.5)
                            TS("dve", xn[:, blk, :], X[:, blk, :], rstd[:, blk:blk + 1], None, ALU.mult, None,
                               [xk, ("rstd", blk)], [("xn", blk)])
                            yield
                        b = bg_bank()
                        for k4 in range(4):
                            kc = half * 4 + k4
                            TR(ps[b][:, k4 * 128:(k4 + 1) * 128], xn[:, blk, kc * 128:(kc + 1) * 128],
                               [("xn", blk)], [psk[b]])
                            if k4 == 1:
                                yield
                        yield
                        for k4 in range(4):
                            kc = half * 4 + k4
                            if bgmode["alone"]:
                                ACT(hT[:, kc, blk * 128:(blk + 1) * 128], ps[b][:, k4 * 128:(k4 + 1) * 128],
                                    AF.Identity, [psk[b], ("GT", 1), ("shT", 1)], [("hT", kc)],
                                    bias=shT[:, 1, kc:kc + 1], scale=GT[:, 1, kc:kc + 1])
                            else:
                                TS("dve", hT[:, kc, blk * 128:(blk + 1) * 128], ps[b][:, k4 * 128:(k4 + 1) * 128],
                                   GT[:, 1, kc:kc + 1], shT[:, 1, kc:kc + 1], ALU.mult, ALU.add,
                                   [psk[b], ("GT", 1), ("shT", 1)], [("hT", kc)])
                    units.append(u)
            return units

        def proj_steps(s, c, b):
            for kc in range(8):
                MM(ps[b][:, :], wsl[s][:, kc, c * 128:(c + 1) * 128], hT[:, kc, :], kc == 0, kc == 7,
                   [("w", s), ("hT", kc)], [psk[b]])
                if kc % 2 == 1 and kc < 7:
                    yield

        def layer1_units(i):
            X = xt[i % 2]
            xk = ("xt", i % 2)
            units = list(norm_units_l1(i))
            first_norm = units[0]

            def kick():
                need(0)
                yield from first_norm()
            units[0] = kick
            USB = [xn[:, 0, 0:512], xn[:, 0, 512:1024], xn[:, 1, 0:512], xn[:, 1, 512:1024]]
            USK = [("xn", 0), ("xn", 0), ("xn", 1), ("xn", 1)]
            UU, UK = xn[:, 2, 0:514], ("xn", 2)
            T1, TK = xn[:, 3, 0:512], ("xn", 3)
            T2 = xn[:, 3, 512:1024]
            PB = 17
            slots = {}

            def do_load(k):
                q, t = k // 5, k % 5
                if t < 4:
                    col = [(12 + q), (8 + q), (4 + q), q][t] * 512
                    return wload_std(win1_d, col, PB + k, i)
                src = wout1_d[q * 512:(q + 1) * 512, :].rearrange("(kc p) n -> p kc n", p=128)
                return wload(lambda w: w[:, :, :].rearrange("p a b -> p (a b)").rearrange(
                    "p (a b) -> p a b", b=1024), src, PB + k, i)

            def need(k):
                for kk in (k, k + 1):
                    if kk < 20 and kk not in slots:
                        slots[kk] = do_load(kk)
                return slots[k]

            for q in range(4):
                yb = q % 2
                hg, hu, hc, hb, ho = [None], [None], [None], [None], [None]
                for c in range(4):
                    def u(c=c, hg=hg, yb=yb, q=q):
                        if c == 0:
                            hg[0] = need(q * 5 + 0)
                        b = bg_bank()
                        yield from proj_steps(hg[0], c, b)
                        yield
                        if bgmode["alone"]:
                            ACT(y4ap(yb, c), ps[b][:, :], AF.Silu, [psk[b]], ["vst"])
                        else:
                            f = T1 if c % 2 == 0 else T2
                            ACT(f, ps[b][:, :], AF.Exp, [psk[b]], [TK], scale=-1.0)
                            TS("dve", f, f, 1.0, None, ALU.add, None, [TK], [TK])
                            P.op("dve", lambda e, o=f: e.reciprocal(o, o), [TK], [TK])
                            TTo("dve", y4ap(yb, c), ps[b][:, :], f, ALU.mult, [psk[b], TK], ["vst"])
                    units.append(u)
                for c in range(4):
                    def u(c=c, hu=hu, q=q):
                        if c == 0:
                            hu[0] = need(q * 5 + 1)
                        b = bg_bank()
                        yield from proj_steps(hu[0], c, b)
                        yield
                        CP("act" if bgmode["alone"] else "dve", USB[c], ps[b][:, :], [psk[b]], [USK[c]])
                    units.append(u)
                for c in range(4):
                    def u(c=c, hc=hc, q=q):
                        if c == 0:
                            hc[0] = need(q * 5 + 2)
                        ch = q * 4 + c
                        b = bg_bank()
                        yield from proj_steps(hc[0], c, b)
                        yield
                        CP("dve", UU[:, 0:2], uuh[:, ch, :], [("uuh", ch)], [UK])
                        TTo("dve", UU[:, 2:514], ps[b][:, :], USB[c], ALU.mult, [psk[b], USK[c]], [UK])
                        CP("dve", uuh[:, ch, :], UU[:, 512:514], [UK], [("uuh", ch)])
                        TS("dve", T1, UU[:, 2:514], cwT[:, ch, 2:3], cbT[:, ch:ch + 1],
                           ALU.mult, ALU.add, [UK, "cwT", "cbT"], [TK])
                        STT("dve", T1, UU[:, 1:513], cwT[:, ch, 1:2], T1, ALU.mult, ALU.add, [UK, "cwT", TK], [TK])
                        STT("dve", USB[c], UU[:, 0:512], cwT[:, ch, 0:1], T1, ALU.mult, ALU.add,
                            [UK, "cwT", TK], [USK[c]])
                    units.append(u)
                for c in range(4):
                    def u(c=c, hb=hb, yb=yb, q=q):
                        if c == 0:
                            hb[0] = need(q * 5 + 3)
                        b = bg_bank()
                        yield from proj_steps(hb[0], c, b)
                        yield
                        TTo("dve", T1, ps[b][:, :], USB[c], ALU.mult, [psk[b], USK[c]], [TK])
                        TTo("dve", y4ap(yb, c), T1, y4ap(yb, c), ALU.mult, [TK, "vst"], ["vst"])
                    units.append(u)
                for half in range(2):
                    for blk in range(4):
                        def u(half=half, blk=blk, ho=ho, yb=yb, q=q):
                            if half == 0 and blk == 0:
                                ho[0] = need(q * 5 + 4)
                            b = bg_bank()
                            wv = wsl[ho[0]][:, :, :].rearrange("p a b -> p (a b)").rearrange("p (a b) -> p a b", b=1024)
                            for c in range(4):
                                MM(ps[b][:, :], y4ap(yb, c, blk * 128, (blk + 1) * 128),
                                   wv[:, c, half * 512:(half + 1) * 512], c == 0, c == 3,
                                   ["vst", ("w", ho[0])], [psk[b]])
                                if c == 1:
                                    yield
                            yield
                            f = T1 if blk % 2 == 0 else T2
                            TTo("dve", f, ps[b][:, :], G1bc[:, 1, half * 512:(half + 1) * 512],
                                ALU.mult, [psk[b], ("G1bc", 1)], [TK])
                            TTo("dve", X[:, blk, half * 512:(half + 1) * 512], X[:, blk, half * 512:(half + 1) * 512],
                                f, ALU.add, [TK, xk], [xk])
                        units.append(u)
            return units

        def finish_units(i):
            def mk(blk):
                def u():
                    finish_blk(i, blk)
                    return
                    yield
                return u
            return [mk(blk) for blk in range(4)]

        def layer0(i, bg=()):
            X = xt[i % 2]
            tok0 = i * TT
            stage_norm(i, 0, X)
            for g in range(2):
                s = wload_std(win0_d, (4 + g) * 512)
                for c in range(4):
                    hp = g * 4 + c
                    b = nbank()
                    proj_fm(s, c, b)
                    CP("act" if c % 2 == 0 else "dve", kst[:, hp, :], ps[b][:, :], [psk[b]], [("pT", hp)])
            P.op("sp", lambda e: e.dma_start(out=kd[:, :, tok0:tok0 + TT].rearrange("h p t -> p h t"),
                                             in_=kst[:]), [("pT", c_) for c_ in range(8)], [("kd", i)], dma_slot="ks")
            for vh in range(2):
                s = wload_std(win0_d, (6 + vh) * 512)
                for blk in range(4):
                    b = nbank()
                    for kc in range(8):
                        MM(ps[b][:, :], hT[:, kc, blk * 128:(blk + 1) * 128], wsl[s][:, kc, :], kc == 0, kc == 7,
                           [("hT", kc), ("w", s)], [psk[b]])
                    CP("act" if blk % 2 == 0 else "dve", vst[:, blk, vh * 512:(vh + 1) * 512], ps[b][:, :],
                       [psk[b]], ["vst"])

            def vstore(e):
                return [e.dma_start(out=vd[hp, :, 4 * i:4 * i + 4, :], in_=vst[:, :, hp * 128:(hp + 1) * 128])
                        for hp in range(8)]
            P.op("sp", vstore, ["vst"], [("vd", i)], dma_slot="vs", ndma=8)
            for g in range(4):
                s = wload_std(win0_d, (8 + g) * 512)
                for c in range(4):
                    ch = g * 4 + c
                    b = nbank()
                    proj_fm(s, c, b)
                    ACT(yT[:, ch, :], ps[b][:, :], AF.Silu, [psk[b]], [("yT", ch)])
            for g in range(2):
                s = wload_std(win0_d, (2 + g) * 512)
                for c in range(4):
                    hp = g * 4 + c
                    b = nbank()
                    proj_fm(s, c, b)
                    TS("dve", qpad[0:64, 2 * hp, :], ps[b][0:64, :], 0.125, None, ALU.mult, None,
                       [psk[b]], [("qp", 2 * hp)])
                    P.op("act", lambda e, o=qpad[64:128, 2 * hp + 1, :], n=ps[b][64:128, :]:
                         e.mul(o, n, 0.125), [psk[b]], [("qp", 2 * hp + 1)])
            for g2 in range(2):
                s = wload_std(win0_d, g2 * 512)
                for c4 in range(4):
                    c = g2 * 4 + c4
                    grp = c // 2
                    w = WINS[grp]
                    b = nbank()
                    proj_fm(s, c4, b)
                    U = Fs[c % 2]
                    uk = ("F", c % 2)
                    CP("dve", U[:, 0:16], uh[:, c, :], [("uh", c)], [uk])
                    CP("act", U[:, 16:528], ps[b][:, :], [psk[b]], [uk])
                    CP("dve", uh[:, c, :], U[:, 512:528], [uk], [("uh", c)])
                    ta, tb = (2, 3) if c % 2 == 0 else (4, 5)
                    TTo("dve", Fs[ta][:, 1:528], U[:, 1:528], U[:, 0:527], ALU.add, [uk], [("F", ta)])
                    cur = ta
                    sh = 2
                    while sh < w:
                        oth = tb if cur == ta else ta
                        TTo("dve", Fs[oth][:, 2 * sh - 1:528], Fs[cur][:, 2 * sh - 1:528],
                            Fs[cur][:, sh - 1:528 - sh], ALU.add, [("F", cur)], [("F", oth)])
                        cur = oth
                        sh *= 2
                    STT("dve", pT[:, c, :], Fs[cur][:, 16:528], 1.0 / w, U[:, 16:528], ALU.mult, ALU.subtract,
                        [("F", cur), uk], [("pT", c)])
                    if i == 0:
                        oth = tb if cur == ta else ta
                        TTo("dve", Fs[oth][:, 0:16], Fs[cur][:, 16:32], invc[:, grp, :], ALU.mult,
                            [("F", cur), "invc"], [("F", oth)])
                        TTo("dve", pT[:, c, 0:16], Fs[oth][:, 0:16], U[:, 16:32], ALU.subtract,
                            [("F", oth), uk], [("pT", c)])
            src = poolw_d.rearrange("g (cc p) d -> p (g cc) d", p=128)
            s = wload(lambda w: w[:, :, 0:256], src)
            for dch in range(8):
                grp, dd = dch // 2, dch % 2
                b = nbank()
                for cc in range(2):
                    MM(ps[b][:, :], wsl[s][:, grp * 2 + cc, dd * 128:(dd + 1) * 128], pT[:, grp * 2 + cc, :],
                       cc == 0, cc == 1, [("w", s), ("pT", grp * 2 + cc)], [psk[b]])
                STT("dve", yT[:, dch, :], ps[b][:, :], pscT[:, dch:dch + 1], yT[:, dch, :], ALU.mult, ALU.mult,
                    [psk[b], "pscT", ("yT", dch)], [("yT", dch)])
            ntok = (i + 1) * TT
            nblk = 4 * (i + 1)

            def kvload(hp):
                j = hp % 2
                rk = [("kd", t) for t in range(i + 1)]
                rv = [("vd", t) for t in range(i + 1)]
                P.op("sp", lambda e: e.dma_start(out=kbuf[j][:, 0:ntok], in_=kd[hp, :, 0:ntok]), rk,
                     [("kb", j)], dma_slot="kb%d" % j)
                P.op("sp", lambda e: e.dma_start(out=vbuf[j][:, 0:nblk, :], in_=vd[hp, :, 0:nblk, :]), rv,
                     [("vb", j)], dma_slot="vb%d" % j)

            kvload(0)
            if overlap:
                ZP = [0, 0]
                CBK = [2, 3]
                accb = [4, 5]
            else:
                ZP = [0, 2]
                CBK = [4, 5]
                accb = [6, 7]
            bg = list(bg)
            bgs = {"cur": None, "idx": 0, "steps": 0, "slot": 0}
            BG_STEPS_EST = 448

            def bg_step():
                while True:
                    if bgs["cur"] is None:
                        if bgs["idx"] >= len(bg):
                            return False
                        bgs["cur"] = bg[bgs["idx"]]()
                        bgs["idx"] += 1
                    try:
                        next(bgs["cur"])
                    except StopIteration:
                        bgs["cur"] = None
                    bgs["steps"] += 1
                    return True

            def run_bg(sub):
                if not bg:
                    return
                tgt = (BG_STEPS_EST * (3 * bgs["slot"] + sub + 1) + 3 * (8 * nblk + 2) - 1) // (3 * (8 * nblk + 2))
                if bgs["steps"] < tgt:
                    bg_step()

            def flush_bg():
                bgmode["alone"] = True
                while bg_step():
                    pass
                bgmode["alone"] = False
            items = [(hp, kb) for hp in range(8) for kb in range(nblk - 1, -1, -1)]
            n = len(items)
            st = {}

            def fk(w):
                return [("F", 2 * w), ("F", 2 * w + 1)]

            def A_pe(p):
                hp, kb = items[p]
                j = hp % 2
                r = kb - 4 * i
                col0 = r * 128 if r >= 0 else 0
                st[p] = (hp, kb, r, col0)
                z0 = ZP[p % 2]
                for hd in range(2):
                    zb = z0 + hd
                    MM(ps[zb][:, col0:512], kbuf[j][:, kb * 128:(kb + 1) * 128], qpad[:, 2 * hp + hd, col0:512],
                       True, r < 0, [("kb", j), ("qp", 2 * hp + hd)], [psk[zb]])
                    if r >= 0:
                        MM(ps[zb][:, col0:col0 + 128], identb[:, :], maskb[:, :], False, True,
                           ["identb", "maskb"], [psk[zb]])

            def EXP1(p):
                hp, kb, r, col0 = st[p]
                z0 = ZP[p % 2]
                e_i = p % 3
                ACT(F2[e_i][:, :, col0:512], psall[:, z0:z0 + 2, col0:512], AF.Exp,
                    [psk[z0], psk[z0 + 1]], fk(e_i))

            def LN(p):
                hp, kb, r, col0 = st[p]
                e_i = p % 3
                ACT(SP2[p % 2][:, :, col0:512], F2[e_i][:, :, col0:512], AF.Ln, fk(e_i), [("SP2", p % 2)],
                    bias=1.0)

            def T_pe(p):
                hp, kb, r, col0 = st[p]
                if kb == nblk - 1:
                    for hd in range(2):
                        MM(ps[CBK[hd]][:, :], zerob[:, :], hT[:, 0, :], True, False,
                           ["zerob", ("hT", 0)], [psk[CBK[hd]]], sgc=True)
                for hd in range(2):
                    MM(ps[CBK[hd]][:, col0:512], tri[:, :], SP2[p % 2][:, hd, col0:512], False, False,
                       ["tri", ("SP2", p % 2)], [psk[CBK[hd]]], sgc=True)

            def O_pe(p):
                hp, kb, r, col0 = st[p]
                if kb > 0:
                    for hd in range(2):
                        MM(ps[CBK[hd]][:, col0:512], otri[:, :], SP2[p % 2][:, hd, col0:512], False, False,
                           ["otri", ("SP2", p % 2)], [psk[CBK[hd]]], sgc=True)

            def EXP2_F(p):
                hp, kb, r, col0 = st[p]
                e_i = p % 3
                w_i = 3 + p % 2
                ACT(F2[w_i][:, :, col0:512], psall[:, CBK[0]:CBK[0] + 2, col0:512], AF.Exp,
                    [psk[CBK[0]], psk[CBK[1]]], fk(w_i), scale=-1.0)
                TTo("dve", A2[p % 2][:, :, col0:512], F2[e_i][:, :, col0:512], F2[w_i][:, :, col0:512], ALU.mult,
                    fk(e_i) + fk(w_i), [("A2", p % 2)])

            def G_pe(p):
                hp, kb, r, col0 = st[p]
                j = hp % 2
                if kb == nblk - 1:
                    if hp + 1 < 8:
                        kvload(hp + 1)
                    for hd in range(2):
                        MM(ps[accb[hd]][:, :], zerob[:, :], hT[:, 0, :], True, False,
                           ["zerob", ("hT", 0)], [psk[accb[hd]]])
                for hd in range(2):
                    MM(ps[accb[hd]][:, col0:512], vbuf[j][:, kb, :], A2[p % 2][:, hd, col0:512], False, kb == 0,
                       [("vb", j), ("A2", p % 2)], [psk[accb[hd]]])

            def Y_evac(p):
                hp, kb, r, col0 = st[p]
                if kb == 0:
                    for hd in range(2):
                        pr = slice(hd * 64, (hd + 1) * 64)
                        TTo("dve", yT[pr, 8 + hp, :], ps[accb[hd]][pr, :], yT[pr, 8 + hp, :], ALU.mult,
                            [psk[accb[hd]], ("yT", 8 + hp)], [("yT", 8 + hp)])

            A_pe(0)
            for p in range(n + 2):
                bgs["slot"] = p
                if overlap:
                    P.group_begin("pe")
                    if 0 <= p - 1 < n:
                        T_pe(p - 1)
                    P.group_end("pe")
                    run_bg(0)
                    if p < n:
                        EXP1(p)
                    P.group_begin("pe")
                    if p + 1 < n:
                        A_pe(p + 1)
                    P.group_end("pe")
                    run_bg(1)
                else:
                    P.group_begin("pe")
                    if 0 <= p - 1 < n:
                        T_pe(p - 1)
                    if p + 1 < n:
                        A_pe(p + 1)
                    P.group_end("pe")
                    if p < n:
                        EXP1(p)
                if 0 <= p - 1 < n:
                    EXP2_F(p - 1)
                if p < n:
                    LN(p)
                P.group_begin("pe")
                if 0 <= p - 2 < n:
                    G_pe(p - 2)
                if 0 <= p - 1 < n:
                    O_pe(p - 1)
                P.group_end("pe")
                if 0 <= p - 2 < n:
                    Y_evac(p - 2)
                if overlap:
                    run_bg(2)
            flush_bg()
            stage_outproj(i, 0, X, wout0_d)

        def layer1(i):
            X = xt[i % 2]
            stage_norm(i, 1, X)
            for q in range(4):
                s = wload_std(win1_d, (12 + q) * 512)
                for c in range(4):
                    ch = q * 4 + c
                    b = nbank()
                    proj_fm(s, c, b)
                    ACT(yT[:, ch, :], ps[b][:, :], AF.Silu, [psk[b]], [("yT", ch)])
                s = wload_std(win1_d, (8 + q) * 512)
                for c in range(4):
                    b = nbank()
                    proj_fm(s, c, b)
                    CP("act", Fs[c][:, 0:512], ps[b][:, :], [psk[b]], [("F", c)])
                s = wload_std(win1_d, (4 + q) * 512)
                for c in range(4):
                    ch = q * 4 + c
                    b = nbank()
                    proj_fm(s, c, b)
                    UU = Fs[4 + c % 2]
                    uk = ("F", 4 + c % 2)
                    CP("dve", UU[:, 0:2], uuh[:, ch, :], [("uuh", ch)], [uk])
                    TTo("dve", UU[:, 2:514], ps[b][:, :], Fs[c][:, 0:512], ALU.mult, [psk[b], ("F", c)], [uk])
                    CP("dve", uuh[:, ch, :], UU[:, 512:514], [uk], [("uuh", ch)])
                    t1 = 6 + c % 2
                    TS("dve", Fs[t1][:, 0:512], UU[:, 2:514], cwT[:, ch, 2:3], cbT[:, ch:ch + 1], ALU.mult, ALU.add,
                       [uk, "cwT", "cbT"], [("F", t1)])
                    STT("dve", Fs[t1][:, 0:512], UU[:, 1:513], cwT[:, ch, 1:2], Fs[t1][:, 0:512], ALU.mult, ALU.add,
                        [uk, "cwT", ("F", t1)], [("F", t1)])
                    STT("dve", Fs[c][:, 0:512], UU[:, 0:512], cwT[:, ch, 0:1], Fs[t1][:, 0:512], ALU.mult, ALU.add,
                        [uk, "cwT", ("F", t1)], [("F", c)])
                s = wload_std(win1_d, q * 512)
                for c in range(4):
                    ch = q * 4 + c
                    b = nbank()
                    proj_fm(s, c, b)
                    t1 = 6 + c % 2
                    TTo("dve", Fs[t1][:, 0:512], ps[b][:, :], Fs[c][:, 0:512], ALU.mult, [psk[b], ("F", c)],
                        [("F", t1)])
                    TTo("dve", yT[:, ch, :], Fs[t1][:, 0:512], yT[:, ch, :], ALU.mult, [("F", t1), ("yT", ch)],
                        [("yT", ch)])
            stage_outproj(i, 1, X, wout1_d)

        def finish_blk(i, blk):
            X = xt[i % 2]
            xk = ("xt", i % 2)
            if final:
                ACT(xn[:, blk, :], X[:, blk, :], AF.Square, [xk], [("xn", blk), ("ss", blk)],
                    accum=ss[:, blk:blk + 1])
                ACT(rstd[:, blk:blk + 1], ss[:, blk:blk + 1], AF.Ln, [("ss", blk), "epsb"], [("rstd", blk)],
                    bias=epsb[:, 0:1], scale=1.0 / D)
                ACT(rstd[:, blk:blk + 1], rstd[:, blk:blk + 1], AF.Exp, [("rstd", blk)], [("rstd", blk)],
                    scale=-0.5)
                STT("dve", xn[:, blk, :], X[:, blk, :], rstd[:, blk:blk + 1], fgbc[:, :], ALU.mult, ALU.mult,
                    [xk, ("rstd", blk), "fgbc"], [("xn", blk)])
            else:
                CP("dve", xn[:, blk, :], X[:, blk, :], [xk], [("xn", blk)])
            r0 = i * TT + blk * 128
            P.op("sp", lambda e, r0=r0, blk=blk: e.dma_start(out=out_d[r0:r0 + 128, :], in_=xn[:, blk, :]),
                 [("xn", blk)], [("out", i, blk)], dma_slot="o%d" % blk)

        def finish(i):
            for blk in range(4):
                finish_blk(i, blk)

        pending = []
        for i in range(NT):
            tctx["i"] = i
            tctx["k"] = 0
            if not (overlap and do0 and do1):
                if i + 1 < NT:
                    xload(i + 1)
                if do0:
                    layer0(i)
                if do1:
                    layer1(i)
                finish(i)
                continue
            layer0(i, pending)
            if i + 1 < NT:
                xload(i + 1)
            pending = layer1_units(i) + finish_units(i)
        if pending:
            bgmode["alone"] = True
            run_all(pending)

        P.emit()
    return nc


_NC_CACHE = {}


def _layout_inputs(x, c, norm_g, ada_w, ada_b, even_w_in, pool_w, pool_scale, even_w_out,
                   odd_w_in, conv_w, conv_b, odd_w_out, final_g):
    f = lambda a: np.ascontiguousarray(np.asarray(a, dtype=np.float32))
    shared = {
        "normgT": f(np.asarray(norm_g).reshape(2, 8, 128).transpose(2, 0, 1)),
        "ada_w": f(ada_w),
        "ada_b": f(np.asarray(ada_b).reshape(1, -1)),
        "w_in0": f(np.asarray(even_w_in)[0]),
        "pool_w": f(np.asarray(pool_w)[0]),
        "pscaleT": f(np.asarray(pool_scale)[0].reshape(8, 128).T),
        "w_out0": f(np.asarray(even_w_out)[0]),
        "w_in1": f(np.asarray(odd_w_in)[0]),
        "convwT": f(np.asarray(conv_w)[0].reshape(3, 16, 128).transpose(2, 1, 0)),
        "convbT": f(np.asarray(conv_b)[0].reshape(16, 128).T),
        "w_out1": f(np.asarray(odd_w_out)[0]),
        "final_g": f(np.asarray(final_g).reshape(1, -1)),
    }
    x = np.asarray(x)
    c = np.asarray(c)
    maps = []
    for b in range(x.shape[0]):
        m = dict(shared)
        m["x"] = f(x[b])
        m["cT"] = f(c[b].reshape(8, 128).T)
        maps.append(m)
    return maps


def kernel(x, c, norm_g, ada_w, ada_b, even_w_in, pool_w, pool_scale, even_w_out,
           odd_w_in, conv_w, conv_b, odd_w_out, final_g):
    x = np.asarray(x)
    B, S, _ = x.shape
    maps = _layout_inputs(x, c, norm_g, ada_w, ada_b, even_w_in, pool_w, pool_scale, even_w_out,
                          odd_w_in, conv_w, conv_b, odd_w_out, final_g)
    if S not in _NC_CACHE:
        _NC_CACHE[S] = build_nc(S)
    nc = _NC_CACHE[S]
    res = run_bass_kernel_spmd(nc, maps, core_ids=list(range(B)))
    return np.stack([np.asarray(r["out"], dtype=np.float32) for r in res.results], axis=0)
```
